# Optimizing a Trainium2 kernel written in Bass

```python
import jax, jax.numpy as jnp
from jax import lax
import numpy as np

D_MODEL = 1024
BATCH = 8
SEQ = 4096
DEPTH = 1

ATTN_HEADS = 8
HEAD_DIM = 64
ATTN_WIDTH = ATTN_HEADS * HEAD_DIM
CONV_GROUPS = 8
CONV_WIDTH = D_MODEL // 2
CONV_K = 3
Q_BLOCK = 128
EPS = 1e-6
IN_SIZES = (ATTN_WIDTH, ATTN_WIDTH, ATTN_WIDTH, ATTN_HEADS, ATTN_WIDTH,
            CONV_WIDTH, CONV_WIDTH, CONV_WIDTH, CONV_WIDTH,
            D_MODEL, D_MODEL)
IN_WIDTH = sum(IN_SIZES)

kernel_name = "fox_shortconv_gated_hybrid"


def _rmsnorm(x, g):
    xf = x.astype(jnp.float32)
    xf = xf * lax.rsqrt(jnp.mean(xf * xf, axis=-1, keepdims=True) + EPS)
    return xf.astype(x.dtype) * g


def _fox_attention(q, k, v, log_f):
    b, s, h, dh = q.shape
    nb = s // Q_BLOCK
    cum = jnp.cumsum(log_f, axis=1).transpose(0, 2, 1)
    q_blocks = q.reshape(b, nb, Q_BLOCK, h, dh).swapaxes(0, 1)
    cum_q_blocks = cum.reshape(b, h, nb, Q_BLOCK).transpose(2, 0, 1, 3)
    k_pos = jnp.arange(s)
    scale = dh ** -0.5

    def one_block(args):
        qb, cq, blk = args
        logits = jnp.einsum('bqhd,bkhd->bhqk', qb, k).astype(jnp.float32) * scale
        logits = logits + cq[..., None] - cum[:, :, None, :]
        q_pos = blk * Q_BLOCK + jnp.arange(Q_BLOCK)
        causal = k_pos[None, :] <= q_pos[:, None]
        logits = jnp.where(causal, logits, -jnp.inf)
        p = jax.nn.softmax(logits, axis=-1).astype(v.dtype)
        return jnp.einsum('bhqk,bkhd->bqhd', p, v)

    out = lax.map(one_block, (q_blocks, cum_q_blocks, jnp.arange(nb)))
    return out.swapaxes(0, 1).reshape(b, s, h * dh)


def _causal_dwconv(v, w):
    return lax.conv_general_dilated(
        v, w[:, None, :].astype(v.dtype), window_strides=(1,),
        padding=[(CONV_K - 1, 0)], dimension_numbers=('NWC', 'WIO', 'NWC'),
        feature_group_count=v.shape[-1])


def _layer(x, c, w_ada, b_ada, norm_g, w_in, b_f, q_norm_g, k_norm_g,
           conv_w, w_attn_out, w_conv_out, w_o):
    b, s, _ = x.shape
    ada = c @ w_ada + b_ada
    shift, scale, gate = jnp.split(ada, 3, axis=-1)
    h = _rmsnorm(x, norm_g) * (1 + scale[:, None, :]) + shift[:, None, :]
    proj = h @ w_in
    split_points = np.cumsum(IN_SIZES)[:-1].tolist()
    q, k, v, f_logit, z_a, gb, gc, u, z_b, g_a, g_b = jnp.split(proj, split_points, axis=-1)

    q = _rmsnorm(q.reshape(b, s, ATTN_HEADS, HEAD_DIM), q_norm_g)
    k = _rmsnorm(k.reshape(b, s, ATTN_HEADS, HEAD_DIM), k_norm_g)
    v = v.reshape(b, s, ATTN_HEADS, HEAD_DIM)
    log_f = jax.nn.log_sigmoid((f_logit + b_f).astype(jnp.float32))
    o_a = _fox_attention(q, k, v, log_f) * jax.nn.silu(z_a)

    o_b = gb * _causal_dwconv(gc * u, conv_w) * jax.nn.silu(z_b)

    merged = jax.nn.sigmoid(g_a) * (o_a @ w_attn_out) + jax.nn.sigmoid(g_b) * (o_b @ w_conv_out)
    return x + gate[:, None, :] * (merged @ w_o)


def setup_inputs(seed: int = 0) -> dict:
    key = jax.random.key(seed)
    ks = jax.random.split(key, 14)
    f32 = jnp.float32
    nrm = lambda k, shape, s: jax.random.normal(k, shape, f32) * s
    x = jax.random.normal(ks[0], (BATCH, SEQ, D_MODEL), f32)
    c = jax.random.normal(ks[1], (BATCH, D_MODEL), f32)
    w_ada = nrm(ks[2], (DEPTH, D_MODEL, 3 * D_MODEL), 0.5 * D_MODEL ** -0.5)
    b_ada = nrm(ks[3], (DEPTH, 3 * D_MODEL), 0.02)
    norm_g = 1.0 + nrm(ks[4], (DEPTH, D_MODEL), 0.02)
    w_in = nrm(ks[5], (DEPTH, D_MODEL, IN_WIDTH), D_MODEL ** -0.5)
    b_f = 3.0 + nrm(ks[6], (DEPTH, ATTN_HEADS), 0.5)
    q_norm_g = 1.0 + nrm(ks[7], (DEPTH, HEAD_DIM), 0.02)
    k_norm_g = 1.0 + nrm(ks[8], (DEPTH, HEAD_DIM), 0.02)
    conv_w = nrm(ks[9], (DEPTH, CONV_K, CONV_WIDTH), CONV_K ** -0.5)
    w_attn_out = nrm(ks[10], (DEPTH, ATTN_WIDTH, D_MODEL), ATTN_WIDTH ** -0.5)
    w_conv_out = nrm(ks[11], (DEPTH, CONV_WIDTH, D_MODEL), CONV_WIDTH ** -0.5)
    w_o = nrm(ks[12], (DEPTH, D_MODEL, D_MODEL), D_MODEL ** -0.5)
    return {"x": x, "c": c, "w_ada": w_ada, "b_ada": b_ada, "norm_g": norm_g,
            "w_in": w_in, "b_f": b_f, "q_norm_g": q_norm_g, "k_norm_g": k_norm_g,
            "conv_w": conv_w, "w_attn_out": w_attn_out, "w_conv_out": w_conv_out,
            "w_o": w_o}


def reference(x, c, w_ada, b_ada, norm_g, w_in, b_f, q_norm_g, k_norm_g,
              conv_w, w_attn_out, w_conv_out, w_o):
    for i in range(DEPTH):
        x = _layer(x, c, w_ada[i], b_ada[i], norm_g[i], w_in[i], b_f[i],
                   q_norm_g[i], k_norm_g[i], conv_w[i], w_attn_out[i],
                   w_conv_out[i], w_o[i])
    return x
```

```python
import contextlib
import numpy as np
import concourse.bass as bass
import concourse.mybir as mybir
from concourse.bass_utils import run_bass_kernel_spmd

F32 = mybir.dt.float32
BF16 = mybir.dt.bfloat16
AF = mybir.ActivationFunctionType
ALU = mybir.AluOpType

S, D = 4096, 1024
NQT = 8
EPS = 1e-6
COL_Q, COL_K, COL_V, COL_F, COL_ZA = 0, 512, 1024, 1536, 1544
COL_GB, COL_GC, COL_U, COL_ZB, COL_GA, COL_GG = 2056, 2568, 3080, 3592, 4104, 5128
NCOL = 6152
ARENA_W = 53000
TB = 256

ENGS = ("pe", "act", "dve", "pool", "sp")
N_DMA_SEMS = 24


class Op:
    __slots__ = ("eng", "fn", "waits", "signal", "cnt", "dma_slot", "dma_val", "is_dma", "idx")


class _Rec:
    def __getattr__(self, name):
        def f(*a, **k):
            self.call = (name, a, k)
            return self
        return f


class Prog:
    def __init__(self):
        self.ops = []
        self.last_w = {}
        self.readers = {}
        self.n_dma = 0
        self.last_dma_on_slot = {}

    def op(self, eng, fn, reads=(), writes=(), dma=False):
        o = Op()
        rec = _Rec()
        fn(rec)
        call = rec.call
        fn = lambda e, call=call: getattr(e, call[0])(*call[1], **call[2])
        o.eng, o.fn, o.signal, o.cnt, o.is_dma = eng, fn, False, None, dma
        o.idx = len(self.ops)
        deps = set()
        for r in reads:
            w = self.last_w.get(r)
            if w is not None:
                deps.add(w)
        def same_eng(d):
            q = self.ops[d]
            return q.eng == eng and not q.is_dma and not dma
        for r in writes:
            w = self.last_w.get(r)
            if w is not None and not same_eng(w):
                deps.add(w)
            for rd in self.readers.get(r, {}).values():
                if not same_eng(rd):
                    deps.add(rd)
        o.waits = []
        for d in deps:
            p = self.ops[d]
            if p.eng == "pe" and eng == "pe" and not p.is_dma and not dma:
                continue
            p.signal = True
            o.waits.append(d)
        if dma:
            o.dma_slot = self.n_dma % N_DMA_SEMS
            o.dma_val = 16 * (self.n_dma // N_DMA_SEMS + 1)
            self.n_dma += 1
            self.last_dma_on_slot[o.dma_slot] = o.idx
        key = eng if not dma else ("dma", o.idx)
        for r in reads:
            self.readers.setdefault(r, {})[key] = o.idx
        for r in writes:
            self.last_w[r] = o.idx
            self.readers[r] = {}
        self.ops.append(o)
        return o

    def barrier(self, mk):
        last = {}
        for o in self.ops:
            if not o.is_dma:
                last[o.eng] = o.idx
        dmas = list(self.last_dma_on_slot.values())
        bops = []
        for e in ENGS:
            o = self.op(e, mk[e])
            for d in list(last.values()) + dmas:
                self.ops[d].signal = True
                o.waits.append(d)
            bops.append(o)
        self.last_w = {}
        self.readers = {}

    def emit(self, nc, final_wait_ops=()):
        engs = {"pe": nc.tensor, "act": nc.scalar, "dve": nc.vector, "pool": nc.gpsimd, "sp": nc.sync}
        cnt = {e: 0 for e in ENGS}
        for o in self.ops:
            if o.is_dma:
                continue
            if o.signal:
                cnt[o.eng] += 1
                o.cnt = cnt[o.eng]
        with contextlib.ExitStack() as st:
            sems = {e: st.enter_context(nc.semaphore("s_" + e)) for e in ENGS}
            dsems = [st.enter_context(nc.semaphore("d_%d" % i)) for i in range(N_DMA_SEMS)]
            block = st.enter_context(nc.Block())
            ops = self.ops

            def body(ename):
                eng = engs[ename]
                waited = {}

                def need(key, sem, val, pend):
                    if waited.get(key, 0) >= val:
                        return
                    waited[key] = val
                    pend.append((sem, val))

                for o in ops:
                    if o.eng != ename:
                        continue
                    pend = []
                    for d in o.waits:
                        p = ops[d]
                        if p.is_dma:
                            need(("d", p.dma_slot), dsems[p.dma_slot], p.dma_val, pend)
                        else:
                            need(("e", p.eng), sems[p.eng], p.cnt, pend)
                    if o.is_dma and o.dma_val > 16:
                        need(("d", o.dma_slot), dsems[o.dma_slot], o.dma_val - 16, pend)
                    for sem, val in pend:
                        eng.wait_ge(sem, val)
                    inst = o.fn(eng)
                    if o.is_dma:
                        inst.then_inc(dsems[o.dma_slot], 16)
                    elif o.signal:
                        inst.then_inc(sems[o.eng], 1)
                if ename == "sp":
                    for o in final_wait_ops:
                        if waited.get(("d", o.dma_slot), 0) < o.dma_val:
                            eng.wait_ge(dsems[o.dma_slot], o.dma_val)
                            waited[("d", o.dma_slot)] = o.dma_val

            @block.tensor
            def _(e):
                body("pe")

            @block.scalar
            def _(e):
                body("act")

            @block.vector
            def _(e):
                body("dve")

            @block.gpsimd
            def _(e):
                body("pool")

            @block.sync
            def _(e):
                body("sp")


class Arena:
    def __init__(self, ar, total):
        self.ar, self.total, self.off = ar, total, 0

    def f32(self, n):
        a = self.ar[:, self.off:self.off + n]
        self.off += n
        assert self.off <= self.total, (self.off, self.total)
        return a

    def bf16(self, n_el):
        w = (n_el + 1) // 2
        a = self.ar[:, self.off:self.off + w].bitcast(BF16)
        self.off += w
        assert self.off <= self.total, (self.off, self.total)
        return a


def build(debug=(), n_pairs=4, do_conv=True, do_b=True):
    nc = bass.Bass("TRN2", target_bir_lowering=False)

    def dram(name, shape, kind="ExternalInput", dtype=F32):
        return nc.dram_tensor(name, shape, dtype, kind=kind).ap()

    x = dram("x", [S, D])
    cT2 = dram("cT2", [128, 16])
    w_ada = dram("w_ada", [D, 3 * D])
    b_adaT = dram("b_adaT", [128, 24])
    norm_gT = dram("norm_gT", [128, 8])
    w_in = dram("w_in", [D, NCOL])
    b_f = dram("b_f", [8, 1])
    gqk = dram("gqk", [128, 2])
    conv_wT = dram("conv_wT", [128, 12])
    w_a = dram("w_a", [512, D])
    w_b = dram("w_b", [512, D])
    w_o = dram("w_o", [D, D])
    y = dram("y", [S, D], kind="ExternalOutput")
    dbg = {}
    w_in_v = w_in.rearrange("(j p) n -> p j n", p=128)
    w_ada_v = w_ada.rearrange("(j p) n -> p j n", p=128)

    P = Prog()
    st = contextlib.ExitStack()
    with st:
        AR = st.enter_context(nc.sbuf_tensor("AR", [128, ARENA_W], F32))
        ps = st.enter_context(nc.psum_tensor("ps", [128, 8, 512], F32))
        A = Arena(AR, ARENA_W)

        ident_f = A.f32(128)
        ones_f = A.f32(128)
        vec = A.f32(128)
        cT_s = vec[:, 0:16]
        bada_s = vec[:, 16:40]
        ng_s = vec[:, 40:48]
        ada = vec[:, 48:72]
        shift = ada[:, 0:8]
        scale = ada[:, 8:16]
        gate = ada[:, 16:24]
        gp = vec[:, 72:80]
        gqk_s = vec[:, 80:82]
        gq8 = vec[:, 82:83]
        gk = vec[:, 81:82]
        cw_s = vec[:, 84:96]
        bf_s = vec[0:8, 96:97]
        nbf = vec[0:8, 97:98]
        ss1 = vec[:, 100:104]
        lnv1 = vec[:, 104:108]
        rstd1 = vec[:, 108:112]
        ident_b = A.bf16(128)
        maskL = A.bf16(128)
        bdiag = A.bf16(128)
        convdiag = A.bf16(12 * 128).rearrange("p (c f) -> p c f", c=12)
        wf_b = A.bf16(64).rearrange("p (j f) -> p j f", j=8)
        wf_s = A.f32(64).rearrange("p (j f) -> p j f", j=8)
        oaT = A.bf16(4 * S).rearrange("p (c t) -> p c t", c=4)
        rs_all = A.f32(32)
        base_off = A.off

        hT = A.bf16(8 * S).rearrange("p (j t) -> p j t", j=8)
        kv_off = A.off
        KA = A.bf16(2 * S).rearrange("p (h t) -> p h t", h=2)
        VA = A.bf16(32 * 2 * 128).rearrange("p (s h f) -> p s h f", s=32, h=2)
        rest_off = A.off
        QA = A.bf16(2 * 2 * 512).rearrange("p (b h t) -> p b h t", b=2, h=2)
        zs = A.bf16(2 * 512).rearrange("p (b t) -> p b t", b=2)
        wbf = A.bf16(2 * 8 * 512).rearrange("p (b j n) -> p b j n", b=2, j=8)
        wst = A.f32(2 * 8 * 128).rearrange("p (b j n) -> p b j n", b=2, j=8)
        Frows = A.bf16(S)
        sc_off = A.off
        qraw = A.f32(1024).rearrange("p (b t) -> p b t", b=2)
        sq = A.bf16(1024).rearrange("p (b t) -> p b t", b=2)
        lnv = A.f32(1024).rearrange("p (b t) -> p b t", b=2)
        rstd = A.f32(1024).rearrange("p (b t) -> p b t", b=2)
        PT = A.bf16(4 * 512).rearrange("p (b t) -> p b t", b=4)
        rden = A.f32(1024).rearrange("p (b t) -> p b t", b=2)
        tmpz = A.f32(1024).rearrange("p (b t) -> p b t", b=2)
        th = A.bf16(1024).rearrange("p (b t) -> p b t", b=2)
        endA = A.off
        A.off = kv_off
        xbA = A.f32(3 * 1024).rearrange("p (b t) -> p b t", b=3)
        xnA = A.f32(2 * 1024).rearrange("p (b t) -> p b t", b=2)
        junkA = A.bf16(1024)
        assert A.off <= rest_off
        A.off = rest_off
        adast = A.f32(3 * 8 * 512).rearrange("p (b j n) -> p b j n", b=3, j=8)
        adarow = A.f32(3072)
        assert A.off <= endA
        A.off = kv_off
        e_all = A.f32(S)
        F_all = A.f32(S)
        assert A.off <= rest_off
        A.off = sc_off
        HMLa = A.bf16(3 * S).rearrange("p (r t) -> p r t", r=3)
        ones512 = A.f32(512)
        assert A.off <= endA
        A.off = kv_off
        obT = A.bf16(4 * S).rearrange("p (c t) -> p c t", c=4)
        assert A.off == rest_off
        A.off = sc_off
        gcs = A.bf16(1024).rearrange("p (b t) -> p b t", b=2)
        gcu = A.bf16(2 * 516).rearrange("p (b t) -> p b t", b=2)
        thz = A.bf16(1024).rearrange("p (b t) -> p b t", b=2)
        a1 = A.f32(1024).rearrange("p (b t) -> p b t", b=2)
        a2 = A.f32(1024).rearrange("p (b t) -> p b t", b=2)
        assert A.off <= endA
        A.off = base_off
        Wg = A.bf16(8 * 2048).rearrange("p (j n) -> p j n", j=8)
        WA = A.bf16(4 * 1024).rearrange("p (c n) -> p c n", c=4)
        WB = A.bf16(4 * 1024).rearrange("p (c n) -> p c n", c=4)
        WO = A.bf16(8 * 1024).rearrange("p (j n) -> p j n", j=8)
        assert A.off == kv_off
        A.off = rest_off
        hTt = A.bf16(2 * 8 * 512).rearrange("p (b j t) -> p b j t", b=2, j=8)
        mT = A.bf16(8 * 512).rearrange("p (j t) -> p j t", j=8)
        xbB = A.f32(2 * 1024).rearrange("p (b t) -> p b t", b=2)
        xnB = A.f32(2 * 1024).rearrange("p (b t) -> p b t", b=2)
        junkB = A.bf16(1024)
        thg = A.bf16(2 * 1024).rearrange("p (b t) -> p b t", b=2)
        t12 = A.f32(2 * 1024).rearrange("p (b t) -> p b t", b=2)
        gate_bc = A.f32(1024)
        Dj = A.f32(128)
        xr_off = A.off
        xr = A.f32(3 * 1024).rearrange("p (b t) -> p b t", b=3)
        assert A.off <= ARENA_W, A.off
        A.off = xr_off
        wstB = A.f32(3 * 1024).rearrange("p (b n) -> p b n", b=3)

        def dma(out, in_, reads=(), writes=()):
            return P.op("sp", lambda e: e.dma_start(out=out, in_=in_), reads=reads, writes=writes, dma=True)

        def barrier():
            mk = {
                "pe": lambda e: e.nop(),
                "act": lambda e: e.nop(),
                "dve": lambda e: e.nop(),
                "pool": lambda e: e.nop(),
                "sp": lambda e: e.nop(),
            }
            P.barrier(mk)

        P.op("pool", lambda e: e.memset(ident_f, 1.0), writes=["ident_f"])
        P.op("pool", lambda e: e.affine_select(out=ident_f, in_=ident_f, pattern=[[-1, 128]], base=0,
                                               channel_multiplier=1, compare_op=ALU.is_equal, fill=0.0),
             reads=["ident_f"], writes=["ident_f"])
        P.op("pool", lambda e: e.tensor_copy(out=ident_b, in_=ident_f), reads=["ident_f"], writes=["ident_b"])
        P.op("pool", lambda e: e.memset(ones_f, 1.0), writes=["ones_f"])
        P.op("pool", lambda e: e.memset(maskL, -30000.0), writes=["maskL"])
        P.op("pool", lambda e: e.affine_select(out=maskL, in_=maskL, pattern=[[1, 128]], base=0,
                                               channel_multiplier=-1, compare_op=ALU.is_gt, fill=0.0),
             reads=["maskL"], writes=["maskL"])
        P.op("pool", lambda e: e.memset(bdiag, 0.0), writes=["bdiag"])
        P.op("pool", lambda e: e.memset(bdiag[0:64, 0:64], 1.0), reads=["bdiag"], writes=["bdiag"])
        P.op("pool", lambda e: e.memset(bdiag[64:128, 64:128], 1.0), reads=["bdiag"], writes=["bdiag"])
        dma(cT_s, cT2[:, :], writes=["cT"])
        dma(bada_s, b_adaT[:, :], writes=["bada"])
        dma(ng_s, norm_gT[:, :], writes=["ng"])
        dma(gqk_s, gqk[:, :], writes=["gqk"])
        dma(cw_s, conv_wT[:, :], writes=["cw"])
        dma(bf_s, b_f[:, :], writes=["bf"])
        dma(wf_s, w_in_v[:, :, COL_F:COL_F + 8], writes=["wf_s"])
        P.op("dve", lambda e: e.tensor_copy(out=wf_b, in_=wf_s), reads=["wf_s"], writes=["wf_b"])
        P.op("dve", lambda e: e.tensor_scalar(out=gq8, in0=gqk_s[:, 0:1], scalar1=0.125, scalar2=None, op0=ALU.mult),
             reads=["gqk"], writes=["gq8"])
        P.op("dve", lambda e: e.tensor_scalar(out=nbf, in0=bf_s, scalar1=-1.0, scalar2=None, op0=ALU.mult),
             reads=["bf"], writes=["nbf"])
        for c in range(12):
            P.op("dve", lambda e, c=c: e.tensor_scalar(out=convdiag[:, c, :], in0=ident_f, scalar1=cw_s[:, c:c + 1],
                                                       scalar2=None, op0=ALU.mult),
                 reads=["ident_f", "cw"], writes=[("convdiag", c)])

        for m4 in range(6):
            b = m4 % 3
            dma(adast[:, b], w_ada_v[:, :, m4 * 512:(m4 + 1) * 512], writes=[("adast", b)])
            for j in range(8):
                P.op("pe", lambda e, m4=m4, j=j, b=b: e.matmul(ps[0:1, m4, :], lhsT=cT_s[:, 2 * j:2 * j + 1], rhs=adast[:, b, j, :],
                                                               start=(j == 0), stop=(j == 7)),
                     reads=[("adast", b), "cT"], writes=[("ps", m4)])
            P.op("dve", lambda e, m4=m4: e.tensor_copy(out=adarow[0:1, m4 * 512:(m4 + 1) * 512], in_=ps[0:1, m4, :]),
                 reads=[("ps", m4)], writes=[("adarow", m4)])
        for m in range(24):
            P.op("pe", lambda e, m=m: e.matmul(ps[:, 7, 2 * m:2 * m + 2], lhsT=adarow[0:1, m * 128:(m + 1) * 128], rhs=ones_f[0:1, 0:2],
                                               start=True, stop=True),
                 reads=[("adarow", m // 4), "ones_f"], writes=[("ps", 7)])
        P.op("dve", lambda e: e.tensor_tensor(out=ada, in0=ps[:, 7, 0:48].rearrange("p (m t) -> p m t", t=2)[:, :, 0],
                                              in1=bada_s, op=ALU.add),
             reads=[("ps", 7), "bada"], writes=["ada"])
        P.op("dve", lambda e: e.scalar_tensor_tensor(out=gp, in0=scale, scalar=1.0, in1=ng_s, op0=ALU.add, op1=ALU.mult),
             reads=["ada", "ng"], writes=["gp"])

        xn_cnt = [0]

        def xnorm_sub(s, xb, nxb, xn, junk, dest, dest_res, psb, part=3, kk=None):
            if part & 1:
                k = xn_cnt[0]
            else:
                k = kk
            if part == 1:
                xn_cnt[0] += 1
            return _xnorm_sub(s, xb, nxb, xn, junk, dest, dest_res, psb, part, k)

        def _xnorm_sub(s, xb, nxb, xn, junk, dest, dest_res, psb, part, k):
            if part == 3:
                xn_cnt[0] += 1
            bx, bn, m = k % nxb, k % 2, k % 4
            if part & 1:
                _xnorm_front(s, xb, xn, junk, bx, bn, m)
            if part & 2:
                _xnorm_back(xn, dest, dest_res, psb, bn)
            return k

        def _xnorm_front(s, xb, xn, junk, bx, bn, m, reuse=False):
            dma(xb[:, bx], x[s * 128:(s + 1) * 128, :], writes=[("xb", bx)])
            if reuse:
                P.op("act", lambda e: e.activation(out=xn[:, bn], in_=xb[:, bx], func=AF.Copy, scale=rs_all[:, s:s + 1]),
                     reads=[("xb", bx)], writes=[("xn", bn)])
                return
            P.op("act", lambda e: e.activation(out=junk, in_=xb[:, bx], func=AF.Square, accum_out=ss1[:, m:m + 1]),
                 reads=[("xb", bx)], writes=["junk", ("ss1", m)])
            P.op("act", lambda e: e.activation(out=lnv1[:, m:m + 1], in_=ss1[:, m:m + 1], func=AF.Ln, bias=EPS, scale=1.0 / D),
                 reads=[("ss1", m)], writes=[("lnv1", m)])
            P.op("act", lambda e: e.activation(out=rs_all[:, s:s + 1], in_=lnv1[:, m:m + 1], func=AF.Exp, scale=-0.5),
                 reads=[("lnv1", m)], writes=[("rs_all", s)])
            P.op("act", lambda e: e.activation(out=xn[:, bn], in_=xb[:, bx], func=AF.Copy, scale=rs_all[:, s:s + 1]),
                 reads=[("xb", bx), ("rs_all", s)], writes=[("xn", bn)])

        def _xnorm_back(xn, dest, dest_res, psb, bn):
            psx = ps[:, psb:psb + 2, :].rearrange("p b (jj t) -> p (b jj) t", t=128)
            for j in range(8):
                P.op("pe", lambda e, j=j: e.transpose(out=psx[:, j, :], in_=xn[:, bn, j * 128:(j + 1) * 128], identity=ident_f),
                     reads=[("xn", bn), "ident_f"], writes=[("ps", psb + j // 4)])
            for j in range(8):
                P.op("dve", lambda e, j=j: e.tensor_scalar(out=dest(j), in0=psx[:, j, :], scalar1=gp[:, j:j + 1],
                                                           scalar2=shift[:, j:j + 1], op0=ALU.mult, op1=ALU.add),
                     reads=[("ps", psb + j // 4), "gp", "ada"], writes=[dest_res])

        def f_proj(i):
                pb = 5 + (i % 2)
                for j in range(8):
                    P.op("pe", lambda e, i=i, j=j, pb=pb: e.matmul(ps[0:8, pb, :], lhsT=wf_b[:, j, :], rhs=hT[:, j, i * 512:(i + 1) * 512],
                                                                   start=(j == 0), stop=(j == 7)),
                         reads=["wf_b", ("hT", i)], writes=[("ps", pb)])
                P.op("act", lambda e, i=i, pb=pb: e.activation(out=e_all[0:8, i * 512:(i + 1) * 512], in_=ps[0:8, pb, :], func=AF.Exp,
                                                               bias=nbf, scale=-1.0),
                     reads=[("ps", pb), "nbf"], writes=[("e_all", i)])

        for s in range(32):
            xnorm_sub(s, xbA, 3, xnA, junkA, lambda j, s=s: hT[:, j, s * 128:(s + 1) * 128], ("hT", s // 4),
                      2 * (s % 2))
        barrier()
        for i in range(NQT):
            f_proj(i)

        P.op("pool", lambda e: e.memset(ones512[0:8, :], 1.0), writes=["ones512"])
        P.op("act", lambda e: e.activation(out=e_all[0:8, :], in_=e_all[0:8, :], func=AF.Ln, bias=1.0, scale=1.0),
             reads=[("e_all", i) for i in range(NQT)], writes=["sp_all"])
        for i in range(NQT):
            init = 0.0 if i == 0 else F_all[0:8, i * 512 - 1:i * 512]
            P.op("dve", lambda e, i=i, init=init: e.tensor_tensor_scan(out=F_all[0:8, i * 512:(i + 1) * 512], data0=ones512[0:8, :],
                                                                       data1=e_all[0:8, i * 512:(i + 1) * 512], initial=init,
                                                                       op0=ALU.mult, op1=ALU.subtract),
                 reads=["sp_all", "ones512", "F_all"], writes=["F_all"])
        P.op("dve", lambda e: e.tensor_copy(out=HMLa[0:8, 0], in_=F_all[0:8, :]), reads=["F_all"], writes=["H0"])
        P.op("dve", lambda e: e.tensor_tensor(out=e_all[0:8, :], in0=F_all[0:8, :], in1=HMLa[0:8, 0], op=ALU.subtract),
             reads=["F_all", "H0", "sp_all"], writes=["R1a"])
        P.op("dve", lambda e: e.tensor_copy(out=HMLa[0:8, 1], in_=e_all[0:8, :]), reads=["R1a"], writes=["H1"])
        P.op("dve", lambda e: e.tensor_tensor(out=F_all[0:8, :], in0=e_all[0:8, :], in1=HMLa[0:8, 1], op=ALU.subtract),
             reads=["R1a", "H1", "F_all"], writes=["R2a"])
        P.op("dve", lambda e: e.tensor_copy(out=HMLa[0:8, 2], in_=F_all[0:8, :]), reads=["R2a"], writes=["H2"])
        for r in range(3):
            dma(Frows[8 * r:8 * r + 8, :], HMLa[0:8, r], reads=["H%d" % r], writes=[("Frows", i) for i in range(NQT)])
        if "Frows" in debug:
            dbg["Frows"] = dram("dbg_Frows", [24, S], kind="ExternalOutput", dtype=BF16)
            dma(dbg["Frows"][:, :], Frows[0:24, :], reads=[("Frows", i) for i in range(NQT)])
        if "hT" in debug:
            dbg["hT"] = dram("dbg_hT", [128, 8, S], kind="ExternalOutput", dtype=BF16)
            dma(dbg["hT"][:, :, :], hT, reads=[("hT", i) for i in range(NQT)])
        barrier()

        P.op("pool", lambda e: e.memset(KA[64:70, :, :], 1.0), writes=["KAaug"])
        P.op("pool", lambda e: e.memset(QA[64:70, :, :, :], -1.0), writes=["QAaug"])
        P.op("pool", lambda e: e.memset(VA[:, :, 0, 64:128], 1.0), writes=["VAones"])
        P.op("pool", lambda e: e.memset(VA[:, :, 1, 0:64], 1.0), writes=["VAones"])
        from collections import deque

        def load_pair_weights(cols, wb, ce="pool"):
            for pi, c0 in enumerate(cols):
                b = load_pair_weights.n % 2
                load_pair_weights.n += 1
                dma(wst[:, b], w_in_v[:, :, c0:c0 + 128], writes=[("wst", b)])
                if ce == "act":
                    P.op("act", lambda e, b=b, pi=pi: e.activation(out=wbf[:, wb, :, pi * 128:(pi + 1) * 128], in_=wst[:, b], func=AF.Copy),
                         reads=[("wst", b)], writes=[("wbf", wb, pi)])
                else:
                    P.op("pool", lambda e, b=b, pi=pi: e.tensor_copy(out=wbf[:, wb, :, pi * 128:(pi + 1) * 128], in_=wst[:, b]),
                         reads=[("wst", b)], writes=[("wbf", wb, pi)])
        load_pair_weights.n = 0

        def proj_fm(wb, pi, i, pb, js=range(8)):
            for j in js:
                P.op("pe", lambda e, j=j: e.matmul(ps[:, pb, :], lhsT=wbf[:, wb, j, pi * 128:(pi + 1) * 128],
                                                   rhs=hT[:, j, i * 512:(i + 1) * 512], start=(j == 0), stop=(j == 7)),
                     reads=[("wbf", wb, pi), ("hT", i)], writes=[("ps", pb)])

        def chain_tasks(wb, pi, i, pb, b, gcol, dest0, dest1, res0, res1):
            t = []
            t.append(lambda: proj_fm(wb, pi, i, pb, range(0, 4)))
            t.append(lambda: proj_fm(wb, pi, i, pb, range(4, 8)))

            def t_copy():
                P.op("dve", lambda e: e.tensor_copy(out=qraw[:, b], in_=ps[:, pb, :]), reads=[("ps", pb)], writes=[("qraw", b)])
                P.op("pool", lambda e: e.tensor_tensor(out=sq[:, b], in0=qraw[:, b], in1=qraw[:, b], op=ALU.mult),
                     reads=[("qraw", b)], writes=[("sq", b)])
            t.append(t_copy)

            def t_ss():
                P.op("pe", lambda e: e.matmul(ps[:, pb, :], lhsT=bdiag, rhs=sq[:, b], start=True, stop=True),
                     reads=[("sq", b), "bdiag"], writes=[("ps", pb)])

            def t_act():
                P.op("act", lambda e: e.activation(out=lnv[:, b], in_=ps[:, pb, :], func=AF.Ln, bias=EPS, scale=1.0 / 64),
                     reads=[("ps", pb)], writes=[("lnv", b)])
                P.op("act", lambda e: e.activation(out=rstd[:, b], in_=lnv[:, b], func=AF.Exp, scale=-0.5),
                     reads=[("lnv", b)], writes=[("rstd", b)])

            def t_out():
                P.op("dve", lambda e: e.scalar_tensor_tensor(out=dest0, in0=qraw[0:64, b], scalar=gcol[0:64], in1=rstd[0:64, b],
                                                             op0=ALU.mult, op1=ALU.mult),
                     reads=[("qraw", b), ("rstd", b), "gq8", "gqk"], writes=[res0])
                P.op("dve", lambda e: e.scalar_tensor_tensor(out=dest1, in0=qraw[64:128, b], scalar=gcol[64:128], in1=rstd[64:128, b],
                                                             op0=ALU.mult, op1=ALU.mult),
                     reads=[("qraw", b), ("rstd", b), "gq8", "gqk"], writes=[res1])
            return t, t_ss, t_act, t_out

        def prep_tasks(g, wb, i, late_writes=False):
            qb = i % 2
            tasks = []

            def t_dma():
                for hh in range(2):
                    for r in range(3):
                        dma(QA[64 + r:65 + r, qb, hh, :], Frows[8 * r + 2 * g + hh:8 * r + 2 * g + hh + 1, i * 512:(i + 1) * 512],
                            reads=[("Frows", i), "QAaug"], writes=[("QAF", qb, hh)])
            tasks.append(t_dma)
            kt, k_ss, k_act, k_out = chain_tasks(wb, 1, i, 6, 0, gk, KA[0:64, 0, i * 512:(i + 1) * 512], KA[0:64, 1, i * 512:(i + 1) * 512],
                                                 ("KA", 0, i), ("KA", 1, i))
            qt, q_ss, q_act, q_out = chain_tasks(wb, 0, i, 7, 1, gq8, QA[0:64, qb, 0, :], QA[0:64, qb, 1, :], ("QA", qb, 0), ("QA", qb, 1))
            tasks += kt
            tasks += qt
            for ssb in range(4):
                def t_v(ssb=ssb):
                    s = 4 * i + ssb
                    for j in range(8):
                        P.op("pe", lambda e, j=j: e.matmul(ps[:, 6, ssb * 128:(ssb + 1) * 128], lhsT=hT[:, j, s * 128:(s + 1) * 128],
                                                           rhs=wbf[:, wb, j, 256:384], start=(j == 0), stop=(j == 7)),
                             reads=[("wbf", wb, 2), ("hT", i)], writes=[("ps", 6)])
                t_v.is_v = True
                tasks.append(t_v)

            def t_vcopy():
                P.op("dve", lambda e: e.tensor_copy(out=VA[:, 4 * i:4 * i + 4, 0, 0:64],
                                                    in_=ps[:, 6, :].rearrange("p (s h f) -> p s h f", s=4, h=2)[:, :, 0, :]),
                     reads=[("ps", 6)], writes=[("VA", i)])
                P.op("dve", lambda e: e.tensor_copy(out=VA[:, 4 * i:4 * i + 4, 1, 64:128],
                                                    in_=ps[:, 6, :].rearrange("p (s h f) -> p s h f", s=4, h=2)[:, :, 1, :]),
                     reads=[("ps", 6)], writes=[("VA", i)])
            t_vcopy.is_v = True
            tasks.append(t_vcopy)
            tasks.append(k_ss)
            tasks.append(q_ss)
            tasks.append(k_act)
            tasks.append(q_act)
            tasks.append(q_out)
            tasks.append(k_out)
            if late_writes:
                vt = [t for t in tasks if getattr(t, "is_v", False)]
                for t in vt:
                    tasks.remove(t)
                tasks += vt
            return tasks

        for g in range(n_pairs):
            wb = g % 2
            if g == 0:
                load_pair_weights([COL_Q + g * 128, COL_K + g * 128, COL_V + g * 128], wb)
            for hh in range(2):
                for r in range(3):
                    dma(KA[67 + r:68 + r, hh, :], Frows[8 * r + 2 * g + hh:8 * r + 2 * g + hh + 1, :],
                        reads=[("Frows", i) for i in range(NQT)], writes=[("KAF", hh)])

            units = []
            for i in range(NQT):
                for hh in range(2):
                    for kb in range(4 * i + 4):
                        units.append((i, hh, kb))

            def qk(n, g=g):
                i, hh, kb = units[n]
                sb = n % 4
                qb = i % 2
                jd = kb - 4 * i
                c0 = max(jd, 0) * 128
                P.op("pe", lambda e: e.matmul(ps[:, sb, c0:512], lhsT=KA[0:70, hh, kb * 128:(kb + 1) * 128],
                                              rhs=QA[0:70, qb, hh, c0:512], start=True, stop=(jd < 0)),
                     reads=[("KA", hh, kb // 4), ("KAF", hh), "KAaug", ("QA", qb, hh), ("QAF", qb, hh), "QAaug"],
                     writes=[("ps", sb)])
                if jd >= 0:
                    P.op("pe", lambda e: e.matmul(ps[:, sb, c0:c0 + 128], lhsT=maskL, rhs=ident_b, start=False, stop=True),
                         reads=["maskL", "ident_b"], writes=[("ps", sb)])

            def exp_pv(n, g=g):
                i, hh, kb = units[n]
                sb, pbuf = n % 4, n % 4
                ob = 4 + ((2 * i + hh) % 2)
                jd = kb - 4 * i
                c0 = max(jd, 0) * 128
                last = (kb == 4 * i + 3)
                P.op("act", lambda e: e.activation(out=PT[:, pbuf, c0:512], in_=ps[:, sb, c0:512], func=AF.Exp),
                     reads=[("ps", sb)], writes=[("PT", pbuf)])
                P.op("pe", lambda e: e.matmul(ps[:, ob, c0:512], lhsT=VA[:, kb, hh, :], rhs=PT[:, pbuf, c0:512],
                                              start=(kb == 0), stop=last),
                     reads=[("VA", kb // 4), "VAones", ("PT", pbuf)], writes=[("ps", ob)])
                if last:
                    rb = (2 * i + hh) % 2
                    qb = i % 2
                    lo, hi = (0, 64) if hh == 0 else (64, 128)
                    dlo, dhi = (64, 128) if hh == 0 else (0, 64)
                    P.op("dve", lambda e: e.reciprocal(out=rden[lo:hi, rb], in_=ps[dlo:dhi, ob, :]),
                         reads=[("ps", ob)], writes=[("rden", rb)])
                    P.op("dve", lambda e: e.tensor_tensor(out=oaT[lo:hi, g, i * 512:(i + 1) * 512], in0=ps[lo:hi, ob, :],
                                                          in1=rden[lo:hi, rb], op=ALU.mult),
                         reads=[("ps", ob), ("rden", rb)], writes=[("oaT", g, i, hh)])

            side = deque()
            if g == 0:
                for t in prep_tasks(g, wb, 0):
                    t()
            first_unit_of_tile = {}
            for n, (i, hh, kb) in enumerate(units):
                if hh == 0 and kb == 0:
                    first_unit_of_tile[i] = n
            qk(0)
            qk(1)
            qk(2)
            for n in range(len(units)):
                i, hh, kb = units[n]
                if hh == 0 and kb == 0:
                    if i + 1 < NQT:
                        side.extend(prep_tasks(g, wb, i + 1))
                    elif g + 1 < n_pairs:
                        g1 = g + 1
                        side.append(lambda g1=g1: load_pair_weights([COL_Q + g1 * 128, COL_K + g1 * 128, COL_V + g1 * 128], g1 % 2))
                        side.extend(prep_tasks(g1, g1 % 2, 0, late_writes=True))
                    elif do_conv:
                        side.append(lambda: load_pair_weights([COL_GB, COL_GC, COL_U, COL_ZB], (n_pairs - 1 + 1) % 2))
                if n + 3 < len(units):
                    i2 = units[n + 3][0]
                    if i2 != i and first_unit_of_tile.get(i2) == n + 3:
                        while side:
                            side.popleft()()
                    qk(n + 3)
                exp_pv(n)
                ntile = 2 * (4 * i + 4)
                per = max(1, -(-28 // max(ntile - 4, 1)))
                for _ in range(per):
                    if side:
                        side.popleft()()
            while side:
                side.popleft()()

        if "oaT" in debug:
            dbg["oaT"] = dram("dbg_oaT", [128, 4, S], kind="ExternalOutput", dtype=BF16)
            dma(dbg["oaT"][:, :, :], oaT, reads=[("oaT", g, i, hh) for g in range(n_pairs) for i in range(NQT) for hh in range(2)])
        if "KA" in debug:
            dbg["KA"] = dram("dbg_KA", [70, 2, S], kind="ExternalOutput", dtype=BF16)
            dma(dbg["KA"][:, :, :], KA[0:70], reads=[("KA", hh, i) for hh in range(2) for i in range(NQT)] + [("KAF", 0), ("KAF", 1), "KAaug"])
        barrier()

        if do_conv:
            sa = n_pairs % 2
            sb = 1 - sa
            if n_pairs == 0:
                load_pair_weights([COL_GB, COL_GC, COL_U, COL_ZB], sa)

            def conv_w(c, slot):
                load_pair_weights([COL_GB + c * 128, COL_GC + c * 128, COL_U + c * 128, COL_ZB + c * 128], slot, ce="act")

            def zgate(wbz):
                nz = 0
                for gg in range(n_pairs):
                    for i in range(NQT):
                        b = nz % 2
                        pbz = 5 + (nz % 2)
                        nz += 1
                        proj_fm(wbz, gg, i, pbz)
                        P.op("act", lambda e, b=b, pbz=pbz: e.activation(out=thz[:, b], in_=ps[:, pbz, :], func=AF.Tanh, scale=0.5),
                             reads=[("ps", pbz)], writes=[("thz", b)])
                        P.op("dve", lambda e, b=b, pbz=pbz: e.scalar_tensor_tensor(out=gcs[:, b], in0=thz[:, b], scalar=1.0, in1=ps[:, pbz, :],
                                                                                  op0=ALU.add, op1=ALU.mult),
                             reads=[("thz", b), ("ps", pbz)], writes=[("gcs", b)])
                        P.op("pool", lambda e, b=b, gg=gg, i=i: e.tensor_tensor(out=oaT[:, gg, i * 512:(i + 1) * 512], in0=oaT[:, gg, i * 512:(i + 1) * 512],
                                                                              in1=gcs[:, b], op=ALU.mult),
                             reads=[("gcs", b)], writes=[("oaTz", gg, i)])

            def conv_chunk(c, wb):
                for i in range(NQT):
                    par = i % 2
                    b = i % 2
                    proj_fm(wb, 1, i, 0)
                    proj_fm(wb, 2, i, 1)
                    P.op("act", lambda e, b=b: e.activation(out=gcs[:, b], in_=ps[:, 0, :], func=AF.Copy),
                         reads=[("ps", 0)], writes=[("gcs", b)])
                    if i == 0:
                        P.op("pool", lambda e, par=par: e.memset(gcu[:, par, 0:2], 0.0), writes=[("gcu", par)])
                    else:
                        P.op("pool", lambda e, par=par: e.tensor_copy(out=gcu[:, par, 0:2], in_=gcu[:, 1 - par, 512:514]),
                             reads=[("gcu", 1 - par)], writes=[("gcu", par)])
                    P.op("dve", lambda e, b=b, par=par: e.tensor_tensor(out=gcu[:, par, 2:514], in0=ps[:, 1, :], in1=gcs[:, b], op=ALU.mult),
                         reads=[("ps", 1), ("gcs", b), ("gcu", par)], writes=[("gcu", par)])
                    proj_fm(wb, 0, i, 2)
                    proj_fm(wb, 3, i, 3)
                    for k in range(3):
                        P.op("pe", lambda e, k=k, par=par, c=c: e.matmul(ps[:, 4, :], lhsT=convdiag[:, c * 3 + k, :], rhs=gcu[:, par, k:k + 512],
                                                                         start=(k == 0), stop=(k == 2)),
                             reads=[("convdiag", c * 3 + k), ("gcu", par)], writes=[("ps", 4)])
                    P.op("act", lambda e, b=b: e.activation(out=thz[:, b], in_=ps[:, 3, :], func=AF.Tanh, scale=0.5),
                         reads=[("ps", 3)], writes=[("thz", b)])
                    P.op("dve", lambda e, b=b: e.scalar_tensor_tensor(out=a1[:, b], in0=thz[:, b], scalar=1.0, in1=ps[:, 3, :],
                                                                     op0=ALU.add, op1=ALU.mult),
                         reads=[("thz", b), ("ps", 3)], writes=[("a1", b)])
                    P.op("dve", lambda e, b=b: e.tensor_tensor(out=a2[:, b], in0=ps[:, 2, :], in1=a1[:, b], op=ALU.mult),
                         reads=[("ps", 2), ("a1", b)], writes=[("a2", b)])
                    P.op("dve", lambda e, b=b, c=c, i=i: e.tensor_tensor(out=obT[:, c, i * 512:(i + 1) * 512], in0=ps[:, 4, :], in1=a2[:, b], op=ALU.mult),
                         reads=[("ps", 4), ("a2", b)], writes=[("obT", c, i)])

            load_pair_weights([COL_ZA + gg * 128 for gg in range(4)], sb, ce="act")
            conv_chunk(0, sa)
            conv_w(1, sa)
            zgate(sb)
            conv_w(2, sb)
            conv_chunk(1, sa)
            conv_w(3, sa)
            conv_chunk(2, sb)
            conv_chunk(3, sa)
            if "obT" in debug:
                dbg["obT"] = dram("dbg_obT", [128, 4, S], kind="ExternalOutput", dtype=BF16)
                dma(dbg["obT"][:, :, :], obT, reads=[("obT", c, i) for c in range(4) for i in range(NQT)])
        barrier()

        out_dmas = []
        if do_b:
            for j in range(8):
                P.op("dve", lambda e, j=j: e.tensor_scalar(out=Dj, in0=ident_f, scalar1=gate[:, j:j + 1], scalar2=0.25,
                                                           op0=ALU.mult, op1=ALU.mult),
                     reads=["ident_f", "ada"], writes=["Dj"])
                P.op("pe", lambda e, j=j: e.matmul(ps[:, 2 + j // 4, (j % 4) * 128:(j % 4 + 1) * 128], lhsT=ones_f, rhs=Dj, start=True, stop=True),
                     reads=["ones_f", "Dj"], writes=[("ps", 2 + j // 4)])
            for hb in range(2):
                P.op("dve", lambda e, hb=hb: e.tensor_copy(out=gate_bc[:, hb * 512:(hb + 1) * 512], in_=ps[:, 2 + hb, :]),
                     reads=[("ps", 2 + hb)], writes=["gate_bc"])

            def xn_tile(it):
                hb = it % 2
                for sub in range(4):
                    s_ = it * 4 + sub
                    kk0 = xn_cnt[0]
                    xn_cnt[0] += 1
                    _xnorm_front(s_, xbB, xnB, junkB, kk0 % 2, kk0 % 2, kk0 % 4, reuse=True)
                    _xnorm_back(xnB, lambda j, sub=sub: hTt[:, hb, j, sub * 128:(sub + 1) * 128], ("hTt", hb), 0, kk0 % 2)

            xn_tile(0)
            nld = [0]
            cast_engs = ["dve", "act"]

            def load_w(src_ap, shape3, dst, wres, mul_gate=None):
                b = nld[0] % 3
                ce = cast_engs[nld[0] % 2]
                nld[0] += 1
                view = wstB[:, b].rearrange("p (j n) -> p j n", j=shape3)
                dma(view, src_ap, writes=[("wstB", b)])
                if mul_gate is not None:
                    P.op("pool" if ce == "act" else ce, lambda e: e.tensor_tensor(out=dst, in0=view, in1=mul_gate, op=ALU.mult),
                         reads=[("wstB", b), "gate_bc"], writes=[wres])
                elif ce == "act":
                    P.op("act", lambda e: e.activation(out=dst, in_=view, func=AF.Copy), reads=[("wstB", b)], writes=[wres])
                else:
                    P.op(ce, lambda e: e.tensor_copy(out=dst, in_=view), reads=[("wstB", b)], writes=[wres])

            w_a_v = w_a.rearrange("(c p) n -> p c n", p=128)
            w_b_v = w_b.rearrange("(c p) n -> p c n", p=128)
            w_o_v = w_o.rearrange("(j p) n -> p j n", p=128)
            for fc in range(8):
                for cch in (fc, 8 + fc):
                    c0 = COL_GA + cch * 128
                    load_w(w_in_v[:, :, c0:c0 + 128], 8, Wg[:, :, cch * 128:(cch + 1) * 128], ("Wg", cch))
                if fc % 2 == 0:
                    q4 = fc // 2
                    load_w(w_a_v[:, :, q4 * 256:(q4 + 1) * 256], 4, WA[:, :, q4 * 256:(q4 + 1) * 256], ("WA", q4))
                    load_w(w_b_v[:, :, q4 * 256:(q4 + 1) * 256], 4, WB[:, :, q4 * 256:(q4 + 1) * 256], ("WB", q4))
            for q8 in range(8):
                load_w(w_o_v[:, :, q8 * 128:(q8 + 1) * 128], 8, WO[:, :, q8 * 128:(q8 + 1) * 128], ("WO", q8),
                       mul_gate=gate_bc[:, q8 * 128:(q8 + 1) * 128].unsqueeze(1).to_broadcast([128, 8, 128]))

            xk = {}
            for it in range(NQT):
                hb = it % 2
                t0 = it * 512
                def gates_mm(hb_, fc_):
                    g0_ = 2 if fc_ % 2 == 0 else 4
                    for half in range(2):
                        for j in range(8):
                            P.op("pe", lambda e, j=j, half=half: e.matmul(
                                ps[:, g0_ + half, :], lhsT=Wg[:, j, half * 1024 + fc_ * 128:half * 1024 + (fc_ + 1) * 128],
                                rhs=hTt[:, hb_, j, :], start=(j == 0), stop=(j == 7)),
                                reads=[("hTt", hb_), ("Wg", half * 8 + fc_)], writes=[("ps", g0_ + half)])

                for fc in range(8):
                    g0 = 2 if fc % 2 == 0 else 4
                    tb_ = fc % 2
                    if not (fc == 0 and it > 0):
                        gates_mm(hb, fc)
                    for c in range(4):
                        P.op("pe", lambda e, c=c: e.matmul(ps[:, 6, :], lhsT=WA[:, c, fc * 128:(fc + 1) * 128],
                                                           rhs=oaT[:, c, t0:t0 + 512], start=(c == 0), stop=(c == 3)),
                             reads=[("WA", fc // 2)], writes=[("ps", 6)])
                    for c in range(4):
                        P.op("pe", lambda e, c=c: e.matmul(ps[:, 7, :], lhsT=WB[:, c, fc * 128:(fc + 1) * 128],
                                                           rhs=obT[:, c, t0:t0 + 512], start=(c == 0), stop=(c == 3)),
                             reads=[("WB", fc // 2)], writes=[("ps", 7)])
                    P.op("act", lambda e: e.activation(out=thg[:, tb_].rearrange("p (a t) -> p a t", a=2), in_=ps[:, g0:g0 + 2, :],
                                                       func=AF.Tanh, scale=0.5),
                         reads=[("ps", g0), ("ps", g0 + 1)], writes=[("thg", tb_)])
                    P.op("dve", lambda e: e.scalar_tensor_tensor(out=t12[:, tb_].rearrange("p (a t) -> p a t", a=2),
                                                                 in0=thg[:, tb_].rearrange("p (a t) -> p a t", a=2), scalar=1.0,
                                                                 in1=ps[:, 6:8, :], op0=ALU.add, op1=ALU.mult),
                         reads=[("thg", tb_), ("ps", 6), ("ps", 7)], writes=[("t12", tb_)])
                    P.op("pool", lambda e: e.tensor_tensor(out=mT[:, fc, :], in0=t12[:, tb_, 0:512], in1=t12[:, tb_, 512:1024], op=ALU.add),
                         reads=[("t12", tb_)], writes=[("mT", fc)])
                    if it + 1 < NQT:
                        hb1 = (it + 1) % 2
                        back_at = {2: [0], 3: [1], 4: [2], 5: [3]}
                        front_at = {0: 0, 1: 1, 2: 2, 3: 3}
                        for sub in back_at.get(fc, []):
                            _xnorm_back(xnB, lambda j, sub=sub: hTt[:, hb1, j, sub * 128:(sub + 1) * 128], ("hTt", hb1), 0, xk[sub] % 2)
                        if fc in front_at:
                            sub = front_at[fc]
                            kk_ = xn_cnt[0]
                            xn_cnt[0] += 1
                            xk[sub] = kk_
                            _xnorm_front((it + 1) * 4 + sub, xbB, xnB, junkB, kk_ % 2, kk_ % 2, kk_ % 4, reuse=True)
                if it == 0 and "mT0" in debug:
                    dbg["mT0"] = dram("dbg_mT0", [128, 8, 512], kind="ExternalOutput", dtype=BF16)
                    dma(dbg["mT0"], mT, reads=[("mT", fc) for fc in range(8)])
                if it + 1 < NQT:
                    gates_mm((it + 1) % 2, 0)
                for sub in range(4):
                    s_ = it * 4 + sub
                    bx = s_ % 3
                    dma(xr[:, bx], x[s_ * 128:(s_ + 1) * 128, :], writes=[("xr", bx)] + ([("wstB", 0), ("wstB", 1), ("wstB", 2)] if s_ < 3 else []))
                    for half in range(2):
                        po = (2 * sub + half) % 2
                        for fc in range(8):
                            P.op("pe", lambda e, fc=fc: e.matmul(ps[:, po, :], lhsT=mT[:, fc, sub * 128:(sub + 1) * 128],
                                                                 rhs=WO[:, fc, half * 512:(half + 1) * 512], start=(fc == 0), stop=(fc == 7)),
                                 reads=[("mT", fc)] + [("WO", q8) for q8 in range(4 * half, 4 * half + 4)], writes=[("ps", po)])
                        P.op("dve", lambda e: e.tensor_tensor(out=xr[:, bx, half * 512:(half + 1) * 512], in0=ps[:, po, :],
                                                              in1=xr[:, bx, half * 512:(half + 1) * 512], op=ALU.add),
                             reads=[("ps", po), ("xr", bx)], writes=[("xr", bx)])
                    out_dmas.append(dma(y[s_ * 128:(s_ + 1) * 128, :], xr[:, bx], reads=[("xr", bx)]))
        P.emit(nc, final_wait_ops=[o for o in P.ops if o.is_dma])
    return nc, dbg


def host_inputs(inputs, b):
    f = lambda a: np.ascontiguousarray(a, dtype=np.float32)
    c = inputs["c"][b]
    cT = c.reshape(8, 128).T
    gq = np.tile(np.asarray(inputs["q_norm_g"][0]), 2)
    gk = np.tile(np.asarray(inputs["k_norm_g"][0]), 2)
    return {
        "x": f(inputs["x"][b]),
        "cT2": f(np.repeat(cT, 2, axis=1)),
        "w_ada": f(inputs["w_ada"][0]),
        "b_adaT": f(np.asarray(inputs["b_ada"][0]).reshape(24, 128).T),
        "norm_gT": f(np.asarray(inputs["norm_g"][0]).reshape(8, 128).T),
        "w_in": f(inputs["w_in"][0]),
        "b_f": f(np.asarray(inputs["b_f"][0]).reshape(8, 1)),
        "gqk": f(np.stack([gq, gk], axis=1)),
        "conv_wT": f(np.asarray(inputs["conv_w"][0]).reshape(3, 4, 128).transpose(2, 1, 0).reshape(128, 12)),
        "w_a": f(inputs["w_attn_out"][0]),
        "w_b": f(inputs["w_conv_out"][0]),
        "w_o": f(inputs["w_o"][0]),
    }


def kernel(**inputs):
    inputs = {k: np.asarray(v) for k, v in inputs.items()}
    nc, _ = build()
    in_maps = [host_inputs(inputs, b) for b in range(8)]
    res = run_bass_kernel_spmd(nc, in_maps, core_ids=list(range(8)))
    return np.stack([np.asarray(r["y"], dtype=np.float32) for r in res.results], axis=0)
```

```python
import contextlib
import numpy as np
import concourse.bass as bass
import concourse.mybir as mybir
from concourse.bass_utils import run_bass_kernel_spmd

F32 = mybir.dt.float32
BF16 = mybir.dt.bfloat16
AF = mybir.ActivationFunctionType
ALU = mybir.AluOpType

S, D = 4096, 1024
NQT = 8
EPS = 1e-6
COL_Q, COL_K, COL_V, COL_F, COL_ZA = 0, 512, 1024, 1536, 1544
COL_GB, COL_GC, COL_U, COL_ZB, COL_GA, COL_GG = 2056, 2568, 3080, 3592, 4104, 5128
NCOL = 6152
ARENA_W = 53000
TB = 256

ENGS = ("pe", "act", "dve", "pool", "sp")
N_DMA_SEMS = 24


class Op:
    __slots__ = ("eng", "fn", "waits", "signal", "cnt", "dma_slot", "dma_val", "is_dma", "idx")


class _Rec:
    def __getattr__(self, name):
        def f(*a, **k):
            self.call = (name, a, k)
            return self
        return f


class Prog:
    def __init__(self):
        self.ops = []
        self.last_w = {}
        self.readers = {}
        self.n_dma = 0
        self.last_dma_on_slot = {}

    def op(self, eng, fn, reads=(), writes=(), dma=False, disjoint=False):
        o = Op()
        rec = _Rec()
        fn(rec)
        call = rec.call
        fn = lambda e, call=call: getattr(e, call[0])(*call[1], **call[2])
        o.eng, o.fn, o.signal, o.cnt, o.is_dma = eng, fn, False, None, dma
        o.idx = len(self.ops)
        deps = set()
        for r in reads:
            w = self.last_w.get(r)
            if w is not None:
                deps.add(w)
        def same_eng(d):
            q = self.ops[d]
            return q.eng == eng and not q.is_dma and not dma
        for r in writes:
            w = self.last_w.get(r)
            if w is not None and not (disjoint and same_eng(w)):
                deps.add(w)
            for rd in self.readers.get(r, {}).values():
                deps.add(rd)
        o.waits = []
        for d in deps:
            p = self.ops[d]
            if p.eng == "pe" and eng == "pe" and not p.is_dma and not dma:
                continue
            p.signal = True
            o.waits.append(d)
        if dma:
            o.dma_slot = self.n_dma % N_DMA_SEMS
            o.dma_val = 16 * (self.n_dma // N_DMA_SEMS + 1)
            self.n_dma += 1
            self.last_dma_on_slot[o.dma_slot] = o.idx
        key = eng if not dma else ("dma", o.idx)
        for r in reads:
            self.readers.setdefault(r, {})[key] = o.idx
        for r in writes:
            self.last_w[r] = o.idx
            self.readers[r] = {}
        self.ops.append(o)
        return o

    def barrier(self, mk):
        last = {}
        for o in self.ops:
            if not o.is_dma:
                last[o.eng] = o.idx
        dmas = list(self.last_dma_on_slot.values())
        bops = []
        for e in ENGS:
            o = self.op(e, mk[e])
            for d in list(last.values()) + dmas:
                self.ops[d].signal = True
                o.waits.append(d)
            bops.append(o)
        self.last_w = {}
        self.readers = {}

    def emit(self, nc, final_wait_ops=()):
        engs = {"pe": nc.tensor, "act": nc.scalar, "dve": nc.vector, "pool": nc.gpsimd, "sp": nc.sync}
        cnt = {e: 0 for e in ENGS}
        for o in self.ops:
            if o.is_dma:
                continue
            if o.signal:
                cnt[o.eng] += 1
                o.cnt = cnt[o.eng]
        with contextlib.ExitStack() as st:
            sems = {e: st.enter_context(nc.semaphore("s_" + e)) for e in ENGS}
            dsems = [st.enter_context(nc.semaphore("d_%d" % i)) for i in range(N_DMA_SEMS)]
            block = st.enter_context(nc.Block())
            ops = self.ops

            def body(ename):
                eng = engs[ename]
                waited = {}

                def need(key, sem, val, pend):
                    if waited.get(key, 0) >= val:
                        return
                    waited[key] = val
                    pend.append((sem, val))

                for o in ops:
                    if o.eng != ename:
                        continue
                    pend = []
                    for d in o.waits:
                        p = ops[d]
                        if p.is_dma:
                            need(("d", p.dma_slot), dsems[p.dma_slot], p.dma_val, pend)
                        else:
                            need(("e", p.eng), sems[p.eng], p.cnt, pend)
                    if o.is_dma and o.dma_val > 16:
                        need(("d", o.dma_slot), dsems[o.dma_slot], o.dma_val - 16, pend)
                    for sem, val in pend:
                        eng.wait_ge(sem, val)
                    inst = o.fn(eng)
                    if o.is_dma:
                        inst.then_inc(dsems[o.dma_slot], 16)
                    elif o.signal:
                        inst.then_inc(sems[o.eng], 1)
                if ename == "sp":
                    for o in final_wait_ops:
                        if waited.get(("d", o.dma_slot), 0) < o.dma_val:
                            eng.wait_ge(dsems[o.dma_slot], o.dma_val)
                            waited[("d", o.dma_slot)] = o.dma_val

            @block.tensor
            def _(e):
                body("pe")

            @block.scalar
            def _(e):
                body("act")

            @block.vector
            def _(e):
                body("dve")

            @block.gpsimd
            def _(e):
                body("pool")

            @block.sync
            def _(e):
                body("sp")


class Arena:
    def __init__(self, ar, total):
        self.ar, self.total, self.off = ar, total, 0

    def f32(self, n):
        a = self.ar[:, self.off:self.off + n]
        self.off += n
        assert self.off <= self.total, (self.off, self.total)
        return a

    def bf16(self, n_el):
        w = (n_el + 1) // 2
        a = self.ar[:, self.off:self.off + w].bitcast(BF16)
        self.off += w
        assert self.off <= self.total, (self.off, self.total)
        return a


def build(debug=(), n_pairs=4, do_conv=True, do_b=True):
    nc = bass.Bass("TRN2", target_bir_lowering=False)

    def dram(name, shape, kind="ExternalInput", dtype=F32):
        return nc.dram_tensor(name, shape, dtype, kind=kind).ap()

    x = dram("x", [S, D])
    cT2 = dram("cT2", [128, 16])
    w_ada = dram("w_ada", [D, 3 * D])
    b_adaT = dram("b_adaT", [128, 24])
    norm_gT = dram("norm_gT", [128, 8])
    w_in = dram("w_in", [D, NCOL])
    b_f = dram("b_f", [8, 1])
    gqk = dram("gqk", [128, 2])
    conv_wT = dram("conv_wT", [128, 12])
    w_a = dram("w_a", [512, D])
    w_b = dram("w_b", [512, D])
    w_o = dram("w_o", [D, D])
    y = dram("y", [S, D], kind="ExternalOutput")
    dbg = {}
    w_in_v = w_in.rearrange("(j p) n -> p j n", p=128)
    w_ada_v = w_ada.rearrange("(j p) n -> p j n", p=128)

    P = Prog()
    st = contextlib.ExitStack()
    with st:
        AR = st.enter_context(nc.sbuf_tensor("AR", [128, ARENA_W], F32))
        ps = st.enter_context(nc.psum_tensor("ps", [128, 8, 512], F32))
        A = Arena(AR, ARENA_W)

        ident_f = A.f32(128)
        ones_f = A.f32(128)
        vec = A.f32(128)
        cT_s = vec[:, 0:16]
        bada_s = vec[:, 16:40]
        ng_s = vec[:, 40:48]
        ada = vec[:, 48:72]
        shift = ada[:, 0:8]
        scale = ada[:, 8:16]
        gate = ada[:, 16:24]
        gp = vec[:, 72:80]
        gqk_s = vec[:, 80:82]
        gq8 = vec[:, 82:83]
        gk = vec[:, 81:82]
        cw_s = vec[:, 84:96]
        bf_s = vec[0:8, 96:97]
        nbf = vec[0:8, 97:98]
        ss1 = vec[:, 100:104]
        lnv1 = vec[:, 104:108]
        rstd1 = vec[:, 108:112]
        ident_b = A.bf16(128)
        maskL = A.bf16(128)
        bdiag = A.bf16(128)
        convdiag = A.bf16(12 * 128).rearrange("p (c f) -> p c f", c=12)
        wf_b = A.bf16(64).rearrange("p (j f) -> p j f", j=8)
        wf_s = A.f32(64).rearrange("p (j f) -> p j f", j=8)
        oaT = A.bf16(4 * S).rearrange("p (c t) -> p c t", c=4)
        rs_all = A.f32(32)
        base_off = A.off

        hT = A.bf16(8 * S).rearrange("p (j t) -> p j t", j=8)
        kv_off = A.off
        KA = A.bf16(2 * S).rearrange("p (h t) -> p h t", h=2)
        VA = A.bf16(32 * 2 * 128).rearrange("p (s h f) -> p s h f", s=32, h=2)
        rest_off = A.off
        QA = A.bf16(2 * 2 * 512).rearrange("p (b h t) -> p b h t", b=2, h=2)
        zs = A.bf16(2 * 512).rearrange("p (b t) -> p b t", b=2)
        wbf = A.bf16(2 * 8 * 512).rearrange("p (b j n) -> p b j n", b=2, j=8)
        wst = A.f32(2 * 8 * 128).rearrange("p (b j n) -> p b j n", b=2, j=8)
        Frows = A.bf16(S)
        sc_off = A.off
        qraw = A.f32(1024).rearrange("p (b t) -> p b t", b=2)
        sq = A.bf16(1024).rearrange("p (b t) -> p b t", b=2)
        lnv = A.f32(1024).rearrange("p (b t) -> p b t", b=2)
        rstd = A.f32(1024).rearrange("p (b t) -> p b t", b=2)
        PT = A.bf16(4 * 512).rearrange("p (b t) -> p b t", b=4)
        rden = A.f32(1024).rearrange("p (b t) -> p b t", b=2)
        tmpz = A.f32(1024).rearrange("p (b t) -> p b t", b=2)
        th = A.bf16(1024).rearrange("p (b t) -> p b t", b=2)
        endA = A.off
        A.off = kv_off
        xbA = A.f32(3 * 1024).rearrange("p (b t) -> p b t", b=3)
        xnA = A.f32(2 * 1024).rearrange("p (b t) -> p b t", b=2)
        junkA = A.bf16(1024)
        assert A.off <= rest_off
        A.off = rest_off
        adast = A.f32(3 * 8 * 512).rearrange("p (b j n) -> p b j n", b=3, j=8)
        adarow = A.f32(3072)
        assert A.off <= endA
        A.off = kv_off
        e_all = A.f32(S)
        F_all = A.f32(S)
        assert A.off <= rest_off
        A.off = sc_off
        HMLa = A.bf16(3 * S).rearrange("p (r t) -> p r t", r=3)
        ones512 = A.f32(512)
        assert A.off <= endA
        A.off = kv_off
        obT = A.bf16(4 * S).rearrange("p (c t) -> p c t", c=4)
        assert A.off == rest_off
        A.off = sc_off
        gcs = A.bf16(1024).rearrange("p (b t) -> p b t", b=2)
        gcu = A.bf16(2 * 516).rearrange("p (b t) -> p b t", b=2)
        thz = A.bf16(1024).rearrange("p (b t) -> p b t", b=2)
        a1 = A.f32(1024).rearrange("p (b t) -> p b t", b=2)
        a2 = A.f32(1024).rearrange("p (b t) -> p b t", b=2)
        assert A.off <= endA
        A.off = base_off
        Wg = A.bf16(8 * 2048).rearrange("p (j n) -> p j n", j=8)
        WA = A.bf16(4 * 1024).rearrange("p (c n) -> p c n", c=4)
        WB = A.bf16(4 * 1024).rearrange("p (c n) -> p c n", c=4)
        WO = A.bf16(8 * 1024).rearrange("p (j n) -> p j n", j=8)
        assert A.off == kv_off
        A.off = rest_off
        hTt = A.bf16(2 * 8 * 512).rearrange("p (b j t) -> p b j t", b=2, j=8)
        mT = A.bf16(8 * 512).rearrange("p (j t) -> p j t", j=8)
        xbB = A.f32(2 * 1024).rearrange("p (b t) -> p b t", b=2)
        xnB = A.f32(2 * 1024).rearrange("p (b t) -> p b t", b=2)
        junkB = A.bf16(1024)
        thg = A.bf16(2 * 1024).rearrange("p (b t) -> p b t", b=2)
        t12 = A.f32(2 * 1024).rearrange("p (b t) -> p b t", b=2)
        gate_bc = A.f32(1024)
        Dj = A.f32(128)
        xr_off = A.off
        xr = A.f32(3 * 1024).rearrange("p (b t) -> p b t", b=3)
        assert A.off <= ARENA_W, A.off
        A.off = xr_off
        wstB = A.f32(3 * 1024).rearrange("p (b n) -> p b n", b=3)

        def dma(out, in_, reads=(), writes=()):
            return P.op("sp", lambda e: e.dma_start(out=out, in_=in_), reads=reads, writes=writes, dma=True)

        def barrier():
            mk = {
                "pe": lambda e: e.nop(),
                "act": lambda e: e.nop(),
                "dve": lambda e: e.nop(),
                "pool": lambda e: e.nop(),
                "sp": lambda e: e.nop(),
            }
            P.barrier(mk)

        P.op("pool", lambda e: e.memset(ident_f, 1.0), writes=["ident_f"])
        P.op("pool", lambda e: e.affine_select(out=ident_f, in_=ident_f, pattern=[[-1, 128]], base=0,
                                               channel_multiplier=1, compare_op=ALU.is_equal, fill=0.0),
             reads=["ident_f"], writes=["ident_f"])
        P.op("pool", lambda e: e.tensor_copy(out=ident_b, in_=ident_f), reads=["ident_f"], writes=["ident_b"])
        P.op("pool", lambda e: e.memset(ones_f, 1.0), writes=["ones_f"])
        P.op("pool", lambda e: e.memset(maskL, -30000.0), writes=["maskL"])
        P.op("pool", lambda e: e.affine_select(out=maskL, in_=maskL, pattern=[[1, 128]], base=0,
                                               channel_multiplier=-1, compare_op=ALU.is_gt, fill=0.0),
             reads=["maskL"], writes=["maskL"])
        P.op("pool", lambda e: e.memset(bdiag, 0.0), writes=["bdiag"])
        P.op("pool", lambda e: e.memset(bdiag[0:64, 0:64], 1.0), reads=["bdiag"], writes=["bdiag"])
        P.op("pool", lambda e: e.memset(bdiag[64:128, 64:128], 1.0), reads=["bdiag"], writes=["bdiag"])
        dma(cT_s, cT2[:, :], writes=["cT"])
        dma(bada_s, b_adaT[:, :], writes=["bada"])
        dma(ng_s, norm_gT[:, :], writes=["ng"])
        dma(gqk_s, gqk[:, :], writes=["gqk"])
        dma(cw_s, conv_wT[:, :], writes=["cw"])
        dma(bf_s, b_f[:, :], writes=["bf"])
        dma(wf_s, w_in_v[:, :, COL_F:COL_F + 8], writes=["wf_s"])
        P.op("dve", lambda e: e.tensor_copy(out=wf_b, in_=wf_s), reads=["wf_s"], writes=["wf_b"])
        P.op("dve", lambda e: e.tensor_scalar(out=gq8, in0=gqk_s[:, 0:1], scalar1=0.125, scalar2=None, op0=ALU.mult),
             reads=["gqk"], writes=["gq8"])
        P.op("dve", lambda e: e.tensor_scalar(out=nbf, in0=bf_s, scalar1=-1.0, scalar2=None, op0=ALU.mult),
             reads=["bf"], writes=["nbf"])
        for c in range(12):
            P.op("dve", lambda e, c=c: e.tensor_scalar(out=convdiag[:, c, :], in0=ident_f, scalar1=cw_s[:, c:c + 1],
                                                       scalar2=None, op0=ALU.mult),
                 reads=["ident_f", "cw"], writes=[("convdiag", c)])

        for m4 in range(6):
            b = m4 % 3
            dma(adast[:, b], w_ada_v[:, :, m4 * 512:(m4 + 1) * 512], writes=[("adast", b)])
            for j in range(8):
                P.op("pe", lambda e, m4=m4, j=j, b=b: e.matmul(ps[0:1, m4, :], lhsT=cT_s[:, 2 * j:2 * j + 1], rhs=adast[:, b, j, :],
                                                               start=(j == 0), stop=(j == 7)),
                     reads=[("adast", b), "cT"], writes=[("ps", m4)])
            P.op("dve", lambda e, m4=m4: e.tensor_copy(out=adarow[0:1, m4 * 512:(m4 + 1) * 512], in_=ps[0:1, m4, :]),
                 reads=[("ps", m4)], writes=[("adarow", m4)])
        for m in range(24):
            P.op("pe", lambda e, m=m: e.matmul(ps[:, 7, 2 * m:2 * m + 2], lhsT=adarow[0:1, m * 128:(m + 1) * 128], rhs=ones_f[0:1, 0:2],
                                               start=True, stop=True),
                 reads=[("adarow", m // 4), "ones_f"], writes=[("ps", 7)])
        P.op("dve", lambda e: e.tensor_tensor(out=ada, in0=ps[:, 7, 0:48].rearrange("p (m t) -> p m t", t=2)[:, :, 0],
                                              in1=bada_s, op=ALU.add),
             reads=[("ps", 7), "bada"], writes=["ada"])
        P.op("dve", lambda e: e.scalar_tensor_tensor(out=gp, in0=scale, scalar=1.0, in1=ng_s, op0=ALU.add, op1=ALU.mult),
             reads=["ada", "ng"], writes=["gp"])

        xn_cnt = [0]

        def xnorm_sub(s, xb, nxb, xn, junk, dest, dest_res, psb, part=3, kk=None):
            if part & 1:
                k = xn_cnt[0]
            else:
                k = kk
            if part == 1:
                xn_cnt[0] += 1
            return _xnorm_sub(s, xb, nxb, xn, junk, dest, dest_res, psb, part, k)

        def _xnorm_sub(s, xb, nxb, xn, junk, dest, dest_res, psb, part, k):
            if part == 3:
                xn_cnt[0] += 1
            bx, bn, m = k % nxb, k % 2, k % 4
            if part & 1:
                _xnorm_front(s, xb, xn, junk, bx, bn, m)
            if part & 2:
                _xnorm_back(xn, dest, dest_res, psb, bn)
            return k

        def _xnorm_front(s, xb, xn, junk, bx, bn, m, reuse=False):
            dma(xb[:, bx], x[s * 128:(s + 1) * 128, :], writes=[("xb", bx)])
            if reuse:
                P.op("act", lambda e: e.activation(out=xn[:, bn], in_=xb[:, bx], func=AF.Copy, scale=rs_all[:, s:s + 1]),
                     reads=[("xb", bx)], writes=[("xn", bn)])
                return
            P.op("act", lambda e: e.activation(out=junk, in_=xb[:, bx], func=AF.Square, accum_out=ss1[:, m:m + 1]),
                 reads=[("xb", bx)], writes=["junk", ("ss1", m)])
            P.op("act", lambda e: e.activation(out=lnv1[:, m:m + 1], in_=ss1[:, m:m + 1], func=AF.Ln, bias=EPS, scale=1.0 / D),
                 reads=[("ss1", m)], writes=[("lnv1", m)])
            P.op("act", lambda e: e.activation(out=rs_all[:, s:s + 1], in_=lnv1[:, m:m + 1], func=AF.Exp, scale=-0.5),
                 reads=[("lnv1", m)], writes=[("rs_all", s)])
            P.op("act", lambda e: e.activation(out=xn[:, bn], in_=xb[:, bx], func=AF.Copy, scale=rs_all[:, s:s + 1]),
                 reads=[("xb", bx), ("rs_all", s)], writes=[("xn", bn)])

        def _xnorm_back(xn, dest, dest_res, psb, bn):
            psx = ps[:, psb:psb + 2, :].rearrange("p b (jj t) -> p (b jj) t", t=128)
            for j in range(8):
                P.op("pe", lambda e, j=j: e.transpose(out=psx[:, j, :], in_=xn[:, bn, j * 128:(j + 1) * 128], identity=ident_f),
                     reads=[("xn", bn), "ident_f"], writes=[("ps", psb + j // 4)])
            for j in range(8):
                P.op("dve", lambda e, j=j: e.tensor_scalar(out=dest(j), in0=psx[:, j, :], scalar1=gp[:, j:j + 1],
                                                           scalar2=shift[:, j:j + 1], op0=ALU.mult, op1=ALU.add),
                     reads=[("ps", psb + j // 4), "gp", "ada"], writes=[dest_res], disjoint=True)

        def f_proj(i):
                pb = 5 + (i % 2)
                for j in range(8):
                    P.op("pe", lambda e, i=i, j=j, pb=pb: e.matmul(ps[0:8, pb, :], lhsT=wf_b[:, j, :], rhs=hT[:, j, i * 512:(i + 1) * 512],
                                                                   start=(j == 0), stop=(j == 7)),
                         reads=["wf_b", ("hT", i)], writes=[("ps", pb)])
                P.op("act", lambda e, i=i, pb=pb: e.activation(out=e_all[0:8, i * 512:(i + 1) * 512], in_=ps[0:8, pb, :], func=AF.Exp,
                                                               bias=nbf, scale=-1.0),
                     reads=[("ps", pb), "nbf"], writes=[("e_all", i)])

        for s in range(32):
            xnorm_sub(s, xbA, 3, xnA, junkA, lambda j, s=s: hT[:, j, s * 128:(s + 1) * 128], ("hT", s // 4),
                      2 * (s % 2))
        barrier()
        for i in range(NQT):
            f_proj(i)

        P.op("pool", lambda e: e.memset(ones512[0:8, :], 1.0), writes=["ones512"])
        P.op("act", lambda e: e.activation(out=e_all[0:8, :], in_=e_all[0:8, :], func=AF.Ln, bias=1.0, scale=1.0),
             reads=[("e_all", i) for i in range(NQT)], writes=["sp_all"])
        for i in range(NQT):
            init = 0.0 if i == 0 else F_all[0:8, i * 512 - 1:i * 512]
            P.op("dve", lambda e, i=i, init=init: e.tensor_tensor_scan(out=F_all[0:8, i * 512:(i + 1) * 512], data0=ones512[0:8, :],
                                                                       data1=e_all[0:8, i * 512:(i + 1) * 512], initial=init,
                                                                       op0=ALU.mult, op1=ALU.subtract),
                 reads=["sp_all", "ones512", "F_all"], writes=["F_all"])
        P.op("dve", lambda e: e.tensor_copy(out=HMLa[0:8, 0], in_=F_all[0:8, :]), reads=["F_all"], writes=["H0"])
        P.op("dve", lambda e: e.tensor_tensor(out=e_all[0:8, :], in0=F_all[0:8, :], in1=HMLa[0:8, 0], op=ALU.subtract),
             reads=["F_all", "H0", "sp_all"], writes=["R1a"])
        P.op("dve", lambda e: e.tensor_copy(out=HMLa[0:8, 1], in_=e_all[0:8, :]), reads=["R1a"], writes=["H1"])
        P.op("dve", lambda e: e.tensor_tensor(out=F_all[0:8, :], in0=e_all[0:8, :], in1=HMLa[0:8, 1], op=ALU.subtract),
             reads=["R1a", "H1", "F_all"], writes=["R2a"])
        P.op("dve", lambda e: e.tensor_copy(out=HMLa[0:8, 2], in_=F_all[0:8, :]), reads=["R2a"], writes=["H2"])
        for r in range(3):
            dma(Frows[8 * r:8 * r + 8, :], HMLa[0:8, r], reads=["H%d" % r], writes=[("Frows", i) for i in range(NQT)])
        if "Frows" in debug:
            dbg["Frows"] = dram("dbg_Frows", [24, S], kind="ExternalOutput", dtype=BF16)
            dma(dbg["Frows"][:, :], Frows[0:24, :], reads=[("Frows", i) for i in range(NQT)])
        if "hT" in debug:
            dbg["hT"] = dram("dbg_hT", [128, 8, S], kind="ExternalOutput", dtype=BF16)
            dma(dbg["hT"][:, :, :], hT, reads=[("hT", i) for i in range(NQT)])
        barrier()

        P.op("pool", lambda e: e.memset(KA[64:70, :, :], 1.0), writes=["KAaug"])
        P.op("pool", lambda e: e.memset(QA[64:70, :, :, :], -1.0), writes=["QAaug"])
        P.op("pool", lambda e: e.memset(VA[:, :, 0, 64:128], 1.0), writes=["VAones"])
        P.op("pool", lambda e: e.memset(VA[:, :, 1, 0:64], 1.0), writes=["VAones"])
        from collections import deque

        def load_pair_weights(cols, wb, ce="pool"):
            for pi, c0 in enumerate(cols):
                b = load_pair_weights.n % 2
                load_pair_weights.n += 1
                dma(wst[:, b], w_in_v[:, :, c0:c0 + 128], writes=[("wst", b)])
                if ce == "act":
                    P.op("act", lambda e, b=b, pi=pi: e.activation(out=wbf[:, wb, :, pi * 128:(pi + 1) * 128], in_=wst[:, b], func=AF.Copy),
                         reads=[("wst", b)], writes=[("wbf", wb, pi)])
                else:
                    P.op("pool", lambda e, b=b, pi=pi: e.tensor_copy(out=wbf[:, wb, :, pi * 128:(pi + 1) * 128], in_=wst[:, b]),
                         reads=[("wst", b)], writes=[("wbf", wb, pi)])
        load_pair_weights.n = 0

        def proj_fm(wb, pi, i, pb, js=range(8)):
            for j in js:
                P.op("pe", lambda e, j=j: e.matmul(ps[:, pb, :], lhsT=wbf[:, wb, j, pi * 128:(pi + 1) * 128],
                                                   rhs=hT[:, j, i * 512:(i + 1) * 512], start=(j == 0), stop=(j == 7)),
                     reads=[("wbf", wb, pi), ("hT", i)], writes=[("ps", pb)])

        def chain_tasks(wb, pi, i, pb, b, gcol, dest0, dest1, res0, res1):
            t = []
            t.append(lambda: proj_fm(wb, pi, i, pb, range(0, 4)))
            t.append(lambda: proj_fm(wb, pi, i, pb, range(4, 8)))

            def t_copy():
                P.op("dve", lambda e: e.tensor_copy(out=qraw[:, b], in_=ps[:, pb, :]), reads=[("ps", pb)], writes=[("qraw", b)])
                P.op("pool", lambda e: e.tensor_tensor(out=sq[:, b], in0=qraw[:, b], in1=qraw[:, b], op=ALU.mult),
                     reads=[("qraw", b)], writes=[("sq", b)])
            t.append(t_copy)

            def t_ss():
                P.op("pe", lambda e: e.matmul(ps[:, pb, :], lhsT=bdiag, rhs=sq[:, b], start=True, stop=True),
                     reads=[("sq", b), "bdiag"], writes=[("ps", pb)])

            def t_act():
                P.op("act", lambda e: e.activation(out=lnv[:, b], in_=ps[:, pb, :], func=AF.Ln, bias=EPS, scale=1.0 / 64),
                     reads=[("ps", pb)], writes=[("lnv", b)])
                P.op("act", lambda e: e.activation(out=rstd[:, b], in_=lnv[:, b], func=AF.Exp, scale=-0.5),
                     reads=[("lnv", b)], writes=[("rstd", b)])

            def t_out():
                P.op("dve", lambda e: e.scalar_tensor_tensor(out=dest0, in0=qraw[0:64, b], scalar=gcol[0:64], in1=rstd[0:64, b],
                                                             op0=ALU.mult, op1=ALU.mult),
                     reads=[("qraw", b), ("rstd", b), "gq8", "gqk"], writes=[res0])
                P.op("dve", lambda e: e.scalar_tensor_tensor(out=dest1, in0=qraw[64:128, b], scalar=gcol[64:128], in1=rstd[64:128, b],
                                                             op0=ALU.mult, op1=ALU.mult),
                     reads=[("qraw", b), ("rstd", b), "gq8", "gqk"], writes=[res1])
            return t, t_ss, t_act, t_out

        def prep_tasks(g, wb, i, late_writes=False):
            qb = i % 2
            tasks = []

            def t_dma():
                for hh in range(2):
                    for r in range(3):
                        dma(QA[64 + r:65 + r, qb, hh, :], Frows[8 * r + 2 * g + hh:8 * r + 2 * g + hh + 1, i * 512:(i + 1) * 512],
                            reads=[("Frows", i), "QAaug"], writes=[("QAF", qb, hh)])
            tasks.append(t_dma)
            kt, k_ss, k_act, k_out = chain_tasks(wb, 1, i, 6, 0, gk, KA[0:64, 0, i * 512:(i + 1) * 512], KA[0:64, 1, i * 512:(i + 1) * 512],
                                                 ("KA", 0, i), ("KA", 1, i))
            qt, q_ss, q_act, q_out = chain_tasks(wb, 0, i, 7, 1, gq8, QA[0:64, qb, 0, :], QA[0:64, qb, 1, :], ("QA", qb, 0), ("QA", qb, 1))
            tasks += kt
            tasks += qt
            for ssb in range(4):
                def t_v(ssb=ssb):
                    s = 4 * i + ssb
                    for j in range(8):
                        P.op("pe", lambda e, j=j: e.matmul(ps[:, 6, ssb * 128:(ssb + 1) * 128], lhsT=hT[:, j, s * 128:(s + 1) * 128],
                                                           rhs=wbf[:, wb, j, 256:384], start=(j == 0), stop=(j == 7)),
                             reads=[("wbf", wb, 2), ("hT", i)], writes=[("ps", 6)])
                t_v.is_v = True
                tasks.append(t_v)

            def t_vcopy():
                P.op("dve", lambda e: e.tensor_copy(out=VA[:, 4 * i:4 * i + 4, 0, 0:64],
                                                    in_=ps[:, 6, :].rearrange("p (s h f) -> p s h f", s=4, h=2)[:, :, 0, :]),
                     reads=[("ps", 6)], writes=[("VA", i)], disjoint=True)
                P.op("dve", lambda e: e.tensor_copy(out=VA[:, 4 * i:4 * i + 4, 1, 64:128],
                                                    in_=ps[:, 6, :].rearrange("p (s h f) -> p s h f", s=4, h=2)[:, :, 1, :]),
                     reads=[("ps", 6)], writes=[("VA", i)], disjoint=True)
            t_vcopy.is_v = True
            tasks.append(t_vcopy)
            tasks.append(k_ss)
            tasks.append(q_ss)
            tasks.append(k_act)
            tasks.append(q_act)
            tasks.append(q_out)
            tasks.append(k_out)
            if late_writes:
                vt = [t for t in tasks if getattr(t, "is_v", False)]
                for t in vt:
                    tasks.remove(t)
                tasks += vt
            return tasks

        for g in range(n_pairs):
            wb = g % 2
            if g == 0:
                load_pair_weights([COL_Q + g * 128, COL_K + g * 128, COL_V + g * 128], wb)
            for hh in range(2):
                for r in range(3):
                    dma(KA[67 + r:68 + r, hh, :], Frows[8 * r + 2 * g + hh:8 * r + 2 * g + hh + 1, :],
                        reads=[("Frows", i) for i in range(NQT)], writes=[("KAF", hh)])

            units = []
            for i in range(NQT):
                for hh in range(2):
                    for kb in range(4 * i + 4):
                        units.append((i, hh, kb))

            def qk(n, g=g):
                i, hh, kb = units[n]
                sb = n % 4
                qb = i % 2
                jd = kb - 4 * i
                c0 = max(jd, 0) * 128
                P.op("pe", lambda e: e.matmul(ps[:, sb, c0:512], lhsT=KA[0:70, hh, kb * 128:(kb + 1) * 128],
                                              rhs=QA[0:70, qb, hh, c0:512], start=True, stop=(jd < 0)),
                     reads=[("KA", hh, kb // 4), ("KAF", hh), "KAaug", ("QA", qb, hh), ("QAF", qb, hh), "QAaug"],
                     writes=[("ps", sb)])
                if jd >= 0:
                    P.op("pe", lambda e: e.matmul(ps[:, sb, c0:c0 + 128], lhsT=maskL, rhs=ident_b, start=False, stop=True),
                         reads=["maskL", "ident_b"], writes=[("ps", sb)])

            def exp_pv(n, g=g):
                i, hh, kb = units[n]
                sb, pbuf = n % 4, n % 4
                ob = 4 + ((2 * i + hh) % 2)
                jd = kb - 4 * i
                c0 = max(jd, 0) * 128
                last = (kb == 4 * i + 3)
                P.op("act", lambda e: e.activation(out=PT[:, pbuf, c0:512], in_=ps[:, sb, c0:512], func=AF.Exp),
                     reads=[("ps", sb)], writes=[("PT", pbuf)])
                P.op("pe", lambda e: e.matmul(ps[:, ob, c0:512], lhsT=VA[:, kb, hh, :], rhs=PT[:, pbuf, c0:512],
                                              start=(kb == 0), stop=last),
                     reads=[("VA", kb // 4), "VAones", ("PT", pbuf)], writes=[("ps", ob)])
                if last:
                    rb = (2 * i + hh) % 2
                    qb = i % 2
                    lo, hi = (0, 64) if hh == 0 else (64, 128)
                    dlo, dhi = (64, 128) if hh == 0 else (0, 64)
                    P.op("dve", lambda e: e.reciprocal(out=rden[lo:hi, rb], in_=ps[dlo:dhi, ob, :]),
                         reads=[("ps", ob)], writes=[("rden", rb)])
                    P.op("dve", lambda e: e.tensor_tensor(out=oaT[lo:hi, g, i * 512:(i + 1) * 512], in0=ps[lo:hi, ob, :],
                                                          in1=rden[lo:hi, rb], op=ALU.mult),
                         reads=[("ps", ob), ("rden", rb)], writes=[("oaT", g, i, hh)])

            side = deque()
            if g == 0:
                for t in prep_tasks(g, wb, 0):
                    t()
            first_unit_of_tile = {}
            for n, (i, hh, kb) in enumerate(units):
                if hh == 0 and kb == 0:
                    first_unit_of_tile[i] = n
            qk(0)
            qk(1)
            qk(2)
            for n in range(len(units)):
                i, hh, kb = units[n]
                if hh == 0 and kb == 0:
                    if i + 1 < NQT:
                        side.extend(prep_tasks(g, wb, i + 1))
                    elif g + 1 < n_pairs:
                        g1 = g + 1
                        side.append(lambda g1=g1: load_pair_weights([COL_Q + g1 * 128, COL_K + g1 * 128, COL_V + g1 * 128], g1 % 2))
                        side.extend(prep_tasks(g1, g1 % 2, 0, late_writes=True))
                    elif do_conv:
                        side.append(lambda: load_pair_weights([COL_GB, COL_GC, COL_U, COL_ZB], (n_pairs - 1 + 1) % 2))
                if n + 3 < len(units):
                    i2 = units[n + 3][0]
                    if i2 != i and first_unit_of_tile.get(i2) == n + 3:
                        while side:
                            side.popleft()()
                    qk(n + 3)
                exp_pv(n)
                ntile = 2 * (4 * i + 4)
                per = max(1, -(-28 // max(ntile - 4, 1)))
                for _ in range(per):
                    if side:
                        side.popleft()()
            while side:
                side.popleft()()

        if "oaT" in debug:
            dbg["oaT"] = dram("dbg_oaT", [128, 4, S], kind="ExternalOutput", dtype=BF16)
            dma(dbg["oaT"][:, :, :], oaT, reads=[("oaT", g, i, hh) for g in range(n_pairs) for i in range(NQT) for hh in range(2)])
        if "KA" in debug:
            dbg["KA"] = dram("dbg_KA", [70, 2, S], kind="ExternalOutput", dtype=BF16)
            dma(dbg["KA"][:, :, :], KA[0:70], reads=[("KA", hh, i) for hh in range(2) for i in range(NQT)] + [("KAF", 0), ("KAF", 1), "KAaug"])
        barrier()

        if do_conv:
            sa = n_pairs % 2
            sb = 1 - sa
            if n_pairs == 0:
                load_pair_weights([COL_GB, COL_GC, COL_U, COL_ZB], sa)

            def conv_w(c, slot):
                load_pair_weights([COL_GB + c * 128, COL_GC + c * 128, COL_U + c * 128, COL_ZB + c * 128], slot, ce="act")

            def zgate(wbz):
                nz = 0
                for gg in range(n_pairs):
                    for i in range(NQT):
                        b = nz % 2
                        pbz = 5 + (nz % 2)
                        nz += 1
                        proj_fm(wbz, gg, i, pbz)
                        P.op("act", lambda e, b=b, pbz=pbz: e.activation(out=thz[:, b], in_=ps[:, pbz, :], func=AF.Tanh, scale=0.5),
                             reads=[("ps", pbz)], writes=[("thz", b)])
                        P.op("dve", lambda e, b=b, pbz=pbz: e.scalar_tensor_tensor(out=gcs[:, b], in0=thz[:, b], scalar=1.0, in1=ps[:, pbz, :],
                                                                                  op0=ALU.add, op1=ALU.mult),
                             reads=[("thz", b), ("ps", pbz)], writes=[("gcs", b)])
                        P.op("pool", lambda e, b=b, gg=gg, i=i: e.tensor_tensor(out=oaT[:, gg, i * 512:(i + 1) * 512], in0=oaT[:, gg, i * 512:(i + 1) * 512],
                                                                              in1=gcs[:, b], op=ALU.mult),
                             reads=[("gcs", b)], writes=[("oaTz", gg, i)])

            def conv_chunk(c, wb):
                for i in range(NQT):
                    par = i % 2
                    b = i % 2
                    proj_fm(wb, 1, i, 0)
                    proj_fm(wb, 2, i, 1)
                    P.op("act", lambda e, b=b: e.activation(out=gcs[:, b], in_=ps[:, 0, :], func=AF.Copy),
                         reads=[("ps", 0)], writes=[("gcs", b)])
                    if i == 0:
                        P.op("pool", lambda e, par=par: e.memset(gcu[:, par, 0:2], 0.0), writes=[("gcu", par)])
                    else:
                        P.op("pool", lambda e, par=par: e.tensor_copy(out=gcu[:, par, 0:2], in_=gcu[:, 1 - par, 512:514]),
                             reads=[("gcu", 1 - par)], writes=[("gcu", par)])
                    P.op("dve", lambda e, b=b, par=par: e.tensor_tensor(out=gcu[:, par, 2:514], in0=ps[:, 1, :], in1=gcs[:, b], op=ALU.mult),
                         reads=[("ps", 1), ("gcs", b), ("gcu", par)], writes=[("gcu", par)])
                    proj_fm(wb, 0, i, 2)
                    proj_fm(wb, 3, i, 3)
                    for k in range(3):
                        P.op("pe", lambda e, k=k, par=par, c=c: e.matmul(ps[:, 4, :], lhsT=convdiag[:, c * 3 + k, :], rhs=gcu[:, par, k:k + 512],
                                                                         start=(k == 0), stop=(k == 2)),
                             reads=[("convdiag", c * 3 + k), ("gcu", par)], writes=[("ps", 4)])
                    P.op("act", lambda e, b=b: e.activation(out=thz[:, b], in_=ps[:, 3, :], func=AF.Tanh, scale=0.5),
                         reads=[("ps", 3)], writes=[("thz", b)])
                    P.op("dve", lambda e, b=b: e.scalar_tensor_tensor(out=a1[:, b], in0=thz[:, b], scalar=1.0, in1=ps[:, 3, :],
                                                                     op0=ALU.add, op1=ALU.mult),
                         reads=[("thz", b), ("ps", 3)], writes=[("a1", b)])
                    P.op("dve", lambda e, b=b: e.tensor_tensor(out=a2[:, b], in0=ps[:, 2, :], in1=a1[:, b], op=ALU.mult),
                         reads=[("ps", 2), ("a1", b)], writes=[("a2", b)])
                    P.op("dve", lambda e, b=b, c=c, i=i: e.tensor_tensor(out=obT[:, c, i * 512:(i + 1) * 512], in0=ps[:, 4, :], in1=a2[:, b], op=ALU.mult),
                         reads=[("ps", 4), ("a2", b)], writes=[("obT", c, i)])

            load_pair_weights([COL_ZA + gg * 128 for gg in range(4)], sb, ce="act")
            conv_chunk(0, sa)
            conv_w(1, sa)
            zgate(sb)
            conv_w(2, sb)
            conv_chunk(1, sa)
            conv_w(3, sa)
            conv_chunk(2, sb)
            conv_chunk(3, sa)
            if "obT" in debug:
                dbg["obT"] = dram("dbg_obT", [128, 4, S], kind="ExternalOutput", dtype=BF16)
                dma(dbg["obT"][:, :, :], obT, reads=[("obT", c, i) for c in range(4) for i in range(NQT)])
        barrier()

        out_dmas = []
        if do_b:
            for j in range(8):
                P.op("dve", lambda e, j=j: e.tensor_scalar(out=Dj, in0=ident_f, scalar1=gate[:, j:j + 1], scalar2=0.25,
                                                           op0=ALU.mult, op1=ALU.mult),
                     reads=["ident_f", "ada"], writes=["Dj"])
                P.op("pe", lambda e, j=j: e.matmul(ps[:, 2 + j // 4, (j % 4) * 128:(j % 4 + 1) * 128], lhsT=ones_f, rhs=Dj, start=True, stop=True),
                     reads=["ones_f", "Dj"], writes=[("ps", 2 + j // 4)])
            for hb in range(2):
                P.op("dve", lambda e, hb=hb: e.tensor_copy(out=gate_bc[:, hb * 512:(hb + 1) * 512], in_=ps[:, 2 + hb, :]),
                     reads=[("ps", 2 + hb)], writes=["gate_bc"])

            def xn_tile(it):
                hb = it % 2
                for sub in range(4):
                    s_ = it * 4 + sub
                    kk0 = xn_cnt[0]
                    xn_cnt[0] += 1
                    _xnorm_front(s_, xbB, xnB, junkB, kk0 % 2, kk0 % 2, kk0 % 4, reuse=True)
                    _xnorm_back(xnB, lambda j, sub=sub: hTt[:, hb, j, sub * 128:(sub + 1) * 128], ("hTt", hb), 0, kk0 % 2)

            xn_tile(0)
            nld = [0]
            cast_engs = ["dve", "act"]

            def load_w(src_ap, shape3, dst, wres, mul_gate=None):
                b = nld[0] % 3
                ce = cast_engs[nld[0] % 2]
                nld[0] += 1
                view = wstB[:, b].rearrange("p (j n) -> p j n", j=shape3)
                dma(view, src_ap, writes=[("wstB", b)])
                if mul_gate is not None:
                    P.op("pool" if ce == "act" else ce, lambda e: e.tensor_tensor(out=dst, in0=view, in1=mul_gate, op=ALU.mult),
                         reads=[("wstB", b), "gate_bc"], writes=[wres])
                elif ce == "act":
                    P.op("act", lambda e: e.activation(out=dst, in_=view, func=AF.Copy), reads=[("wstB", b)], writes=[wres])
                else:
                    P.op(ce, lambda e: e.tensor_copy(out=dst, in_=view), reads=[("wstB", b)], writes=[wres])

            w_a_v = w_a.rearrange("(c p) n -> p c n", p=128)
            w_b_v = w_b.rearrange("(c p) n -> p c n", p=128)
            w_o_v = w_o.rearrange("(j p) n -> p j n", p=128)
            for fc in range(8):
                for cch in (fc, 8 + fc):
                    c0 = COL_GA + cch * 128
                    load_w(w_in_v[:, :, c0:c0 + 128], 8, Wg[:, :, cch * 128:(cch + 1) * 128], ("Wg", cch))
                if fc % 2 == 0:
                    q4 = fc // 2
                    load_w(w_a_v[:, :, q4 * 256:(q4 + 1) * 256], 4, WA[:, :, q4 * 256:(q4 + 1) * 256], ("WA", q4))
                    load_w(w_b_v[:, :, q4 * 256:(q4 + 1) * 256], 4, WB[:, :, q4 * 256:(q4 + 1) * 256], ("WB", q4))
            for q8 in range(8):
                load_w(w_o_v[:, :, q8 * 128:(q8 + 1) * 128], 8, WO[:, :, q8 * 128:(q8 + 1) * 128], ("WO", q8),
                       mul_gate=gate_bc[:, q8 * 128:(q8 + 1) * 128].unsqueeze(1).to_broadcast([128, 8, 128]))

            xk = {}
            for it in range(NQT):
                hb = it % 2
                t0 = it * 512
                def gates_mm(hb_, fc_):
                    g0_ = 2 if fc_ % 2 == 0 else 4
                    for half in range(2):
                        for j in range(8):
                            P.op("pe", lambda e, j=j, half=half: e.matmul(
                                ps[:, g0_ + half, :], lhsT=Wg[:, j, half * 1024 + fc_ * 128:half * 1024 + (fc_ + 1) * 128],
                                rhs=hTt[:, hb_, j, :], start=(j == 0), stop=(j == 7)),
                                reads=[("hTt", hb_), ("Wg", half * 8 + fc_)], writes=[("ps", g0_ + half)])

                for fc in range(8):
                    g0 = 2 if fc % 2 == 0 else 4
                    tb_ = fc % 2
                    if not (fc == 0 and it > 0):
                        gates_mm(hb, fc)
                    for c in range(4):
                        P.op("pe", lambda e, c=c: e.matmul(ps[:, 6, :], lhsT=WA[:, c, fc * 128:(fc + 1) * 128],
                                                           rhs=oaT[:, c, t0:t0 + 512], start=(c == 0), stop=(c == 3)),
                             reads=[("WA", fc // 2)], writes=[("ps", 6)])
                    for c in range(4):
                        P.op("pe", lambda e, c=c: e.matmul(ps[:, 7, :], lhsT=WB[:, c, fc * 128:(fc + 1) * 128],
                                                           rhs=obT[:, c, t0:t0 + 512], start=(c == 0), stop=(c == 3)),
                             reads=[("WB", fc // 2)], writes=[("ps", 7)])
                    P.op("act", lambda e: e.activation(out=thg[:, tb_].rearrange("p (a t) -> p a t", a=2), in_=ps[:, g0:g0 + 2, :],
                                                       func=AF.Tanh, scale=0.5),
                         reads=[("ps", g0), ("ps", g0 + 1)], writes=[("thg", tb_)])
                    P.op("dve", lambda e: e.scalar_tensor_tensor(out=t12[:, tb_].rearrange("p (a t) -> p a t", a=2),
                                                                 in0=thg[:, tb_].rearrange("p (a t) -> p a t", a=2), scalar=1.0,
                                                                 in1=ps[:, 6:8, :], op0=ALU.add, op1=ALU.mult),
                         reads=[("thg", tb_), ("ps", 6), ("ps", 7)], writes=[("t12", tb_)])
                    P.op("pool", lambda e: e.tensor_tensor(out=mT[:, fc, :], in0=t12[:, tb_, 0:512], in1=t12[:, tb_, 512:1024], op=ALU.add),
                         reads=[("t12", tb_)], writes=[("mT", fc)])
                    if it + 1 < NQT:
                        hb1 = (it + 1) % 2
                        back_at = {2: [0], 3: [1], 4: [2], 5: [3]}
                        front_at = {0: 0, 1: 1, 2: 2, 3: 3}
                        for sub in back_at.get(fc, []):
                            _xnorm_back(xnB, lambda j, sub=sub: hTt[:, hb1, j, sub * 128:(sub + 1) * 128], ("hTt", hb1), 0, xk[sub] % 2)
                        if fc in front_at:
                            sub = front_at[fc]
                            kk_ = xn_cnt[0]
                            xn_cnt[0] += 1
                            xk[sub] = kk_
                            _xnorm_front((it + 1) * 4 + sub, xbB, xnB, junkB, kk_ % 2, kk_ % 2, kk_ % 4, reuse=True)
                if it == 0 and "mT0" in debug:
                    dbg["mT0"] = dram("dbg_mT0", [128, 8, 512], kind="ExternalOutput", dtype=BF16)
                    dma(dbg["mT0"], mT, reads=[("mT", fc) for fc in range(8)])
                if it + 1 < NQT:
                    gates_mm((it + 1) % 2, 0)
                for sub in range(4):
                    s_ = it * 4 + sub
                    bx = s_ % 3
                    dma(xr[:, bx], x[s_ * 128:(s_ + 1) * 128, :], writes=[("xr", bx)] + ([("wstB", 0), ("wstB", 1), ("wstB", 2)] if s_ < 3 else []))
                    for half in range(2):
                        po = (2 * sub + half) % 2
                        for fc in range(8):
                            P.op("pe", lambda e, fc=fc: e.matmul(ps[:, po, :], lhsT=mT[:, fc, sub * 128:(sub + 1) * 128],
                                                                 rhs=WO[:, fc, half * 512:(half + 1) * 512], start=(fc == 0), stop=(fc == 7)),
                                 reads=[("mT", fc)] + [("WO", q8) for q8 in range(4 * half, 4 * half + 4)], writes=[("ps", po)])
                        P.op("dve", lambda e: e.tensor_tensor(out=xr[:, bx, half * 512:(half + 1) * 512], in0=ps[:, po, :],
                                                              in1=xr[:, bx, half * 512:(half + 1) * 512], op=ALU.add),
                             reads=[("ps", po), ("xr", bx)], writes=[("xr", bx)])
                    out_dmas.append(dma(y[s_ * 128:(s_ + 1) * 128, :], xr[:, bx], reads=[("xr", bx)]))
        P.emit(nc, final_wait_ops=[o for o in P.ops if o.is_dma])
    return nc, dbg


def host_inputs(inputs, b):
    f = lambda a: np.ascontiguousarray(a, dtype=np.float32)
    c = inputs["c"][b]
    cT = c.reshape(8, 128).T
    gq = np.tile(np.asarray(inputs["q_norm_g"][0]), 2)
    gk = np.tile(np.asarray(inputs["k_norm_g"][0]), 2)
    return {
        "x": f(inputs["x"][b]),
        "cT2": f(np.repeat(cT, 2, axis=1)),
        "w_ada": f(inputs["w_ada"][0]),
        "b_adaT": f(np.asarray(inputs["b_ada"][0]).reshape(24, 128).T),
        "norm_gT": f(np.asarray(inputs["norm_g"][0]).reshape(8, 128).T),
        "w_in": f(inputs["w_in"][0]),
        "b_f": f(np.asarray(inputs["b_f"][0]).reshape(8, 1)),
        "gqk": f(np.stack([gq, gk], axis=1)),
        "conv_wT": f(np.asarray(inputs["conv_w"][0]).reshape(3, 4, 128).transpose(2, 1, 0).reshape(128, 12)),
        "w_a": f(inputs["w_attn_out"][0]),
        "w_b": f(inputs["w_conv_out"][0]),
        "w_o": f(inputs["w_o"][0]),
    }


def kernel(**inputs):
    inputs = {k: np.asarray(v) for k, v in inputs.items()}
    nc, _ = build()
    in_maps = [host_inputs(inputs, b) for b in range(8)]
    res = run_bass_kernel_spmd(nc, in_maps, core_ids=list(range(8)))
    return np.stack([np.asarray(r["y"], dtype=np.float32) for r in res.results], axis=0)
```

```python
import contextlib
import numpy as np
import concourse.bass as bass
import concourse.mybir as mybir
from concourse.bass_utils import run_bass_kernel_spmd

F32 = mybir.dt.float32
BF16 = mybir.dt.bfloat16
AF = mybir.ActivationFunctionType
ALU = mybir.AluOpType

S, D = 4096, 1024
NQT = 8
EPS = 1e-6
COL_Q, COL_K, COL_V, COL_F, COL_ZA = 0, 512, 1024, 1536, 1544
COL_GB, COL_GC, COL_U, COL_ZB, COL_GA, COL_GG = 2056, 2568, 3080, 3592, 4104, 5128
NCOL = 6152
ARENA_W = 53000
TB = 256

ENGS = ("pe", "act", "dve", "pool", "sp")
N_DMA_SEMS = 24


class Op:
    __slots__ = ("eng", "fn", "waits", "signal", "cnt", "dma_slot", "dma_val", "is_dma", "idx")


class _Rec:
    def __getattr__(self, name):
        def f(*a, **k):
            self.call = (name, a, k)
            return self
        return f


class Prog:
    def __init__(self):
        self.ops = []
        self.last_w = {}
        self.readers = {}
        self.n_dma = 0
        self.last_dma_on_slot = {}

    def op(self, eng, fn, reads=(), writes=(), dma=False, disjoint=False):
        o = Op()
        rec = _Rec()
        fn(rec)
        call = rec.call
        fn = lambda e, call=call: getattr(e, call[0])(*call[1], **call[2])
        o.eng, o.fn, o.signal, o.cnt, o.is_dma = eng, fn, False, None, dma
        o.idx = len(self.ops)
        deps = set()
        for r in reads:
            w = self.last_w.get(r)
            if w is not None:
                deps.add(w)
        def same_eng(d):
            q = self.ops[d]
            return q.eng == eng and not q.is_dma and not dma
        for r in writes:
            w = self.last_w.get(r)
            if w is not None and not (disjoint and same_eng(w)):
                deps.add(w)
            for rd in self.readers.get(r, {}).values():
                deps.add(rd)
        o.waits = []
        for d in deps:
            p = self.ops[d]
            if p.eng == "pe" and eng == "pe" and not p.is_dma and not dma:
                continue
            p.signal = True
            o.waits.append(d)
        if dma:
            o.dma_slot = self.n_dma % N_DMA_SEMS
            o.dma_val = 16 * (self.n_dma // N_DMA_SEMS + 1)
            self.n_dma += 1
            self.last_dma_on_slot[o.dma_slot] = o.idx
        key = eng if not dma else ("dma", o.idx)
        for r in reads:
            self.readers.setdefault(r, {})[key] = o.idx
        for r in writes:
            self.last_w[r] = o.idx
            self.readers[r] = {}
        self.ops.append(o)
        return o

    def barrier(self, mk):
        last = {}
        for o in self.ops:
            if not o.is_dma:
                last[o.eng] = o.idx
        dmas = list(self.last_dma_on_slot.values())
        bops = []
        for e in ENGS:
            o = self.op(e, mk[e])
            for d in list(last.values()) + dmas:
                self.ops[d].signal = True
                o.waits.append(d)
            bops.append(o)
        self.last_w = {}
        self.readers = {}

    def emit(self, nc, final_wait_ops=()):
        engs = {"pe": nc.tensor, "act": nc.scalar, "dve": nc.vector, "pool": nc.gpsimd, "sp": nc.sync}
        cnt = {e: 0 for e in ENGS}
        for o in self.ops:
            if o.is_dma:
                continue
            if o.signal:
                cnt[o.eng] += 1
                o.cnt = cnt[o.eng]
        with contextlib.ExitStack() as st:
            sems = {e: st.enter_context(nc.semaphore("s_" + e)) for e in ENGS}
            dsems = [st.enter_context(nc.semaphore("d_%d" % i)) for i in range(N_DMA_SEMS)]
            block = st.enter_context(nc.Block())
            ops = self.ops

            def body(ename):
                eng = engs[ename]
                waited = {}

                def need(key, sem, val, pend):
                    if waited.get(key, 0) >= val:
                        return
                    waited[key] = val
                    pend.append((sem, val))

                for o in ops:
                    if o.eng != ename:
                        continue
                    pend = []
                    for d in o.waits:
                        p = ops[d]
                        if p.is_dma:
                            need(("d", p.dma_slot), dsems[p.dma_slot], p.dma_val, pend)
                        else:
                            need(("e", p.eng), sems[p.eng], p.cnt, pend)
                    if o.is_dma and o.dma_val > 16:
                        need(("d", o.dma_slot), dsems[o.dma_slot], o.dma_val - 16, pend)
                    for sem, val in pend:
                        eng.wait_ge(sem, val)
                    inst = o.fn(eng)
                    if o.is_dma:
                        inst.then_inc(dsems[o.dma_slot], 16)
                    elif o.signal:
                        inst.then_inc(sems[o.eng], 1)
                if ename == "sp":
                    for o in final_wait_ops:
                        if waited.get(("d", o.dma_slot), 0) < o.dma_val:
                            eng.wait_ge(dsems[o.dma_slot], o.dma_val)
                            waited[("d", o.dma_slot)] = o.dma_val

            @block.tensor
            def _(e):
                body("pe")

            @block.scalar
            def _(e):
                body("act")

            @block.vector
            def _(e):
                body("dve")

            @block.gpsimd
            def _(e):
                body("pool")

            @block.sync
            def _(e):
                body("sp")


class Arena:
    def __init__(self, ar, total):
        self.ar, self.total, self.off = ar, total, 0

    def f32(self, n):
        a = self.ar[:, self.off:self.off + n]
        self.off += n
        assert self.off <= self.total, (self.off, self.total)
        return a

    def bf16(self, n_el):
        w = (n_el + 1) // 2
        a = self.ar[:, self.off:self.off + w].bitcast(BF16)
        self.off += w
        assert self.off <= self.total, (self.off, self.total)
        return a


def build(debug=(), n_pairs=4, do_conv=True, do_b=True):
    nc = bass.Bass("TRN2", target_bir_lowering=False)

    def dram(name, shape, kind="ExternalInput", dtype=F32):
        return nc.dram_tensor(name, shape, dtype, kind=kind).ap()

    x = dram("x", [S, D])
    cT2 = dram("cT2", [128, 16])
    w_ada = dram("w_ada", [D, 3 * D])
    b_adaT = dram("b_adaT", [128, 24])
    norm_gT = dram("norm_gT", [128, 8])
    w_in = dram("w_in", [D, NCOL])
    b_f = dram("b_f", [8, 1])
    gqk = dram("gqk", [128, 2])
    conv_wT = dram("conv_wT", [128, 12])
    w_a = dram("w_a", [512, D])
    w_b = dram("w_b", [512, D])
    w_o = dram("w_o", [D, D])
    y = dram("y", [S, D], kind="ExternalOutput")
    dbg = {}
    w_in_v = w_in.rearrange("(j p) n -> p j n", p=128)
    w_ada_v = w_ada.rearrange("(j p) n -> p j n", p=128)

    P = Prog()
    st = contextlib.ExitStack()
    with st:
        AR = st.enter_context(nc.sbuf_tensor("AR", [128, ARENA_W], F32))
        ps = st.enter_context(nc.psum_tensor("ps", [128, 8, 512], F32))
        A = Arena(AR, ARENA_W)

        ident_f = A.f32(128)
        ones_f = A.f32(128)
        vec = A.f32(128)
        cT_s = vec[:, 0:16]
        bada_s = vec[:, 16:40]
        ng_s = vec[:, 40:48]
        ada = vec[:, 48:72]
        shift = ada[:, 0:8]
        scale = ada[:, 8:16]
        gate = ada[:, 16:24]
        gp = vec[:, 72:80]
        gqk_s = vec[:, 80:82]
        gq8 = vec[:, 82:83]
        gk = vec[:, 81:82]
        cw_s = vec[:, 84:96]
        bf_s = vec[0:8, 96:97]
        nbf = vec[0:8, 97:98]
        ss1 = vec[:, 100:104]
        lnv1 = vec[:, 104:108]
        rstd1 = vec[:, 108:112]
        ident_b = A.bf16(128)
        maskL = A.bf16(128)
        bdiag = A.bf16(128)
        convdiag = A.bf16(12 * 128).rearrange("p (c f) -> p c f", c=12)
        wf_b = A.bf16(64).rearrange("p (j f) -> p j f", j=8)
        wf_s = A.f32(64).rearrange("p (j f) -> p j f", j=8)
        oaT = A.bf16(4 * S).rearrange("p (c t) -> p c t", c=4)
        rs_all = A.f32(32)
        base_off = A.off

        hT = A.bf16(8 * S).rearrange("p (j t) -> p j t", j=8)
        kv_off = A.off
        KA = A.bf16(2 * S).rearrange("p (h t) -> p h t", h=2)
        VA = A.bf16(32 * 2 * 128).rearrange("p (s h f) -> p s h f", s=32, h=2)
        rest_off = A.off
        QA = A.bf16(2 * 2 * 512).rearrange("p (b h t) -> p b h t", b=2, h=2)
        zs = A.bf16(2 * 512).rearrange("p (b t) -> p b t", b=2)
        wbf = A.bf16(2 * 8 * 512).rearrange("p (b j n) -> p b j n", b=2, j=8)
        wst = A.f32(2 * 8 * 128).rearrange("p (b j n) -> p b j n", b=2, j=8)
        Frows = A.bf16(S)
        sc_off = A.off
        qraw = A.f32(1024).rearrange("p (b t) -> p b t", b=2)
        sq = A.bf16(1024).rearrange("p (b t) -> p b t", b=2)
        lnv = A.f32(1024).rearrange("p (b t) -> p b t", b=2)
        rstd = A.f32(1024).rearrange("p (b t) -> p b t", b=2)
        PT = A.bf16(4 * 512).rearrange("p (b t) -> p b t", b=4)
        rden = A.f32(1024).rearrange("p (b t) -> p b t", b=2)
        tmpz = A.f32(1024).rearrange("p (b t) -> p b t", b=2)
        th = A.bf16(1024).rearrange("p (b t) -> p b t", b=2)
        endA = A.off
        A.off = kv_off
        xbA = A.f32(3 * 1024).rearrange("p (b t) -> p b t", b=3)
        xnA = A.f32(2 * 1024).rearrange("p (b t) -> p b t", b=2)
        junkA = A.bf16(1024)
        assert A.off <= rest_off
        A.off = rest_off
        adast = A.f32(3 * 8 * 512).rearrange("p (b j n) -> p b j n", b=3, j=8)
        adarow = A.f32(3072)
        assert A.off <= endA
        A.off = kv_off
        e_all = A.f32(S)
        F_all = A.f32(S)
        assert A.off <= rest_off
        A.off = sc_off
        HMLa = A.bf16(3 * S).rearrange("p (r t) -> p r t", r=3)
        ones512 = A.f32(512)
        assert A.off <= endA
        A.off = kv_off
        obT = A.bf16(4 * S).rearrange("p (c t) -> p c t", c=4)
        assert A.off == rest_off
        A.off = sc_off
        gcs = A.bf16(1024).rearrange("p (b t) -> p b t", b=2)
        gcu = A.bf16(2 * 516).rearrange("p (b t) -> p b t", b=2)
        thz = A.bf16(1024).rearrange("p (b t) -> p b t", b=2)
        a1 = A.f32(1024).rearrange("p (b t) -> p b t", b=2)
        a2 = A.f32(1024).rearrange("p (b t) -> p b t", b=2)
        assert A.off <= endA
        A.off = base_off
        Wg = A.bf16(8 * 2048).rearrange("p (j n) -> p j n", j=8)
        WA = A.bf16(4 * 1024).rearrange("p (c n) -> p c n", c=4)
        WB = A.bf16(4 * 1024).rearrange("p (c n) -> p c n", c=4)
        WO = A.bf16(8 * 1024).rearrange("p (j n) -> p j n", j=8)
        assert A.off == kv_off
        A.off = rest_off
        hTt = A.bf16(2 * 8 * 512).rearrange("p (b j t) -> p b j t", b=2, j=8)
        mT = A.bf16(8 * 512).rearrange("p (j t) -> p j t", j=8)
        xbB = A.f32(2 * 1024).rearrange("p (b t) -> p b t", b=2)
        xnB = A.f32(2 * 1024).rearrange("p (b t) -> p b t", b=2)
        junkB = A.bf16(1024)
        thg = A.bf16(2 * 1024).rearrange("p (b t) -> p b t", b=2)
        t12 = A.f32(2 * 1024).rearrange("p (b t) -> p b t", b=2)
        gate_bc = A.f32(1024)
        Dj = A.f32(128)
        xr_off = A.off
        xr = A.f32(3 * 1024).rearrange("p (b t) -> p b t", b=3)
        assert A.off <= ARENA_W, A.off
        A.off = xr_off
        wstB = A.f32(3 * 1024).rearrange("p (b n) -> p b n", b=3)

        def dma(out, in_, reads=(), writes=()):
            return P.op("sp", lambda e: e.dma_start(out=out, in_=in_), reads=reads, writes=writes, dma=True)

        def barrier():
            mk = {
                "pe": lambda e: e.nop(),
                "act": lambda e: e.nop(),
                "dve": lambda e: e.nop(),
                "pool": lambda e: e.nop(),
                "sp": lambda e: e.nop(),
            }
            P.barrier(mk)

        P.op("pool", lambda e: e.memset(ident_f, 1.0), writes=["ident_f"])
        P.op("pool", lambda e: e.affine_select(out=ident_f, in_=ident_f, pattern=[[-1, 128]], base=0,
                                               channel_multiplier=1, compare_op=ALU.is_equal, fill=0.0),
             reads=["ident_f"], writes=["ident_f"])
        P.op("pool", lambda e: e.tensor_copy(out=ident_b, in_=ident_f), reads=["ident_f"], writes=["ident_b"])
        P.op("pool", lambda e: e.memset(ones_f, 1.0), writes=["ones_f"])
        P.op("pool", lambda e: e.memset(maskL, -30000.0), writes=["maskL"])
        P.op("pool", lambda e: e.affine_select(out=maskL, in_=maskL, pattern=[[1, 128]], base=0,
                                               channel_multiplier=-1, compare_op=ALU.is_gt, fill=0.0),
             reads=["maskL"], writes=["maskL"])
        P.op("pool", lambda e: e.memset(bdiag, 0.0), writes=["bdiag"])
        P.op("pool", lambda e: e.memset(bdiag[0:64, 0:64], 1.0), reads=["bdiag"], writes=["bdiag"])
        P.op("pool", lambda e: e.memset(bdiag[64:128, 64:128], 1.0), reads=["bdiag"], writes=["bdiag"])
        dma(cT_s, cT2[:, :], writes=["cT"])
        dma(bada_s, b_adaT[:, :], writes=["bada"])
        dma(ng_s, norm_gT[:, :], writes=["ng"])
        dma(gqk_s, gqk[:, :], writes=["gqk"])
        dma(cw_s, conv_wT[:, :], writes=["cw"])
        dma(bf_s, b_f[:, :], writes=["bf"])
        dma(wf_s, w_in_v[:, :, COL_F:COL_F + 8], writes=["wf_s"])
        P.op("dve", lambda e: e.tensor_copy(out=wf_b, in_=wf_s), reads=["wf_s"], writes=["wf_b"])
        P.op("dve", lambda e: e.tensor_scalar(out=gq8, in0=gqk_s[:, 0:1], scalar1=0.125, scalar2=None, op0=ALU.mult),
             reads=["gqk"], writes=["gq8"])
        P.op("dve", lambda e: e.tensor_scalar(out=nbf, in0=bf_s, scalar1=-1.0, scalar2=None, op0=ALU.mult),
             reads=["bf"], writes=["nbf"])
        for c in range(12):
            P.op("dve", lambda e, c=c: e.tensor_scalar(out=convdiag[:, c, :], in0=ident_f, scalar1=cw_s[:, c:c + 1],
                                                       scalar2=None, op0=ALU.mult),
                 reads=["ident_f", "cw"], writes=[("convdiag", c)])

        for m4 in range(6):
            b = m4 % 3
            dma(adast[:, b], w_ada_v[:, :, m4 * 512:(m4 + 1) * 512], writes=[("adast", b)])
            for j in range(8):
                P.op("pe", lambda e, m4=m4, j=j, b=b: e.matmul(ps[0:1, m4, :], lhsT=cT_s[:, 2 * j:2 * j + 1], rhs=adast[:, b, j, :],
                                                               start=(j == 0), stop=(j == 7)),
                     reads=[("adast", b), "cT"], writes=[("ps", m4)])
            P.op("dve", lambda e, m4=m4: e.tensor_copy(out=adarow[0:1, m4 * 512:(m4 + 1) * 512], in_=ps[0:1, m4, :]),
                 reads=[("ps", m4)], writes=[("adarow", m4)])
        for m in range(24):
            P.op("pe", lambda e, m=m: e.matmul(ps[:, 7, 2 * m:2 * m + 2], lhsT=adarow[0:1, m * 128:(m + 1) * 128], rhs=ones_f[0:1, 0:2],
                                               start=True, stop=True),
                 reads=[("adarow", m // 4), "ones_f"], writes=[("ps", 7)])
        P.op("dve", lambda e: e.tensor_tensor(out=ada, in0=ps[:, 7, 0:48].rearrange("p (m t) -> p m t", t=2)[:, :, 0],
                                              in1=bada_s, op=ALU.add),
             reads=[("ps", 7), "bada"], writes=["ada"])
        P.op("dve", lambda e: e.scalar_tensor_tensor(out=gp, in0=scale, scalar=1.0, in1=ng_s, op0=ALU.add, op1=ALU.mult),
             reads=["ada", "ng"], writes=["gp"])

        xn_cnt = [0]

        def xnorm_sub(s, xb, nxb, xn, junk, dest, dest_res, psb, part=3, kk=None):
            if part & 1:
                k = xn_cnt[0]
            else:
                k = kk
            if part == 1:
                xn_cnt[0] += 1
            return _xnorm_sub(s, xb, nxb, xn, junk, dest, dest_res, psb, part, k)

        def _xnorm_sub(s, xb, nxb, xn, junk, dest, dest_res, psb, part, k):
            if part == 3:
                xn_cnt[0] += 1
            bx, bn, m = k % nxb, k % 2, k % 4
            if part & 1:
                _xnorm_front(s, xb, xn, junk, bx, bn, m)
            if part & 2:
                _xnorm_back(xn, dest, dest_res, psb, bn)
            return k

        def _xnorm_front(s, xb, xn, junk, bx, bn, m, reuse=False):
            dma(xb[:, bx], x[s * 128:(s + 1) * 128, :], writes=[("xb", bx)])
            if reuse:
                P.op("act", lambda e: e.activation(out=xn[:, bn], in_=xb[:, bx], func=AF.Copy, scale=rs_all[:, s:s + 1]),
                     reads=[("xb", bx)], writes=[("xn", bn)])
                return
            P.op("act", lambda e: e.activation(out=junk, in_=xb[:, bx], func=AF.Square, accum_out=ss1[:, m:m + 1]),
                 reads=[("xb", bx)], writes=["junk", ("ss1", m)])
            P.op("act", lambda e: e.activation(out=lnv1[:, m:m + 1], in_=ss1[:, m:m + 1], func=AF.Ln, bias=EPS, scale=1.0 / D),
                 reads=[("ss1", m)], writes=[("lnv1", m)])
            P.op("act", lambda e: e.activation(out=rs_all[:, s:s + 1], in_=lnv1[:, m:m + 1], func=AF.Exp, scale=-0.5),
                 reads=[("lnv1", m)], writes=[("rs_all", s)])
            P.op("act", lambda e: e.activation(out=xn[:, bn], in_=xb[:, bx], func=AF.Copy, scale=rs_all[:, s:s + 1]),
                 reads=[("xb", bx), ("rs_all", s)], writes=[("xn", bn)])

        def _xnorm_back(xn, dest, dest_res, psb, bn):
            psx = ps[:, psb:psb + 2, :].rearrange("p b (jj t) -> p (b jj) t", t=128)
            for j in range(8):
                P.op("pe", lambda e, j=j: e.transpose(out=psx[:, j, :], in_=xn[:, bn, j * 128:(j + 1) * 128], identity=ident_f),
                     reads=[("xn", bn), "ident_f"], writes=[("ps", psb + j // 4)])
            for j in range(8):
                P.op("dve", lambda e, j=j: e.tensor_scalar(out=dest(j), in0=psx[:, j, :], scalar1=gp[:, j:j + 1],
                                                           scalar2=shift[:, j:j + 1], op0=ALU.mult, op1=ALU.add),
                     reads=[("ps", psb + j // 4), "gp", "ada"], writes=[dest_res], disjoint=True)

        def f_proj(i):
                pb = 5 + (i % 2)
                for j in range(8):
                    P.op("pe", lambda e, i=i, j=j, pb=pb: e.matmul(ps[0:8, pb, :], lhsT=wf_b[:, j, :], rhs=hT[:, j, i * 512:(i + 1) * 512],
                                                                   start=(j == 0), stop=(j == 7)),
                         reads=["wf_b", ("hT", i)], writes=[("ps", pb)])
                P.op("act", lambda e, i=i, pb=pb: e.activation(out=e_all[0:8, i * 512:(i + 1) * 512], in_=ps[0:8, pb, :], func=AF.Exp,
                                                               bias=nbf, scale=-1.0),
                     reads=[("ps", pb), "nbf"], writes=[("e_all", i)])

        for t4 in range(NQT):
            for ssb in range(4):
                s = 4 * t4 + ssb
                k = xn_cnt[0]
                xn_cnt[0] += 1
                bx, bn, m = k % 3, k % 2, k % 4
                _xnorm_front(s, xbA, xnA, junkA, bx, bn, m)
                for j in range(8):
                    P.op("pe", lambda e, j=j: e.transpose(out=ps[:, j, ssb * 128:(ssb + 1) * 128], in_=xnA[:, bn, j * 128:(j + 1) * 128],
                                                          identity=ident_f),
                         reads=[("xn", bn), "ident_f"], writes=[("ps", j)])
            for j in range(8):
                P.op("dve", lambda e, j=j: e.tensor_scalar(out=hT[:, j, t4 * 512:(t4 + 1) * 512], in0=ps[:, j, :], scalar1=gp[:, j:j + 1],
                                                           scalar2=shift[:, j:j + 1], op0=ALU.mult, op1=ALU.add),
                     reads=[("ps", j), "gp", "ada"], writes=[("hT", t4)], disjoint=True)
        barrier()
        for i in range(NQT):
            f_proj(i)

        P.op("pool", lambda e: e.memset(ones512[0:8, :], 1.0), writes=["ones512"])
        P.op("act", lambda e: e.activation(out=e_all[0:8, :], in_=e_all[0:8, :], func=AF.Ln, bias=1.0, scale=1.0),
             reads=[("e_all", i) for i in range(NQT)], writes=["sp_all"])
        for i in range(NQT):
            init = 0.0 if i == 0 else F_all[0:8, i * 512 - 1:i * 512]
            P.op("dve", lambda e, i=i, init=init: e.tensor_tensor_scan(out=F_all[0:8, i * 512:(i + 1) * 512], data0=ones512[0:8, :],
                                                                       data1=e_all[0:8, i * 512:(i + 1) * 512], initial=init,
                                                                       op0=ALU.mult, op1=ALU.subtract),
                 reads=["sp_all", "ones512", "F_all"], writes=["F_all"])
        P.op("dve", lambda e: e.tensor_copy(out=HMLa[0:8, 0], in_=F_all[0:8, :]), reads=["F_all"], writes=["H0"])
        P.op("dve", lambda e: e.tensor_tensor(out=e_all[0:8, :], in0=F_all[0:8, :], in1=HMLa[0:8, 0], op=ALU.subtract),
             reads=["F_all", "H0", "sp_all"], writes=["R1a"])
        P.op("dve", lambda e: e.tensor_copy(out=HMLa[0:8, 1], in_=e_all[0:8, :]), reads=["R1a"], writes=["H1"])
        P.op("dve", lambda e: e.tensor_tensor(out=F_all[0:8, :], in0=e_all[0:8, :], in1=HMLa[0:8, 1], op=ALU.subtract),
             reads=["R1a", "H1", "F_all"], writes=["R2a"])
        P.op("dve", lambda e: e.tensor_copy(out=HMLa[0:8, 2], in_=F_all[0:8, :]), reads=["R2a"], writes=["H2"])
        for r in range(3):
            dma(Frows[8 * r:8 * r + 8, :], HMLa[0:8, r], reads=["H%d" % r], writes=[("Frows", i) for i in range(NQT)])
        if "Frows" in debug:
            dbg["Frows"] = dram("dbg_Frows", [24, S], kind="ExternalOutput", dtype=BF16)
            dma(dbg["Frows"][:, :], Frows[0:24, :], reads=[("Frows", i) for i in range(NQT)])
        if "hT" in debug:
            dbg["hT"] = dram("dbg_hT", [128, 8, S], kind="ExternalOutput", dtype=BF16)
            dma(dbg["hT"][:, :, :], hT, reads=[("hT", i) for i in range(NQT)])
        barrier()

        P.op("pool", lambda e: e.memset(KA[64:70, :, :], 1.0), writes=["KAaug"])
        P.op("pool", lambda e: e.memset(QA[64:70, :, :, :], -1.0), writes=["QAaug"])
        P.op("pool", lambda e: e.memset(VA[:, :, 0, 64:128], 1.0), writes=["VAones"])
        P.op("pool", lambda e: e.memset(VA[:, :, 1, 0:64], 1.0), writes=["VAones"])
        from collections import deque

        def load_pair_weights(cols, wb, ce="pool"):
            for pi, c0 in enumerate(cols):
                b = load_pair_weights.n % 2
                load_pair_weights.n += 1
                dma(wst[:, b], w_in_v[:, :, c0:c0 + 128], writes=[("wst", b)])
                if ce == "act":
                    P.op("act", lambda e, b=b, pi=pi: e.activation(out=wbf[:, wb, :, pi * 128:(pi + 1) * 128], in_=wst[:, b], func=AF.Copy),
                         reads=[("wst", b)], writes=[("wbf", wb, pi)])
                else:
                    P.op("pool", lambda e, b=b, pi=pi: e.tensor_copy(out=wbf[:, wb, :, pi * 128:(pi + 1) * 128], in_=wst[:, b]),
                         reads=[("wst", b)], writes=[("wbf", wb, pi)])
        load_pair_weights.n = 0

        def proj_fm(wb, pi, i, pb, js=range(8)):
            for j in js:
                P.op("pe", lambda e, j=j: e.matmul(ps[:, pb, :], lhsT=wbf[:, wb, j, pi * 128:(pi + 1) * 128],
                                                   rhs=hT[:, j, i * 512:(i + 1) * 512], start=(j == 0), stop=(j == 7)),
                     reads=[("wbf", wb, pi), ("hT", i)], writes=[("ps", pb)])

        def chain_tasks(wb, pi, i, pb, b, gcol, dest0, dest1, res0, res1):
            t = []
            t.append(lambda: proj_fm(wb, pi, i, pb, range(0, 4)))
            t.append(lambda: proj_fm(wb, pi, i, pb, range(4, 8)))

            def t_copy():
                P.op("dve", lambda e: e.tensor_copy(out=qraw[:, b], in_=ps[:, pb, :]), reads=[("ps", pb)], writes=[("qraw", b)])
                P.op("pool", lambda e: e.tensor_tensor(out=sq[:, b], in0=qraw[:, b], in1=qraw[:, b], op=ALU.mult),
                     reads=[("qraw", b)], writes=[("sq", b)])
            t.append(t_copy)

            def t_ss():
                P.op("pe", lambda e: e.matmul(ps[:, pb, :], lhsT=bdiag, rhs=sq[:, b], start=True, stop=True),
                     reads=[("sq", b), "bdiag"], writes=[("ps", pb)])

            def t_act():
                P.op("act", lambda e: e.activation(out=lnv[:, b], in_=ps[:, pb, :], func=AF.Ln, bias=EPS, scale=1.0 / 64),
                     reads=[("ps", pb)], writes=[("lnv", b)])
                P.op("act", lambda e: e.activation(out=rstd[:, b], in_=lnv[:, b], func=AF.Exp, scale=-0.5),
                     reads=[("lnv", b)], writes=[("rstd", b)])

            def t_out():
                P.op("dve", lambda e: e.scalar_tensor_tensor(out=dest0, in0=qraw[0:64, b], scalar=gcol[0:64], in1=rstd[0:64, b],
                                                             op0=ALU.mult, op1=ALU.mult),
                     reads=[("qraw", b), ("rstd", b), "gq8", "gqk"], writes=[res0])
                P.op("dve", lambda e: e.scalar_tensor_tensor(out=dest1, in0=qraw[64:128, b], scalar=gcol[64:128], in1=rstd[64:128, b],
                                                             op0=ALU.mult, op1=ALU.mult),
                     reads=[("qraw", b), ("rstd", b), "gq8", "gqk"], writes=[res1])
            return t, t_ss, t_act, t_out

        def prep_tasks(g, wb, i, late_writes=False):
            qb = i % 2
            tasks = []

            def t_dma():
                for hh in range(2):
                    for r in range(3):
                        dma(QA[64 + r:65 + r, qb, hh, :], Frows[8 * r + 2 * g + hh:8 * r + 2 * g + hh + 1, i * 512:(i + 1) * 512],
                            reads=[("Frows", i), "QAaug"], writes=[("QAF", qb, hh)])
            tasks.append(t_dma)
            kt, k_ss, k_act, k_out = chain_tasks(wb, 1, i, 6, 0, gk, KA[0:64, 0, i * 512:(i + 1) * 512], KA[0:64, 1, i * 512:(i + 1) * 512],
                                                 ("KA", 0, i), ("KA", 1, i))
            qt, q_ss, q_act, q_out = chain_tasks(wb, 0, i, 7, 1, gq8, QA[0:64, qb, 0, :], QA[0:64, qb, 1, :], ("QA", qb, 0), ("QA", qb, 1))
            tasks += kt
            tasks += qt
            for ssb in range(4):
                def t_v(ssb=ssb):
                    s = 4 * i + ssb
                    for j in range(8):
                        P.op("pe", lambda e, j=j: e.matmul(ps[:, 6, ssb * 128:(ssb + 1) * 128], lhsT=hT[:, j, s * 128:(s + 1) * 128],
                                                           rhs=wbf[:, wb, j, 256:384], start=(j == 0), stop=(j == 7)),
                             reads=[("wbf", wb, 2), ("hT", i)], writes=[("ps", 6)])
                t_v.is_v = True
                tasks.append(t_v)

            def t_vcopy():
                P.op("dve", lambda e: e.tensor_copy(out=VA[:, 4 * i:4 * i + 4, 0, 0:64],
                                                    in_=ps[:, 6, :].rearrange("p (s h f) -> p s h f", s=4, h=2)[:, :, 0, :]),
                     reads=[("ps", 6)], writes=[("VA", i)], disjoint=True)
                P.op("dve", lambda e: e.tensor_copy(out=VA[:, 4 * i:4 * i + 4, 1, 64:128],
                                                    in_=ps[:, 6, :].rearrange("p (s h f) -> p s h f", s=4, h=2)[:, :, 1, :]),
                     reads=[("ps", 6)], writes=[("VA", i)], disjoint=True)
            t_vcopy.is_v = True
            tasks.append(t_vcopy)
            tasks.append(k_ss)
            tasks.append(q_ss)
            tasks.append(k_act)
            tasks.append(q_act)
            tasks.append(q_out)
            tasks.append(k_out)
            if late_writes:
                vt = [t for t in tasks if getattr(t, "is_v", False)]
                for t in vt:
                    tasks.remove(t)
                tasks += vt
            return tasks

        for g in range(n_pairs):
            wb = g % 2
            if g == 0:
                load_pair_weights([COL_Q + g * 128, COL_K + g * 128, COL_V + g * 128], wb)
            for hh in range(2):
                for r in range(3):
                    dma(KA[67 + r:68 + r, hh, :], Frows[8 * r + 2 * g + hh:8 * r + 2 * g + hh + 1, :],
                        reads=[("Frows", i) for i in range(NQT)], writes=[("KAF", hh)])

            units = []
            for i in range(NQT):
                for hh in range(2):
                    for kb in range(4 * i + 4):
                        units.append((i, hh, kb))

            def qk(n, g=g):
                i, hh, kb = units[n]
                sb = n % 4
                qb = i % 2
                jd = kb - 4 * i
                c0 = max(jd, 0) * 128
                P.op("pe", lambda e: e.matmul(ps[:, sb, c0:512], lhsT=KA[0:70, hh, kb * 128:(kb + 1) * 128],
                                              rhs=QA[0:70, qb, hh, c0:512], start=True, stop=(jd < 0)),
                     reads=[("KA", hh, kb // 4), ("KAF", hh), "KAaug", ("QA", qb, hh), ("QAF", qb, hh), "QAaug"],
                     writes=[("ps", sb)])
                if jd >= 0:
                    P.op("pe", lambda e: e.matmul(ps[:, sb, c0:c0 + 128], lhsT=maskL, rhs=ident_b, start=False, stop=True),
                         reads=["maskL", "ident_b"], writes=[("ps", sb)])

            def exp_pv(n, g=g):
                i, hh, kb = units[n]
                sb, pbuf = n % 4, n % 4
                ob = 4 + ((2 * i + hh) % 2)
                jd = kb - 4 * i
                c0 = max(jd, 0) * 128
                last = (kb == 4 * i + 3)
                P.op("act", lambda e: e.activation(out=PT[:, pbuf, c0:512], in_=ps[:, sb, c0:512], func=AF.Exp),
                     reads=[("ps", sb)], writes=[("PT", pbuf)])
                P.op("pe", lambda e: e.matmul(ps[:, ob, c0:512], lhsT=VA[:, kb, hh, :], rhs=PT[:, pbuf, c0:512],
                                              start=(kb == 0), stop=last),
                     reads=[("VA", kb // 4), "VAones", ("PT", pbuf)], writes=[("ps", ob)])
                if last:
                    rb = (2 * i + hh) % 2
                    qb = i % 2
                    lo, hi = (0, 64) if hh == 0 else (64, 128)
                    dlo, dhi = (64, 128) if hh == 0 else (0, 64)
                    P.op("dve", lambda e: e.reciprocal(out=rden[lo:hi, rb], in_=ps[dlo:dhi, ob, :]),
                         reads=[("ps", ob)], writes=[("rden", rb)])
                    P.op("dve", lambda e: e.tensor_tensor(out=oaT[lo:hi, g, i * 512:(i + 1) * 512], in0=ps[lo:hi, ob, :],
                                                          in1=rden[lo:hi, rb], op=ALU.mult),
                         reads=[("ps", ob), ("rden", rb)], writes=[("oaT", g, i, hh)])

            side = deque()
            if g == 0:
                for t in prep_tasks(g, wb, 0):
                    t()
            first_unit_of_tile = {}
            for n, (i, hh, kb) in enumerate(units):
                if hh == 0 and kb == 0:
                    first_unit_of_tile[i] = n
            qk(0)
            qk(1)
            qk(2)
            for n in range(len(units)):
                i, hh, kb = units[n]
                if hh == 0 and kb == 0:
                    if i + 1 < NQT:
                        side.extend(prep_tasks(g, wb, i + 1))
                    elif g + 1 < n_pairs:
                        g1 = g + 1
                        side.append(lambda g1=g1: load_pair_weights([COL_Q + g1 * 128, COL_K + g1 * 128, COL_V + g1 * 128], g1 % 2))
                        side.extend(prep_tasks(g1, g1 % 2, 0, late_writes=True))
                    elif do_conv:
                        side.append(lambda: load_pair_weights([COL_GB, COL_GC, COL_U, COL_ZB], (n_pairs - 1 + 1) % 2))
                if n + 3 < len(units):
                    i2 = units[n + 3][0]
                    if i2 != i and first_unit_of_tile.get(i2) == n + 3:
                        while side:
                            side.popleft()()
                    qk(n + 3)
                exp_pv(n)
                ntile = 2 * (4 * i + 4)
                per = max(1, -(-28 // max(ntile - 4, 1)))
                for _ in range(per):
                    if side:
                        side.popleft()()
            while side:
                side.popleft()()

        if "oaT" in debug:
            dbg["oaT"] = dram("dbg_oaT", [128, 4, S], kind="ExternalOutput", dtype=BF16)
            dma(dbg["oaT"][:, :, :], oaT, reads=[("oaT", g, i, hh) for g in range(n_pairs) for i in range(NQT) for hh in range(2)])
        if "KA" in debug:
            dbg["KA"] = dram("dbg_KA", [70, 2, S], kind="ExternalOutput", dtype=BF16)
            dma(dbg["KA"][:, :, :], KA[0:70], reads=[("KA", hh, i) for hh in range(2) for i in range(NQT)] + [("KAF", 0), ("KAF", 1), "KAaug"])
        barrier()

        if do_conv:
            sa = n_pairs % 2
            sb = 1 - sa
            if n_pairs == 0:
                load_pair_weights([COL_GB, COL_GC, COL_U, COL_ZB], sa)

            def conv_w(c, slot):
                load_pair_weights([COL_GB + c * 128, COL_GC + c * 128, COL_U + c * 128, COL_ZB + c * 128], slot, ce="act")

            def zgate(wbz):
                nz = 0
                for gg in range(n_pairs):
                    for i in range(NQT):
                        b = nz % 2
                        pbz = 5 + (nz % 2)
                        nz += 1
                        proj_fm(wbz, gg, i, pbz)
                        P.op("act", lambda e, b=b, pbz=pbz: e.activation(out=thz[:, b], in_=ps[:, pbz, :], func=AF.Tanh, scale=0.5),
                             reads=[("ps", pbz)], writes=[("thz", b)])
                        P.op("dve", lambda e, b=b, pbz=pbz: e.scalar_tensor_tensor(out=gcs[:, b], in0=thz[:, b], scalar=1.0, in1=ps[:, pbz, :],
                                                                                  op0=ALU.add, op1=ALU.mult),
                             reads=[("thz", b), ("ps", pbz)], writes=[("gcs", b)])
                        P.op("pool", lambda e, b=b, gg=gg, i=i: e.tensor_tensor(out=oaT[:, gg, i * 512:(i + 1) * 512], in0=oaT[:, gg, i * 512:(i + 1) * 512],
                                                                              in1=gcs[:, b], op=ALU.mult),
                             reads=[("gcs", b)], writes=[("oaTz", gg, i)])

            def conv_chunk(c, wb):
                for i in range(NQT):
                    par = i % 2
                    b = i % 2
                    proj_fm(wb, 1, i, 0)
                    proj_fm(wb, 2, i, 1)
                    P.op("act", lambda e, b=b: e.activation(out=gcs[:, b], in_=ps[:, 0, :], func=AF.Copy),
                         reads=[("ps", 0)], writes=[("gcs", b)])
                    if i == 0:
                        P.op("pool", lambda e, par=par: e.memset(gcu[:, par, 0:2], 0.0), writes=[("gcu", par)])
                    else:
                        P.op("pool", lambda e, par=par: e.tensor_copy(out=gcu[:, par, 0:2], in_=gcu[:, 1 - par, 512:514]),
                             reads=[("gcu", 1 - par)], writes=[("gcu", par)])
                    P.op("dve", lambda e, b=b, par=par: e.tensor_tensor(out=gcu[:, par, 2:514], in0=ps[:, 1, :], in1=gcs[:, b], op=ALU.mult),
                         reads=[("ps", 1), ("gcs", b), ("gcu", par)], writes=[("gcu", par)])
                    proj_fm(wb, 0, i, 2)
                    proj_fm(wb, 3, i, 3)
                    for k in range(3):
                        P.op("pe", lambda e, k=k, par=par, c=c: e.matmul(ps[:, 4, :], lhsT=convdiag[:, c * 3 + k, :], rhs=gcu[:, par, k:k + 512],
                                                                         start=(k == 0), stop=(k == 2)),
                             reads=[("convdiag", c * 3 + k), ("gcu", par)], writes=[("ps", 4)])
                    P.op("act", lambda e, b=b: e.activation(out=thz[:, b], in_=ps[:, 3, :], func=AF.Tanh, scale=0.5),
                         reads=[("ps", 3)], writes=[("thz", b)])
                    P.op("dve", lambda e, b=b: e.scalar_tensor_tensor(out=a1[:, b], in0=thz[:, b], scalar=1.0, in1=ps[:, 3, :],
                                                                     op0=ALU.add, op1=ALU.mult),
                         reads=[("thz", b), ("ps", 3)], writes=[("a1", b)])
                    P.op("dve", lambda e, b=b: e.tensor_tensor(out=a2[:, b], in0=ps[:, 2, :], in1=a1[:, b], op=ALU.mult),
                         reads=[("ps", 2), ("a1", b)], writes=[("a2", b)])
                    P.op("dve", lambda e, b=b, c=c, i=i: e.tensor_tensor(out=obT[:, c, i * 512:(i + 1) * 512], in0=ps[:, 4, :], in1=a2[:, b], op=ALU.mult),
                         reads=[("ps", 4), ("a2", b)], writes=[("obT", c, i)])

            load_pair_weights([COL_ZA + gg * 128 for gg in range(4)], sb, ce="act")
            conv_chunk(0, sa)
            conv_w(1, sa)
            zgate(sb)
            conv_w(2, sb)
            conv_chunk(1, sa)
            conv_w(3, sa)
            conv_chunk(2, sb)
            conv_chunk(3, sa)
            if "obT" in debug:
                dbg["obT"] = dram("dbg_obT", [128, 4, S], kind="ExternalOutput", dtype=BF16)
                dma(dbg["obT"][:, :, :], obT, reads=[("obT", c, i) for c in range(4) for i in range(NQT)])
        barrier()

        out_dmas = []
        if do_b:
            for j in range(8):
                P.op("dve", lambda e, j=j: e.tensor_scalar(out=Dj, in0=ident_f, scalar1=gate[:, j:j + 1], scalar2=0.25,
                                                           op0=ALU.mult, op1=ALU.mult),
                     reads=["ident_f", "ada"], writes=["Dj"])
                P.op("pe", lambda e, j=j: e.matmul(ps[:, 2 + j // 4, (j % 4) * 128:(j % 4 + 1) * 128], lhsT=ones_f, rhs=Dj, start=True, stop=True),
                     reads=["ones_f", "Dj"], writes=[("ps", 2 + j // 4)])
            for hb in range(2):
                P.op("dve", lambda e, hb=hb: e.tensor_copy(out=gate_bc[:, hb * 512:(hb + 1) * 512], in_=ps[:, 2 + hb, :]),
                     reads=[("ps", 2 + hb)], writes=["gate_bc"])

            def xn_tile(it):
                hb = it % 2
                for sub in range(4):
                    s_ = it * 4 + sub
                    kk0 = xn_cnt[0]
                    xn_cnt[0] += 1
                    _xnorm_front(s_, xbB, xnB, junkB, kk0 % 2, kk0 % 2, kk0 % 4, reuse=True)
                    _xnorm_back(xnB, lambda j, sub=sub: hTt[:, hb, j, sub * 128:(sub + 1) * 128], ("hTt", hb), 0, kk0 % 2)

            xn_tile(0)
            nld = [0]
            cast_engs = ["dve", "act"]

            def load_w(src_ap, shape3, dst, wres, mul_gate=None):
                b = nld[0] % 3
                ce = cast_engs[nld[0] % 2]
                nld[0] += 1
                view = wstB[:, b].rearrange("p (j n) -> p j n", j=shape3)
                dma(view, src_ap, writes=[("wstB", b)])
                if mul_gate is not None:
                    P.op("pool" if ce == "act" else ce, lambda e: e.tensor_tensor(out=dst, in0=view, in1=mul_gate, op=ALU.mult),
                         reads=[("wstB", b), "gate_bc"], writes=[wres])
                elif ce == "act":
                    P.op("act", lambda e: e.activation(out=dst, in_=view, func=AF.Copy), reads=[("wstB", b)], writes=[wres])
                else:
                    P.op(ce, lambda e: e.tensor_copy(out=dst, in_=view), reads=[("wstB", b)], writes=[wres])

            w_a_v = w_a.rearrange("(c p) n -> p c n", p=128)
            w_b_v = w_b.rearrange("(c p) n -> p c n", p=128)
            w_o_v = w_o.rearrange("(j p) n -> p j n", p=128)
            for fc in range(8):
                for cch in (fc, 8 + fc):
                    c0 = COL_GA + cch * 128
                    load_w(w_in_v[:, :, c0:c0 + 128], 8, Wg[:, :, cch * 128:(cch + 1) * 128], ("Wg", cch))
                if fc % 2 == 0:
                    q4 = fc // 2
                    load_w(w_a_v[:, :, q4 * 256:(q4 + 1) * 256], 4, WA[:, :, q4 * 256:(q4 + 1) * 256], ("WA", q4))
                    load_w(w_b_v[:, :, q4 * 256:(q4 + 1) * 256], 4, WB[:, :, q4 * 256:(q4 + 1) * 256], ("WB", q4))
            for q8 in range(8):
                load_w(w_o_v[:, :, q8 * 128:(q8 + 1) * 128], 8, WO[:, :, q8 * 128:(q8 + 1) * 128], ("WO", q8),
                       mul_gate=gate_bc[:, q8 * 128:(q8 + 1) * 128].unsqueeze(1).to_broadcast([128, 8, 128]))

            xk = {}
            for it in range(NQT):
                hb = it % 2
                t0 = it * 512
                def gates_mm(hb_, fc_):
                    g0_ = 2 if fc_ % 2 == 0 else 4
                    for half in range(2):
                        for j in range(8):
                            P.op("pe", lambda e, j=j, half=half: e.matmul(
                                ps[:, g0_ + half, :], lhsT=Wg[:, j, half * 1024 + fc_ * 128:half * 1024 + (fc_ + 1) * 128],
                                rhs=hTt[:, hb_, j, :], start=(j == 0), stop=(j == 7)),
                                reads=[("hTt", hb_), ("Wg", half * 8 + fc_)], writes=[("ps", g0_ + half)])

                for fc in range(8):
                    g0 = 2 if fc % 2 == 0 else 4
                    tb_ = fc % 2
                    if not (fc == 0 and it > 0):
                        gates_mm(hb, fc)
                    for c in range(4):
                        P.op("pe", lambda e, c=c: e.matmul(ps[:, 6, :], lhsT=WA[:, c, fc * 128:(fc + 1) * 128],
                                                           rhs=oaT[:, c, t0:t0 + 512], start=(c == 0), stop=(c == 3)),
                             reads=[("WA", fc // 2)], writes=[("ps", 6)])
                    for c in range(4):
                        P.op("pe", lambda e, c=c: e.matmul(ps[:, 7, :], lhsT=WB[:, c, fc * 128:(fc + 1) * 128],
                                                           rhs=obT[:, c, t0:t0 + 512], start=(c == 0), stop=(c == 3)),
                             reads=[("WB", fc // 2)], writes=[("ps", 7)])
                    P.op("act", lambda e: e.activation(out=thg[:, tb_].rearrange("p (a t) -> p a t", a=2), in_=ps[:, g0:g0 + 2, :],
                                                       func=AF.Tanh, scale=0.5),
                         reads=[("ps", g0), ("ps", g0 + 1)], writes=[("thg", tb_)])
                    P.op("dve", lambda e: e.scalar_tensor_tensor(out=t12[:, tb_].rearrange("p (a t) -> p a t", a=2),
                                                                 in0=thg[:, tb_].rearrange("p (a t) -> p a t", a=2), scalar=1.0,
                                                                 in1=ps[:, 6:8, :], op0=ALU.add, op1=ALU.mult),
                         reads=[("thg", tb_), ("ps", 6), ("ps", 7)], writes=[("t12", tb_)])
                    P.op("pool", lambda e: e.tensor_tensor(out=mT[:, fc, :], in0=t12[:, tb_, 0:512], in1=t12[:, tb_, 512:1024], op=ALU.add),
                         reads=[("t12", tb_)], writes=[("mT", fc)])
                    if it + 1 < NQT:
                        hb1 = (it + 1) % 2
                        back_at = {2: [0], 3: [1], 4: [2], 5: [3]}
                        front_at = {0: 0, 1: 1, 2: 2, 3: 3}
                        for sub in back_at.get(fc, []):
                            _xnorm_back(xnB, lambda j, sub=sub: hTt[:, hb1, j, sub * 128:(sub + 1) * 128], ("hTt", hb1), 0, xk[sub] % 2)
                        if fc in front_at:
                            sub = front_at[fc]
                            kk_ = xn_cnt[0]
                            xn_cnt[0] += 1
                            xk[sub] = kk_
                            _xnorm_front((it + 1) * 4 + sub, xbB, xnB, junkB, kk_ % 2, kk_ % 2, kk_ % 4, reuse=True)
                if it == 0 and "mT0" in debug:
                    dbg["mT0"] = dram("dbg_mT0", [128, 8, 512], kind="ExternalOutput", dtype=BF16)
                    dma(dbg["mT0"], mT, reads=[("mT", fc) for fc in range(8)])
                if it + 1 < NQT:
                    gates_mm((it + 1) % 2, 0)
                for sub in range(4):
                    s_ = it * 4 + sub
                    bx = s_ % 3
                    dma(xr[:, bx], x[s_ * 128:(s_ + 1) * 128, :], writes=[("xr", bx)] + ([("wstB", 0), ("wstB", 1), ("wstB", 2)] if s_ < 3 else []))
                    for half in range(2):
                        po = (2 * sub + half) % 2
                        for fc in range(8):
                            P.op("pe", lambda e, fc=fc: e.matmul(ps[:, po, :], lhsT=mT[:, fc, sub * 128:(sub + 1) * 128],
                                                                 rhs=WO[:, fc, half * 512:(half + 1) * 512], start=(fc == 0), stop=(fc == 7)),
                                 reads=[("mT", fc)] + [("WO", q8) for q8 in range(4 * half, 4 * half + 4)], writes=[("ps", po)])
                        P.op("dve", lambda e: e.tensor_tensor(out=xr[:, bx, half * 512:(half + 1) * 512], in0=ps[:, po, :],
                                                              in1=xr[:, bx, half * 512:(half + 1) * 512], op=ALU.add),
                             reads=[("ps", po), ("xr", bx)], writes=[("xr", bx)])
                    out_dmas.append(dma(y[s_ * 128:(s_ + 1) * 128, :], xr[:, bx], reads=[("xr", bx)]))
        P.emit(nc, final_wait_ops=[o for o in P.ops if o.is_dma])
    return nc, dbg


def host_inputs(inputs, b):
    f = lambda a: np.ascontiguousarray(a, dtype=np.float32)
    c = inputs["c"][b]
    cT = c.reshape(8, 128).T
    gq = np.tile(np.asarray(inputs["q_norm_g"][0]), 2)
    gk = np.tile(np.asarray(inputs["k_norm_g"][0]), 2)
    return {
        "x": f(inputs["x"][b]),
        "cT2": f(np.repeat(cT, 2, axis=1)),
        "w_ada": f(inputs["w_ada"][0]),
        "b_adaT": f(np.asarray(inputs["b_ada"][0]).reshape(24, 128).T),
        "norm_gT": f(np.asarray(inputs["norm_g"][0]).reshape(8, 128).T),
        "w_in": f(inputs["w_in"][0]),
        "b_f": f(np.asarray(inputs["b_f"][0]).reshape(8, 1)),
        "gqk": f(np.stack([gq, gk], axis=1)),
        "conv_wT": f(np.asarray(inputs["conv_w"][0]).reshape(3, 4, 128).transpose(2, 1, 0).reshape(128, 12)),
        "w_a": f(inputs["w_attn_out"][0]),
        "w_b": f(inputs["w_conv_out"][0]),
        "w_o": f(inputs["w_o"][0]),
    }


def kernel(**inputs):
    inputs = {k: np.asarray(v) for k, v in inputs.items()}
    nc, _ = build()
    in_maps = [host_inputs(inputs, b) for b in range(8)]
    res = run_bass_kernel_spmd(nc, in_maps, core_ids=list(range(8)))
    return np.stack([np.asarray(r["y"], dtype=np.float32) for r in res.results], axis=0)
```

```python
import contextlib
import numpy as np
import concourse.bass as bass
import concourse.mybir as mybir
from concourse.bass_utils import run_bass_kernel_spmd

F32 = mybir.dt.float32
BF16 = mybir.dt.bfloat16
AF = mybir.ActivationFunctionType
ALU = mybir.AluOpType

S, D = 4096, 1024
NQT = 8
EPS = 1e-6
COL_Q, COL_K, COL_V, COL_F, COL_ZA = 0, 512, 1024, 1536, 1544
COL_GB, COL_GC, COL_U, COL_ZB, COL_GA, COL_GG = 2056, 2568, 3080, 3592, 4104, 5128
NCOL = 6152
ARENA_W = 53000
TB = 256

ENGS = ("pe", "act", "dve", "pool", "sp")
N_DMA_SEMS = 24


class Op:
    __slots__ = ("eng", "fn", "waits", "signal", "cnt", "dma_slot", "dma_val", "is_dma", "idx")


class _Rec:
    def __getattr__(self, name):
        def f(*a, **k):
            self.call = (name, a, k)
            return self
        return f


class Prog:
    def __init__(self):
        self.ops = []
        self.last_w = {}
        self.readers = {}
        self.n_dma = 0
        self.last_dma_on_slot = {}

    def op(self, eng, fn, reads=(), writes=(), dma=False, disjoint=False):
        o = Op()
        rec = _Rec()
        fn(rec)
        call = rec.call
        fn = lambda e, call=call: getattr(e, call[0])(*call[1], **call[2])
        o.eng, o.fn, o.signal, o.cnt, o.is_dma = eng, fn, False, None, dma
        o.idx = len(self.ops)
        deps = set()
        for r in reads:
            w = self.last_w.get(r)
            if w is not None:
                deps.add(w)
        def same_eng(d):
            q = self.ops[d]
            return q.eng == eng and not q.is_dma and not dma
        for r in writes:
            w = self.last_w.get(r)
            if w is not None and not (disjoint and same_eng(w)):
                deps.add(w)
            for rd in self.readers.get(r, {}).values():
                deps.add(rd)
        o.waits = []
        for d in deps:
            p = self.ops[d]
            if p.eng == "pe" and eng == "pe" and not p.is_dma and not dma:
                continue
            p.signal = True
            o.waits.append(d)
        if dma:
            o.dma_slot = self.n_dma % N_DMA_SEMS
            o.dma_val = 16 * (self.n_dma // N_DMA_SEMS + 1)
            self.n_dma += 1
            self.last_dma_on_slot[o.dma_slot] = o.idx
        key = eng if not dma else ("dma", o.idx)
        for r in reads:
            self.readers.setdefault(r, {})[key] = o.idx
        for r in writes:
            self.last_w[r] = o.idx
            self.readers[r] = {}
        self.ops.append(o)
        return o

    def barrier(self, mk):
        last = {}
        for o in self.ops:
            if not o.is_dma:
                last[o.eng] = o.idx
        dmas = list(self.last_dma_on_slot.values())
        bops = []
        for e in ENGS:
            o = self.op(e, mk[e])
            for d in list(last.values()) + dmas:
                self.ops[d].signal = True
                o.waits.append(d)
            bops.append(o)
        self.last_w = {}
        self.readers = {}

    def emit(self, nc, final_wait_ops=()):
        engs = {"pe": nc.tensor, "act": nc.scalar, "dve": nc.vector, "pool": nc.gpsimd, "sp": nc.sync}
        cnt = {e: 0 for e in ENGS}
        for o in self.ops:
            if o.is_dma:
                continue
            if o.signal:
                cnt[o.eng] += 1
                o.cnt = cnt[o.eng]
        with contextlib.ExitStack() as st:
            sems = {e: st.enter_context(nc.semaphore("s_" + e)) for e in ENGS}
            dsems = [st.enter_context(nc.semaphore("d_%d" % i)) for i in range(N_DMA_SEMS)]
            block = st.enter_context(nc.Block())
            ops = self.ops

            def body(ename):
                eng = engs[ename]
                waited = {}

                def need(key, sem, val, pend):
                    if waited.get(key, 0) >= val:
                        return
                    waited[key] = val
                    pend.append((sem, val))

                for o in ops:
                    if o.eng != ename:
                        continue
                    pend = []
                    for d in o.waits:
                        p = ops[d]
                        if p.is_dma:
                            need(("d", p.dma_slot), dsems[p.dma_slot], p.dma_val, pend)
                        else:
                            need(("e", p.eng), sems[p.eng], p.cnt, pend)
                    if o.is_dma and o.dma_val > 16:
                        need(("d", o.dma_slot), dsems[o.dma_slot], o.dma_val - 16, pend)
                    for sem, val in pend:
                        eng.wait_ge(sem, val)
                    inst = o.fn(eng)
                    if o.is_dma:
                        inst.then_inc(dsems[o.dma_slot], 16)
                    elif o.signal:
                        inst.then_inc(sems[o.eng], 1)
                if ename == "sp":
                    for o in final_wait_ops:
                        if waited.get(("d", o.dma_slot), 0) < o.dma_val:
                            eng.wait_ge(dsems[o.dma_slot], o.dma_val)
                            waited[("d", o.dma_slot)] = o.dma_val

            @block.tensor
            def _(e):
                body("pe")

            @block.scalar
            def _(e):
                body("act")

            @block.vector
            def _(e):
                body("dve")

            @block.gpsimd
            def _(e):
                body("pool")

            @block.sync
            def _(e):
                body("sp")


class Arena:
    def __init__(self, ar, total):
        self.ar, self.total, self.off = ar, total, 0

    def f32(self, n):
        a = self.ar[:, self.off:self.off + n]
        self.off += n
        assert self.off <= self.total, (self.off, self.total)
        return a

    def bf16(self, n_el):
        w = (n_el + 1) // 2
        a = self.ar[:, self.off:self.off + w].bitcast(BF16)
        self.off += w
        assert self.off <= self.total, (self.off, self.total)
        return a


def build(debug=(), n_pairs=4, do_conv=True, do_b=True):
    nc = bass.Bass("TRN2", target_bir_lowering=False)

    def dram(name, shape, kind="ExternalInput", dtype=F32):
        return nc.dram_tensor(name, shape, dtype, kind=kind).ap()

    x = dram("x", [S, D])
    cT2 = dram("cT2", [128, 16])
    w_ada = dram("w_ada", [D, 3 * D])
    b_adaT = dram("b_adaT", [128, 24])
    norm_gT = dram("norm_gT", [128, 8])
    w_in = dram("w_in", [D, NCOL])
    b_f = dram("b_f", [8, 1])
    gqk = dram("gqk", [128, 2])
    conv_wT = dram("conv_wT", [128, 12])
    w_a = dram("w_a", [512, D])
    w_b = dram("w_b", [512, D])
    w_o = dram("w_o", [D, D])
    y = dram("y", [S, D], kind="ExternalOutput")
    dbg = {}
    w_in_v = w_in.rearrange("(j p) n -> p j n", p=128)
    w_ada_v = w_ada.rearrange("(j p) n -> p j n", p=128)

    P = Prog()
    st = contextlib.ExitStack()
    with st:
        AR = st.enter_context(nc.sbuf_tensor("AR", [128, ARENA_W], F32))
        ps = st.enter_context(nc.psum_tensor("ps", [128, 8, 512], F32))
        A = Arena(AR, ARENA_W)

        ident_f = A.f32(128)
        ones_f = A.f32(128)
        vec = A.f32(128)
        cT_s = vec[:, 0:16]
        bada_s = vec[:, 16:40]
        ng_s = vec[:, 40:48]
        ada = vec[:, 48:72]
        shift = ada[:, 0:8]
        scale = ada[:, 8:16]
        gate = ada[:, 16:24]
        gp = vec[:, 72:80]
        gqk_s = vec[:, 80:82]
        gq8 = vec[:, 82:83]
        gk = vec[:, 81:82]
        cw_s = vec[:, 84:96]
        bf_s = vec[0:8, 96:97]
        nbf = vec[0:8, 97:98]
        ss1 = vec[:, 100:104]
        lnv1 = vec[:, 104:108]
        rstd1 = vec[:, 108:112]
        ident_b = A.bf16(128)
        maskL = A.bf16(128)
        bdiag = A.bf16(128)
        convdiag = A.bf16(12 * 128).rearrange("p (c f) -> p c f", c=12)
        wf_b = A.bf16(64).rearrange("p (j f) -> p j f", j=8)
        wf_s = A.f32(64).rearrange("p (j f) -> p j f", j=8)
        oaT = A.bf16(4 * S).rearrange("p (c t) -> p c t", c=4)
        rs_all = A.f32(32)
        base_off = A.off

        hT = A.bf16(8 * S).rearrange("p (j t) -> p j t", j=8)
        kv_off = A.off
        KA = A.bf16(2 * S).rearrange("p (h t) -> p h t", h=2)
        VA = A.bf16(32 * 2 * 128).rearrange("p (s h f) -> p s h f", s=32, h=2)
        rest_off = A.off
        QA = A.bf16(2 * 2 * 512).rearrange("p (b h t) -> p b h t", b=2, h=2)
        zs = A.bf16(2 * 512).rearrange("p (b t) -> p b t", b=2)
        wbf = A.bf16(2 * 8 * 512).rearrange("p (b j n) -> p b j n", b=2, j=8)
        wst = A.f32(2 * 8 * 128).rearrange("p (b j n) -> p b j n", b=2, j=8)
        Frows = A.bf16(S)
        sc_off = A.off
        qraw = A.f32(1024).rearrange("p (b t) -> p b t", b=2)
        sq = A.bf16(1024).rearrange("p (b t) -> p b t", b=2)
        lnv = A.f32(1024).rearrange("p (b t) -> p b t", b=2)
        rstd = A.f32(1024).rearrange("p (b t) -> p b t", b=2)
        PT = A.bf16(4 * 512).rearrange("p (b t) -> p b t", b=4)
        rden = A.f32(1024).rearrange("p (b t) -> p b t", b=2)
        tmpz = A.f32(1024).rearrange("p (b t) -> p b t", b=2)
        th = A.bf16(1024).rearrange("p (b t) -> p b t", b=2)
        endA = A.off
        A.off = kv_off
        xbA = A.f32(3 * 1024).rearrange("p (b t) -> p b t", b=3)
        xnA = A.f32(2 * 1024).rearrange("p (b t) -> p b t", b=2)
        junkA = A.bf16(1024)
        assert A.off <= rest_off
        A.off = rest_off
        adast = A.f32(3 * 8 * 512).rearrange("p (b j n) -> p b j n", b=3, j=8)
        adarow = A.f32(3072)
        assert A.off <= endA
        A.off = kv_off
        e_all = A.f32(S)
        F_all = A.f32(S)
        assert A.off <= rest_off
        A.off = sc_off
        HMLa = A.bf16(3 * S).rearrange("p (r t) -> p r t", r=3)
        ones512 = A.f32(512)
        assert A.off <= endA
        A.off = kv_off
        obT = A.bf16(4 * S).rearrange("p (c t) -> p c t", c=4)
        assert A.off == rest_off
        A.off = sc_off
        gcs = A.bf16(1024).rearrange("p (b t) -> p b t", b=2)
        gcu = A.bf16(2 * 516).rearrange("p (b t) -> p b t", b=2)
        thz = A.bf16(1024).rearrange("p (b t) -> p b t", b=2)
        a1 = A.f32(1024).rearrange("p (b t) -> p b t", b=2)
        a2 = A.f32(1024).rearrange("p (b t) -> p b t", b=2)
        assert A.off <= endA
        A.off = base_off
        Wg = A.bf16(8 * 2048).rearrange("p (j n) -> p j n", j=8)
        WA = A.bf16(4 * 1024).rearrange("p (c n) -> p c n", c=4)
        WB = A.bf16(4 * 1024).rearrange("p (c n) -> p c n", c=4)
        WO = A.bf16(8 * 1024).rearrange("p (j n) -> p j n", j=8)
        assert A.off == kv_off
        A.off = rest_off
        hTt = A.bf16(2 * 8 * 512).rearrange("p (b j t) -> p b j t", b=2, j=8)
        mT = A.bf16(8 * 512).rearrange("p (j t) -> p j t", j=8)
        xbB = A.f32(2 * 1024).rearrange("p (b t) -> p b t", b=2)
        xnB = A.f32(2 * 1024).rearrange("p (b t) -> p b t", b=2)
        junkB = A.bf16(1024)
        thg = A.bf16(2 * 1024).rearrange("p (b t) -> p b t", b=2)
        t12 = A.f32(2 * 1024).rearrange("p (b t) -> p b t", b=2)
        gate_bc = A.f32(1024)
        Dj = A.f32(128)
        xr_off = A.off
        xr = A.f32(3 * 1024).rearrange("p (b t) -> p b t", b=3)
        assert A.off <= ARENA_W, A.off
        A.off = xr_off
        wstB = A.f32(3 * 1024).rearrange("p (b n) -> p b n", b=3)

        def dma(out, in_, reads=(), writes=()):
            return P.op("sp", lambda e: e.dma_start(out=out, in_=in_), reads=reads, writes=writes, dma=True)

        def barrier():
            mk = {
                "pe": lambda e: e.nop(),
                "act": lambda e: e.nop(),
                "dve": lambda e: e.nop(),
                "pool": lambda e: e.nop(),
                "sp": lambda e: e.nop(),
            }
            P.barrier(mk)

        P.op("pool", lambda e: e.memset(ident_f, 1.0), writes=["ident_f"])
        P.op("pool", lambda e: e.affine_select(out=ident_f, in_=ident_f, pattern=[[-1, 128]], base=0,
                                               channel_multiplier=1, compare_op=ALU.is_equal, fill=0.0),
             reads=["ident_f"], writes=["ident_f"])
        P.op("pool", lambda e: e.tensor_copy(out=ident_b, in_=ident_f), reads=["ident_f"], writes=["ident_b"])
        P.op("pool", lambda e: e.memset(ones_f, 1.0), writes=["ones_f"])
        P.op("pool", lambda e: e.memset(maskL, -30000.0), writes=["maskL"])
        P.op("pool", lambda e: e.affine_select(out=maskL, in_=maskL, pattern=[[1, 128]], base=0,
                                               channel_multiplier=-1, compare_op=ALU.is_gt, fill=0.0),
             reads=["maskL"], writes=["maskL"])
        P.op("pool", lambda e: e.memset(bdiag, 0.0), writes=["bdiag"])
        P.op("pool", lambda e: e.memset(bdiag[0:64, 0:64], 1.0), reads=["bdiag"], writes=["bdiag"])
        P.op("pool", lambda e: e.memset(bdiag[64:128, 64:128], 1.0), reads=["bdiag"], writes=["bdiag"])
        dma(cT_s, cT2[:, :], writes=["cT"])
        dma(bada_s, b_adaT[:, :], writes=["bada"])
        dma(ng_s, norm_gT[:, :], writes=["ng"])
        dma(gqk_s, gqk[:, :], writes=["gqk"])
        dma(cw_s, conv_wT[:, :], writes=["cw"])
        dma(bf_s, b_f[:, :], writes=["bf"])
        dma(wf_s, w_in_v[:, :, COL_F:COL_F + 8], writes=["wf_s"])
        P.op("dve", lambda e: e.tensor_copy(out=wf_b, in_=wf_s), reads=["wf_s"], writes=["wf_b"])
        P.op("dve", lambda e: e.tensor_scalar(out=gq8, in0=gqk_s[:, 0:1], scalar1=0.125, scalar2=None, op0=ALU.mult),
             reads=["gqk"], writes=["gq8"])
        P.op("dve", lambda e: e.tensor_scalar(out=nbf, in0=bf_s, scalar1=-1.0, scalar2=None, op0=ALU.mult),
             reads=["bf"], writes=["nbf"])
        for c in range(12):
            P.op("dve", lambda e, c=c: e.tensor_scalar(out=convdiag[:, c, :], in0=ident_f, scalar1=cw_s[:, c:c + 1],
                                                       scalar2=None, op0=ALU.mult),
                 reads=["ident_f", "cw"], writes=[("convdiag", c)])

        for m4 in range(6):
            b = m4 % 3
            dma(adast[:, b], w_ada_v[:, :, m4 * 512:(m4 + 1) * 512], writes=[("adast", b)])
            for j in range(8):
                P.op("pe", lambda e, m4=m4, j=j, b=b: e.matmul(ps[0:1, m4, :], lhsT=cT_s[:, 2 * j:2 * j + 1], rhs=adast[:, b, j, :],
                                                               start=(j == 0), stop=(j == 7)),
                     reads=[("adast", b), "cT"], writes=[("ps", m4)])
            P.op("dve", lambda e, m4=m4: e.tensor_copy(out=adarow[0:1, m4 * 512:(m4 + 1) * 512], in_=ps[0:1, m4, :]),
                 reads=[("ps", m4)], writes=[("adarow", m4)])
        for m in range(24):
            P.op("pe", lambda e, m=m: e.matmul(ps[:, 7, 2 * m:2 * m + 2], lhsT=adarow[0:1, m * 128:(m + 1) * 128], rhs=ones_f[0:1, 0:2],
                                               start=True, stop=True),
                 reads=[("adarow", m // 4), "ones_f"], writes=[("ps", 7)])
        P.op("dve", lambda e: e.tensor_tensor(out=ada, in0=ps[:, 7, 0:48].rearrange("p (m t) -> p m t", t=2)[:, :, 0],
                                              in1=bada_s, op=ALU.add),
             reads=[("ps", 7), "bada"], writes=["ada"])
        P.op("dve", lambda e: e.scalar_tensor_tensor(out=gp, in0=scale, scalar=1.0, in1=ng_s, op0=ALU.add, op1=ALU.mult),
             reads=["ada", "ng"], writes=["gp"])

        xn_cnt = [0]

        def xnorm_sub(s, xb, nxb, xn, junk, dest, dest_res, psb, part=3, kk=None):
            if part & 1:
                k = xn_cnt[0]
            else:
                k = kk
            if part == 1:
                xn_cnt[0] += 1
            return _xnorm_sub(s, xb, nxb, xn, junk, dest, dest_res, psb, part, k)

        def _xnorm_sub(s, xb, nxb, xn, junk, dest, dest_res, psb, part, k):
            if part == 3:
                xn_cnt[0] += 1
            bx, bn, m = k % nxb, k % 2, k % 4
            if part & 1:
                _xnorm_front(s, xb, xn, junk, bx, bn, m)
            if part & 2:
                _xnorm_back(xn, dest, dest_res, psb, bn)
            return k

        def _xnorm_front(s, xb, xn, junk, bx, bn, m, reuse=False):
            dma(xb[:, bx], x[s * 128:(s + 1) * 128, :], writes=[("xb", bx)])
            if reuse:
                P.op("act", lambda e: e.activation(out=xn[:, bn], in_=xb[:, bx], func=AF.Copy, scale=rs_all[:, s:s + 1]),
                     reads=[("xb", bx)], writes=[("xn", bn)])
                return
            P.op("act", lambda e: e.activation(out=junk, in_=xb[:, bx], func=AF.Square, accum_out=ss1[:, m:m + 1]),
                 reads=[("xb", bx)], writes=["junk", ("ss1", m)])
            P.op("act", lambda e: e.activation(out=lnv1[:, m:m + 1], in_=ss1[:, m:m + 1], func=AF.Ln, bias=EPS, scale=1.0 / D),
                 reads=[("ss1", m)], writes=[("lnv1", m)])
            P.op("act", lambda e: e.activation(out=rs_all[:, s:s + 1], in_=lnv1[:, m:m + 1], func=AF.Exp, scale=-0.5),
                 reads=[("lnv1", m)], writes=[("rs_all", s)])
            P.op("act", lambda e: e.activation(out=xn[:, bn], in_=xb[:, bx], func=AF.Copy, scale=rs_all[:, s:s + 1]),
                 reads=[("xb", bx), ("rs_all", s)], writes=[("xn", bn)])

        def _xnorm_back(xn, dest, dest_res, psb, bn):
            psx = ps[:, psb:psb + 2, :].rearrange("p b (jj t) -> p (b jj) t", t=128)
            for j in range(8):
                P.op("pe", lambda e, j=j: e.transpose(out=psx[:, j, :], in_=xn[:, bn, j * 128:(j + 1) * 128], identity=ident_f),
                     reads=[("xn", bn), "ident_f"], writes=[("ps", psb + j // 4)])
            for j in range(8):
                P.op("dve", lambda e, j=j: e.tensor_scalar(out=dest(j), in0=psx[:, j, :], scalar1=gp[:, j:j + 1],
                                                           scalar2=shift[:, j:j + 1], op0=ALU.mult, op1=ALU.add),
                     reads=[("ps", psb + j // 4), "gp", "ada"], writes=[dest_res], disjoint=True)

        def f_proj(i):
                pb = 5 + (i % 2)
                for j in range(8):
                    P.op("pe", lambda e, i=i, j=j, pb=pb: e.matmul(ps[0:8, pb, :], lhsT=wf_b[:, j, :], rhs=hT[:, j, i * 512:(i + 1) * 512],
                                                                   start=(j == 0), stop=(j == 7)),
                         reads=["wf_b", ("hT", i)], writes=[("ps", pb)])
                P.op("act", lambda e, i=i, pb=pb: e.activation(out=e_all[0:8, i * 512:(i + 1) * 512], in_=ps[0:8, pb, :], func=AF.Exp,
                                                               bias=nbf, scale=-1.0),
                     reads=[("ps", pb), "nbf"], writes=[("e_all", i)])

        for t4 in range(NQT):
            for ssb in range(4):
                s = 4 * t4 + ssb
                k = xn_cnt[0]
                xn_cnt[0] += 1
                bx, bn, m = k % 3, k % 2, k % 4
                _xnorm_front(s, xbA, xnA, junkA, bx, bn, m)
                for j in range(8):
                    P.op("pe", lambda e, j=j: e.transpose(out=ps[:, j, ssb * 128:(ssb + 1) * 128], in_=xnA[:, bn, j * 128:(j + 1) * 128],
                                                          identity=ident_f),
                         reads=[("xn", bn), "ident_f"], writes=[("ps", j)])
            for j in range(8):
                P.op("dve", lambda e, j=j: e.tensor_scalar(out=hT[:, j, t4 * 512:(t4 + 1) * 512], in0=ps[:, j, :], scalar1=gp[:, j:j + 1],
                                                           scalar2=shift[:, j:j + 1], op0=ALU.mult, op1=ALU.add),
                     reads=[("ps", j), "gp", "ada"], writes=[("hT", t4)], disjoint=True)
        barrier()
        for i in range(NQT):
            f_proj(i)

        P.op("pool", lambda e: e.memset(ones512[0:8, :], 1.0), writes=["ones512"])
        P.op("act", lambda e: e.activation(out=e_all[0:8, :], in_=e_all[0:8, :], func=AF.Ln, bias=1.0, scale=1.0),
             reads=[("e_all", i) for i in range(NQT)], writes=["sp_all"])
        for i in range(NQT):
            init = 0.0 if i == 0 else F_all[0:8, i * 512 - 1:i * 512]
            P.op("dve", lambda e, i=i, init=init: e.tensor_tensor_scan(out=F_all[0:8, i * 512:(i + 1) * 512], data0=ones512[0:8, :],
                                                                       data1=e_all[0:8, i * 512:(i + 1) * 512], initial=init,
                                                                       op0=ALU.mult, op1=ALU.subtract),
                 reads=["sp_all", "ones512", "F_all"], writes=["F_all"])
        P.op("dve", lambda e: e.tensor_copy(out=HMLa[0:8, 0], in_=F_all[0:8, :]), reads=["F_all"], writes=["H0"])
        P.op("dve", lambda e: e.tensor_tensor(out=e_all[0:8, :], in0=F_all[0:8, :], in1=HMLa[0:8, 0], op=ALU.subtract),
             reads=["F_all", "H0", "sp_all"], writes=["R1a"])
        P.op("dve", lambda e: e.tensor_copy(out=HMLa[0:8, 1], in_=e_all[0:8, :]), reads=["R1a"], writes=["H1"])
        P.op("dve", lambda e: e.tensor_tensor(out=F_all[0:8, :], in0=e_all[0:8, :], in1=HMLa[0:8, 1], op=ALU.subtract),
             reads=["R1a", "H1", "F_all"], writes=["R2a"])
        P.op("dve", lambda e: e.tensor_copy(out=HMLa[0:8, 2], in_=F_all[0:8, :]), reads=["R2a"], writes=["H2"])
        for r in range(3):
            dma(Frows[8 * r:8 * r + 8, :], HMLa[0:8, r], reads=["H%d" % r], writes=[("Frows", i) for i in range(NQT)])
        if "Frows" in debug:
            dbg["Frows"] = dram("dbg_Frows", [24, S], kind="ExternalOutput", dtype=BF16)
            dma(dbg["Frows"][:, :], Frows[0:24, :], reads=[("Frows", i) for i in range(NQT)])
        if "hT" in debug:
            dbg["hT"] = dram("dbg_hT", [128, 8, S], kind="ExternalOutput", dtype=BF16)
            dma(dbg["hT"][:, :, :], hT, reads=[("hT", i) for i in range(NQT)])
        barrier()

        P.op("pool", lambda e: e.memset(KA[64:70, :, :], 1.0), writes=["KAaug"])
        P.op("pool", lambda e: e.memset(QA[64:70, :, :, :], -1.0), writes=["QAaug"])
        P.op("pool", lambda e: e.memset(VA[:, :, 0, 64:128], 1.0), writes=["VAones"])
        P.op("pool", lambda e: e.memset(VA[:, :, 1, 0:64], 1.0), writes=["VAones"])
        from collections import deque

        def load_pair_weights(cols, wb, ce="pool"):
            for pi, c0 in enumerate(cols):
                b = load_pair_weights.n % 2
                load_pair_weights.n += 1
                dma(wst[:, b], w_in_v[:, :, c0:c0 + 128], writes=[("wst", b)])
                if ce == "act":
                    P.op("act", lambda e, b=b, pi=pi: e.activation(out=wbf[:, wb, :, pi * 128:(pi + 1) * 128], in_=wst[:, b], func=AF.Copy),
                         reads=[("wst", b)], writes=[("wbf", wb, pi)])
                else:
                    P.op("pool", lambda e, b=b, pi=pi: e.tensor_copy(out=wbf[:, wb, :, pi * 128:(pi + 1) * 128], in_=wst[:, b]),
                         reads=[("wst", b)], writes=[("wbf", wb, pi)])
        load_pair_weights.n = 0

        def proj_fm(wb, pi, i, pb, js=range(8)):
            for j in js:
                P.op("pe", lambda e, j=j: e.matmul(ps[:, pb, :], lhsT=wbf[:, wb, j, pi * 128:(pi + 1) * 128],
                                                   rhs=hT[:, j, i * 512:(i + 1) * 512], start=(j == 0), stop=(j == 7)),
                     reads=[("wbf", wb, pi), ("hT", i)], writes=[("ps", pb)])

        def chain_tasks(wb, pi, i, pb, b, gcol, dest0, dest1, res0, res1):
            t = []
            t.append(lambda: proj_fm(wb, pi, i, pb, range(0, 4)))
            t.append(lambda: proj_fm(wb, pi, i, pb, range(4, 8)))

            def t_copy():
                P.op("dve", lambda e: e.tensor_copy(out=qraw[:, b], in_=ps[:, pb, :]), reads=[("ps", pb)], writes=[("qraw", b)])
                P.op("pool", lambda e: e.tensor_tensor(out=sq[:, b], in0=qraw[:, b], in1=qraw[:, b], op=ALU.mult),
                     reads=[("qraw", b)], writes=[("sq", b)])
            t.append(t_copy)

            def t_ss():
                P.op("pe", lambda e: e.matmul(ps[:, pb, :], lhsT=bdiag, rhs=sq[:, b], start=True, stop=True),
                     reads=[("sq", b), "bdiag"], writes=[("ps", pb)])

            def t_act():
                P.op("act", lambda e: e.activation(out=lnv[:, b], in_=ps[:, pb, :], func=AF.Ln, bias=EPS, scale=1.0 / 64),
                     reads=[("ps", pb)], writes=[("lnv", b)])
                P.op("act", lambda e: e.activation(out=rstd[:, b], in_=lnv[:, b], func=AF.Exp, scale=-0.5),
                     reads=[("lnv", b)], writes=[("rstd", b)])

            def t_out():
                P.op("dve", lambda e: e.scalar_tensor_tensor(out=dest0, in0=qraw[0:64, b], scalar=gcol[0:64], in1=rstd[0:64, b],
                                                             op0=ALU.mult, op1=ALU.mult),
                     reads=[("qraw", b), ("rstd", b), "gq8", "gqk"], writes=[res0])
                P.op("dve", lambda e: e.scalar_tensor_tensor(out=dest1, in0=qraw[64:128, b], scalar=gcol[64:128], in1=rstd[64:128, b],
                                                             op0=ALU.mult, op1=ALU.mult),
                     reads=[("qraw", b), ("rstd", b), "gq8", "gqk"], writes=[res1])
            return t, t_ss, t_act, t_out

        def prep_tasks(g, wb, i, late_writes=False):
            qb = i % 2
            tasks = []

            def t_dma():
                for hh in range(2):
                    for r in range(3):
                        dma(QA[64 + r:65 + r, qb, hh, :], Frows[8 * r + 2 * g + hh:8 * r + 2 * g + hh + 1, i * 512:(i + 1) * 512],
                            reads=[("Frows", i), "QAaug"], writes=[("QAF", qb, hh)])
            tasks.append(t_dma)
            kt, k_ss, k_act, k_out = chain_tasks(wb, 1, i, 6, 0, gk, KA[0:64, 0, i * 512:(i + 1) * 512], KA[0:64, 1, i * 512:(i + 1) * 512],
                                                 ("KA", 0, i), ("KA", 1, i))
            qt, q_ss, q_act, q_out = chain_tasks(wb, 0, i, 7, 1, gq8, QA[0:64, qb, 0, :], QA[0:64, qb, 1, :], ("QA", qb, 0), ("QA", qb, 1))
            tasks += kt
            tasks += qt
            for ssb in range(4):
                def t_v(ssb=ssb):
                    s = 4 * i + ssb
                    for j in range(8):
                        P.op("pe", lambda e, j=j: e.matmul(ps[:, 6, ssb * 128:(ssb + 1) * 128], lhsT=hT[:, j, s * 128:(s + 1) * 128],
                                                           rhs=wbf[:, wb, j, 256:384], start=(j == 0), stop=(j == 7)),
                             reads=[("wbf", wb, 2), ("hT", i)], writes=[("ps", 6)])
                t_v.is_v = True
                tasks.append(t_v)

            def t_vcopy():
                P.op("dve", lambda e: e.tensor_copy(out=VA[:, 4 * i:4 * i + 4, 0, 0:64],
                                                    in_=ps[:, 6, :].rearrange("p (s h f) -> p s h f", s=4, h=2)[:, :, 0, :]),
                     reads=[("ps", 6)], writes=[("VA", i)], disjoint=True)
                P.op("dve", lambda e: e.tensor_copy(out=VA[:, 4 * i:4 * i + 4, 1, 64:128],
                                                    in_=ps[:, 6, :].rearrange("p (s h f) -> p s h f", s=4, h=2)[:, :, 1, :]),
                     reads=[("ps", 6)], writes=[("VA", i)], disjoint=True)
            t_vcopy.is_v = True
            tasks.append(t_vcopy)
            tasks.append(k_ss)
            tasks.append(q_ss)
            tasks.append(k_act)
            tasks.append(q_act)
            tasks.append(q_out)
            tasks.append(k_out)
            if late_writes:
                vt = [t for t in tasks if getattr(t, "is_v", False)]
                for t in vt:
                    tasks.remove(t)
                tasks += vt
            return tasks

        for g in range(n_pairs):
            wb = g % 2
            if g == 0:
                load_pair_weights([COL_Q + g * 128, COL_K + g * 128, COL_V + g * 128], wb)
            for hh in range(2):
                for r in range(3):
                    dma(KA[67 + r:68 + r, hh, :], Frows[8 * r + 2 * g + hh:8 * r + 2 * g + hh + 1, :],
                        reads=[("Frows", i) for i in range(NQT)] + ["KAaug"], writes=[("KAF", hh)])

            units = []
            for i in range(NQT):
                for hh in range(2):
                    for kb in range(4 * i + 4):
                        units.append((i, hh, kb))

            def qk(n, g=g):
                i, hh, kb = units[n]
                sb = n % 4
                qb = i % 2
                jd = kb - 4 * i
                c0 = max(jd, 0) * 128
                P.op("pe", lambda e: e.matmul(ps[:, sb, c0:512], lhsT=KA[0:70, hh, kb * 128:(kb + 1) * 128],
                                              rhs=QA[0:70, qb, hh, c0:512], start=True, stop=(jd < 0)),
                     reads=[("KA", hh, kb // 4), ("KAF", hh), "KAaug", ("QA", qb, hh), ("QAF", qb, hh), "QAaug"],
                     writes=[("ps", sb)])
                if jd >= 0:
                    P.op("pe", lambda e: e.matmul(ps[:, sb, c0:c0 + 128], lhsT=maskL, rhs=ident_b, start=False, stop=True),
                         reads=["maskL", "ident_b"], writes=[("ps", sb)])

            def exp_pv(n, g=g):
                i, hh, kb = units[n]
                sb, pbuf = n % 4, n % 4
                ob = 4 + ((2 * i + hh) % 2)
                jd = kb - 4 * i
                c0 = max(jd, 0) * 128
                last = (kb == 4 * i + 3)
                P.op("act", lambda e: e.activation(out=PT[:, pbuf, c0:512], in_=ps[:, sb, c0:512], func=AF.Exp),
                     reads=[("ps", sb)], writes=[("PT", pbuf)])
                P.op("pe", lambda e: e.matmul(ps[:, ob, c0:512], lhsT=VA[:, kb, hh, :], rhs=PT[:, pbuf, c0:512],
                                              start=(kb == 0), stop=last),
                     reads=[("VA", kb // 4), "VAones", ("PT", pbuf)], writes=[("ps", ob)])
                if last:
                    rb = (2 * i + hh) % 2
                    qb = i % 2
                    lo, hi = (0, 64) if hh == 0 else (64, 128)
                    dlo, dhi = (64, 128) if hh == 0 else (0, 64)
                    P.op("dve", lambda e: e.reciprocal(out=rden[lo:hi, rb], in_=ps[dlo:dhi, ob, :]),
                         reads=[("ps", ob)], writes=[("rden", rb)])
                    P.op("dve", lambda e: e.tensor_tensor(out=oaT[lo:hi, g, i * 512:(i + 1) * 512], in0=ps[lo:hi, ob, :],
                                                          in1=rden[lo:hi, rb], op=ALU.mult),
                         reads=[("ps", ob), ("rden", rb)], writes=[("oaT", g, i, hh)])

            side = deque()
            if g == 0:
                for t in prep_tasks(g, wb, 0):
                    t()
            first_unit_of_tile = {}
            for n, (i, hh, kb) in enumerate(units):
                if hh == 0 and kb == 0:
                    first_unit_of_tile[i] = n
            qk(0)
            qk(1)
            qk(2)
            for n in range(len(units)):
                i, hh, kb = units[n]
                if hh == 0 and kb == 0:
                    if i + 1 < NQT:
                        side.extend(prep_tasks(g, wb, i + 1))
                    elif g + 1 < n_pairs:
                        g1 = g + 1
                        side.append(lambda g1=g1: load_pair_weights([COL_Q + g1 * 128, COL_K + g1 * 128, COL_V + g1 * 128], g1 % 2))
                        side.extend(prep_tasks(g1, g1 % 2, 0, late_writes=True))
                    elif do_conv:
                        side.append(lambda: load_pair_weights([COL_GB, COL_GC, COL_U, COL_ZB], (n_pairs - 1 + 1) % 2))
                if n + 3 < len(units):
                    i2 = units[n + 3][0]
                    if i2 != i and first_unit_of_tile.get(i2) == n + 3:
                        while side:
                            side.popleft()()
                    qk(n + 3)
                exp_pv(n)
                ntile = 2 * (4 * i + 4)
                per = max(1, -(-28 // max(ntile - 4, 1)))
                for _ in range(per):
                    if side:
                        side.popleft()()
            while side:
                side.popleft()()

        if "oaT" in debug:
            dbg["oaT"] = dram("dbg_oaT", [128, 4, S], kind="ExternalOutput", dtype=BF16)
            dma(dbg["oaT"][:, :, :], oaT, reads=[("oaT", g, i, hh) for g in range(n_pairs) for i in range(NQT) for hh in range(2)])
        if "KA" in debug:
            dbg["KA"] = dram("dbg_KA", [70, 2, S], kind="ExternalOutput", dtype=BF16)
            dma(dbg["KA"][:, :, :], KA[0:70], reads=[("KA", hh, i) for hh in range(2) for i in range(NQT)] + [("KAF", 0), ("KAF", 1), "KAaug"])
        barrier()

        if do_conv:
            sa = n_pairs % 2
            sb = 1 - sa
            if n_pairs == 0:
                load_pair_weights([COL_GB, COL_GC, COL_U, COL_ZB], sa)

            def conv_w(c, slot):
                load_pair_weights([COL_GB + c * 128, COL_GC + c * 128, COL_U + c * 128, COL_ZB + c * 128], slot, ce="act")

            def zgate(wbz):
                nz = 0
                for gg in range(n_pairs):
                    for i in range(NQT):
                        b = nz % 2
                        pbz = 5 + (nz % 2)
                        nz += 1
                        proj_fm(wbz, gg, i, pbz)
                        P.op("act", lambda e, b=b, pbz=pbz: e.activation(out=thz[:, b], in_=ps[:, pbz, :], func=AF.Tanh, scale=0.5),
                             reads=[("ps", pbz)], writes=[("thz", b)])
                        P.op("dve", lambda e, b=b, pbz=pbz: e.scalar_tensor_tensor(out=gcs[:, b], in0=thz[:, b], scalar=1.0, in1=ps[:, pbz, :],
                                                                                  op0=ALU.add, op1=ALU.mult),
                             reads=[("thz", b), ("ps", pbz)], writes=[("gcs", b)])
                        P.op("pool", lambda e, b=b, gg=gg, i=i: e.tensor_tensor(out=oaT[:, gg, i * 512:(i + 1) * 512], in0=oaT[:, gg, i * 512:(i + 1) * 512],
                                                                              in1=gcs[:, b], op=ALU.mult),
                             reads=[("gcs", b)], writes=[("oaTz", gg, i)])

            def conv_chunk(c, wb):
                for i in range(NQT):
                    par = i % 2
                    b = i % 2
                    proj_fm(wb, 1, i, 0)
                    proj_fm(wb, 2, i, 1)
                    P.op("act", lambda e, b=b: e.activation(out=gcs[:, b], in_=ps[:, 0, :], func=AF.Copy),
                         reads=[("ps", 0)], writes=[("gcs", b)])
                    if i == 0:
                        P.op("pool", lambda e, par=par: e.memset(gcu[:, par, 0:2], 0.0), writes=[("gcu", par)])
                    else:
                        P.op("pool", lambda e, par=par: e.tensor_copy(out=gcu[:, par, 0:2], in_=gcu[:, 1 - par, 512:514]),
                             reads=[("gcu", 1 - par)], writes=[("gcu", par)])
                    P.op("dve", lambda e, b=b, par=par: e.tensor_tensor(out=gcu[:, par, 2:514], in0=ps[:, 1, :], in1=gcs[:, b], op=ALU.mult),
                         reads=[("ps", 1), ("gcs", b), ("gcu", par)], writes=[("gcu", par)])
                    proj_fm(wb, 0, i, 2)
                    proj_fm(wb, 3, i, 3)
                    for k in range(3):
                        P.op("pe", lambda e, k=k, par=par, c=c: e.matmul(ps[:, 4, :], lhsT=convdiag[:, c * 3 + k, :], rhs=gcu[:, par, k:k + 512],
                                                                         start=(k == 0), stop=(k == 2)),
                             reads=[("convdiag", c * 3 + k), ("gcu", par)], writes=[("ps", 4)])
                    P.op("act", lambda e, b=b: e.activation(out=thz[:, b], in_=ps[:, 3, :], func=AF.Tanh, scale=0.5),
                         reads=[("ps", 3)], writes=[("thz", b)])
                    P.op("dve", lambda e, b=b: e.scalar_tensor_tensor(out=a1[:, b], in0=thz[:, b], scalar=1.0, in1=ps[:, 3, :],
                                                                     op0=ALU.add, op1=ALU.mult),
                         reads=[("thz", b), ("ps", 3)], writes=[("a1", b)])
                    P.op("dve", lambda e, b=b: e.tensor_tensor(out=a2[:, b], in0=ps[:, 2, :], in1=a1[:, b], op=ALU.mult),
                         reads=[("ps", 2), ("a1", b)], writes=[("a2", b)])
                    P.op("dve", lambda e, b=b, c=c, i=i: e.tensor_tensor(out=obT[:, c, i * 512:(i + 1) * 512], in0=ps[:, 4, :], in1=a2[:, b], op=ALU.mult),
                         reads=[("ps", 4), ("a2", b)], writes=[("obT", c, i)])

            load_pair_weights([COL_ZA + gg * 128 for gg in range(4)], sb, ce="act")
            conv_chunk(0, sa)
            conv_w(1, sa)
            zgate(sb)
            conv_w(2, sb)
            conv_chunk(1, sa)
            conv_w(3, sa)
            conv_chunk(2, sb)
            conv_chunk(3, sa)
            if "obT" in debug:
                dbg["obT"] = dram("dbg_obT", [128, 4, S], kind="ExternalOutput", dtype=BF16)
                dma(dbg["obT"][:, :, :], obT, reads=[("obT", c, i) for c in range(4) for i in range(NQT)])
        barrier()

        out_dmas = []
        if do_b:
            for j in range(8):
                P.op("dve", lambda e, j=j: e.tensor_scalar(out=Dj, in0=ident_f, scalar1=gate[:, j:j + 1], scalar2=0.25,
                                                           op0=ALU.mult, op1=ALU.mult),
                     reads=["ident_f", "ada"], writes=["Dj"])
                P.op("pe", lambda e, j=j: e.matmul(ps[:, 2 + j // 4, (j % 4) * 128:(j % 4 + 1) * 128], lhsT=ones_f, rhs=Dj, start=True, stop=True),
                     reads=["ones_f", "Dj"], writes=[("ps", 2 + j // 4)])
            for hb in range(2):
                P.op("dve", lambda e, hb=hb: e.tensor_copy(out=gate_bc[:, hb * 512:(hb + 1) * 512], in_=ps[:, 2 + hb, :]),
                     reads=[("ps", 2 + hb)], writes=["gate_bc"])

            def xn_tile(it):
                hb = it % 2
                for sub in range(4):
                    s_ = it * 4 + sub
                    kk0 = xn_cnt[0]
                    xn_cnt[0] += 1
                    _xnorm_front(s_, xbB, xnB, junkB, kk0 % 2, kk0 % 2, kk0 % 4, reuse=True)
                    _xnorm_back(xnB, lambda j, sub=sub: hTt[:, hb, j, sub * 128:(sub + 1) * 128], ("hTt", hb), 0, kk0 % 2)

            xn_tile(0)
            nld = [0]
            cast_engs = ["dve", "act"]

            def load_w(src_ap, shape3, dst, wres, mul_gate=None):
                b = nld[0] % 3
                ce = cast_engs[nld[0] % 2]
                nld[0] += 1
                view = wstB[:, b].rearrange("p (j n) -> p j n", j=shape3)
                dma(view, src_ap, writes=[("wstB", b)])
                if mul_gate is not None:
                    P.op("pool" if ce == "act" else ce, lambda e: e.tensor_tensor(out=dst, in0=view, in1=mul_gate, op=ALU.mult),
                         reads=[("wstB", b), "gate_bc"], writes=[wres])
                elif ce == "act":
                    P.op("act", lambda e: e.activation(out=dst, in_=view, func=AF.Copy), reads=[("wstB", b)], writes=[wres])
                else:
                    P.op(ce, lambda e: e.tensor_copy(out=dst, in_=view), reads=[("wstB", b)], writes=[wres])

            w_a_v = w_a.rearrange("(c p) n -> p c n", p=128)
            w_b_v = w_b.rearrange("(c p) n -> p c n", p=128)
            w_o_v = w_o.rearrange("(j p) n -> p j n", p=128)
            for fc in range(8):
                for cch in (fc, 8 + fc):
                    c0 = COL_GA + cch * 128
                    load_w(w_in_v[:, :, c0:c0 + 128], 8, Wg[:, :, cch * 128:(cch + 1) * 128], ("Wg", cch))
                if fc % 2 == 0:
                    q4 = fc // 2
                    load_w(w_a_v[:, :, q4 * 256:(q4 + 1) * 256], 4, WA[:, :, q4 * 256:(q4 + 1) * 256], ("WA", q4))
                    load_w(w_b_v[:, :, q4 * 256:(q4 + 1) * 256], 4, WB[:, :, q4 * 256:(q4 + 1) * 256], ("WB", q4))
            for q8 in range(8):
                load_w(w_o_v[:, :, q8 * 128:(q8 + 1) * 128], 8, WO[:, :, q8 * 128:(q8 + 1) * 128], ("WO", q8),
                       mul_gate=gate_bc[:, q8 * 128:(q8 + 1) * 128].unsqueeze(1).to_broadcast([128, 8, 128]))

            xk = {}
            for it in range(NQT):
                hb = it % 2
                t0 = it * 512
                def gates_mm(hb_, fc_):
                    g0_ = 2 if fc_ % 2 == 0 else 4
                    for half in range(2):
                        for j in range(8):
                            P.op("pe", lambda e, j=j, half=half: e.matmul(
                                ps[:, g0_ + half, :], lhsT=Wg[:, j, half * 1024 + fc_ * 128:half * 1024 + (fc_ + 1) * 128],
                                rhs=hTt[:, hb_, j, :], start=(j == 0), stop=(j == 7)),
                                reads=[("hTt", hb_), ("Wg", half * 8 + fc_)], writes=[("ps", g0_ + half)])

                for fc in range(8):
                    g0 = 2 if fc % 2 == 0 else 4
                    tb_ = fc % 2
                    if not (fc == 0 and it > 0):
                        gates_mm(hb, fc)
                    for c in range(4):
                        P.op("pe", lambda e, c=c: e.matmul(ps[:, 6, :], lhsT=WA[:, c, fc * 128:(fc + 1) * 128],
                                                           rhs=oaT[:, c, t0:t0 + 512], start=(c == 0), stop=(c == 3)),
                             reads=[("WA", fc // 2)], writes=[("ps", 6)])
                    for c in range(4):
                        P.op("pe", lambda e, c=c: e.matmul(ps[:, 7, :], lhsT=WB[:, c, fc * 128:(fc + 1) * 128],
                                                           rhs=obT[:, c, t0:t0 + 512], start=(c == 0), stop=(c == 3)),
                             reads=[("WB", fc // 2)], writes=[("ps", 7)])
                    P.op("act", lambda e: e.activation(out=thg[:, tb_].rearrange("p (a t) -> p a t", a=2), in_=ps[:, g0:g0 + 2, :],
                                                       func=AF.Tanh, scale=0.5),
                         reads=[("ps", g0), ("ps", g0 + 1)], writes=[("thg", tb_)])
                    P.op("dve", lambda e: e.scalar_tensor_tensor(out=t12[:, tb_].rearrange("p (a t) -> p a t", a=2),
                                                                 in0=thg[:, tb_].rearrange("p (a t) -> p a t", a=2), scalar=1.0,
                                                                 in1=ps[:, 6:8, :], op0=ALU.add, op1=ALU.mult),
                         reads=[("thg", tb_), ("ps", 6), ("ps", 7)], writes=[("t12", tb_)])
                    P.op("pool", lambda e: e.tensor_tensor(out=mT[:, fc, :], in0=t12[:, tb_, 0:512], in1=t12[:, tb_, 512:1024], op=ALU.add),
                         reads=[("t12", tb_)], writes=[("mT", fc)])
                    if it + 1 < NQT:
                        hb1 = (it + 1) % 2
                        back_at = {2: [0], 3: [1], 4: [2], 5: [3]}
                        front_at = {0: 0, 1: 1, 2: 2, 3: 3}
                        for sub in back_at.get(fc, []):
                            _xnorm_back(xnB, lambda j, sub=sub: hTt[:, hb1, j, sub * 128:(sub + 1) * 128], ("hTt", hb1), 0, xk[sub] % 2)
                        if fc in front_at:
                            sub = front_at[fc]
                            kk_ = xn_cnt[0]
                            xn_cnt[0] += 1
                            xk[sub] = kk_
                            _xnorm_front((it + 1) * 4 + sub, xbB, xnB, junkB, kk_ % 2, kk_ % 2, kk_ % 4, reuse=True)
                if it == 0 and "mT0" in debug:
                    dbg["mT0"] = dram("dbg_mT0", [128, 8, 512], kind="ExternalOutput", dtype=BF16)
                    dma(dbg["mT0"], mT, reads=[("mT", fc) for fc in range(8)])
                if it + 1 < NQT:
                    gates_mm((it + 1) % 2, 0)
                for sub in range(4):
                    s_ = it * 4 + sub
                    bx = s_ % 3
                    dma(xr[:, bx], x[s_ * 128:(s_ + 1) * 128, :], writes=[("xr", bx)] + ([("wstB", 0), ("wstB", 1), ("wstB", 2)] if s_ < 3 else []))
                    for half in range(2):
                        po = (2 * sub + half) % 2
                        for fc in range(8):
                            P.op("pe", lambda e, fc=fc: e.matmul(ps[:, po, :], lhsT=mT[:, fc, sub * 128:(sub + 1) * 128],
                                                                 rhs=WO[:, fc, half * 512:(half + 1) * 512], start=(fc == 0), stop=(fc == 7)),
                                 reads=[("mT", fc)] + [("WO", q8) for q8 in range(4 * half, 4 * half + 4)], writes=[("ps", po)])
                        P.op("dve", lambda e: e.tensor_tensor(out=xr[:, bx, half * 512:(half + 1) * 512], in0=ps[:, po, :],
                                                              in1=xr[:, bx, half * 512:(half + 1) * 512], op=ALU.add),
                             reads=[("ps", po), ("xr", bx)], writes=[("xr", bx)])
                    out_dmas.append(dma(y[s_ * 128:(s_ + 1) * 128, :], xr[:, bx], reads=[("xr", bx)]))
        P.emit(nc, final_wait_ops=[o for o in P.ops if o.is_dma])
    return nc, dbg


def host_inputs(inputs, b):
    f = lambda a: np.ascontiguousarray(a, dtype=np.float32)
    c = inputs["c"][b]
    cT = c.reshape(8, 128).T
    gq = np.tile(np.asarray(inputs["q_norm_g"][0]), 2)
    gk = np.tile(np.asarray(inputs["k_norm_g"][0]), 2)
    return {
        "x": f(inputs["x"][b]),
        "cT2": f(np.repeat(cT, 2, axis=1)),
        "w_ada": f(inputs["w_ada"][0]),
        "b_adaT": f(np.asarray(inputs["b_ada"][0]).reshape(24, 128).T),
        "norm_gT": f(np.asarray(inputs["norm_g"][0]).reshape(8, 128).T),
        "w_in": f(inputs["w_in"][0]),
        "b_f": f(np.asarray(inputs["b_f"][0]).reshape(8, 1)),
        "gqk": f(np.stack([gq, gk], axis=1)),
        "conv_wT": f(np.asarray(inputs["conv_w"][0]).reshape(3, 4, 128).transpose(2, 1, 0).reshape(128, 12)),
        "w_a": f(inputs["w_attn_out"][0]),
        "w_b": f(inputs["w_conv_out"][0]),
        "w_o": f(inputs["w_o"][0]),
    }


def kernel(**inputs):
    inputs = {k: np.asarray(v) for k, v in inputs.items()}
    nc, _ = build()
    in_maps = [host_inputs(inputs, b) for b in range(8)]
    res = run_bass_kernel_spmd(nc, in_maps, core_ids=list(range(8)))
    return np.stack([np.asarray(r["y"], dtype=np.float32) for r in res.results], axis=0)
```
